# Optimizing a Trainium2 kernel written in Bass

```python
import math
import jax, jax.numpy as jnp
from jax import lax
import numpy as np

D_MODEL = 1024
BATCH = 4
SEQ = 4096
DEPTH = 2

PLE_DIM = 256
D_FF = 2816
HEAD_DIM = 64
SB_HEADS = 8
SWA_HEADS = 8
SWA_KV_HEADS = 2
WINDOW = 128
Q_BLOCK = 128
GDN_K_HEADS = 8
GDN_V_HEADS = 16
GDN_HEAD_DIM = 128
GDN_CONV = 4
GDN_CHUNK = 64
EPS = 1e-6
N_EVEN = (DEPTH + 1) // 2
N_ODD = DEPTH // 2

SB_W = SB_HEADS * HEAD_DIM
SWA_QW = SWA_HEADS * HEAD_DIM
SWA_KVW = SWA_KV_HEADS * HEAD_DIM
ATT_IN = 3 * SB_W + SWA_QW + 2 * SWA_KVW
ATT_OUT = SB_W + SWA_QW
GDN_KW = GDN_K_HEADS * GDN_HEAD_DIM
GDN_VW = GDN_V_HEADS * GDN_HEAD_DIM
GDN_CONV_W = 2 * GDN_KW + GDN_VW
GDN_IN = GDN_CONV_W + GDN_VW + 2 * GDN_V_HEADS

kernel_name = 'hybrid_stickbreak_swa_gdn_macaron'


def rmsnorm(x, g):
    xf = x.astype(jnp.float32)
    y = xf * lax.rsqrt(jnp.mean(xf * xf, axis=-1, keepdims=True) + EPS) * g.astype(jnp.float32)
    return y.astype(x.dtype)


def l2norm(x):
    xf = x.astype(jnp.float32)
    return xf * lax.rsqrt(jnp.sum(xf * xf, axis=-1, keepdims=True) + EPS)


def swiglu(h, w_gate, w_up, w_down):
    return (jax.nn.silu(h @ w_gate) * (h @ w_up)) @ w_down


def alibi_slopes(n):
    return jnp.asarray(2.0 ** (-8.0 * (np.arange(n) + 1) / n), dtype=jnp.float32)


def stick_breaking_attention(q, k, v):
    b, h, t, d = q.shape
    nblk = t // Q_BLOCK
    qb = q.reshape(b, h, nblk, Q_BLOCK, d).transpose(2, 0, 1, 3, 4)
    key_pos = jnp.arange(t)
    scale = d ** -0.5

    def block(args):
        qi, i = args
        z = jnp.einsum('bhqd,bhkd->bhqk', qi, k).astype(jnp.float32) * scale
        q_pos = i * Q_BLOCK + jnp.arange(Q_BLOCK)
        causal = key_pos[None, :] < q_pos[:, None]
        log_keep = jnp.where(causal, jax.nn.log_sigmoid(-z), 0.0)
        log_between = lax.cumsum(log_keep, axis=3, reverse=True) - log_keep
        w = jnp.where(causal, jnp.exp(jax.nn.log_sigmoid(z) + log_between), 0.0)
        return jnp.einsum('bhqk,bhkd->bhqd', w.astype(v.dtype), v)

    out = lax.map(block, (qb, jnp.arange(nblk)))
    return out.transpose(1, 2, 0, 3, 4).reshape(b, h, t, d)


def swa_sink_attention(q, k, v, q_gain, k_gain, sinks, slopes):
    q = rmsnorm(q, q_gain)
    k = rmsnorm(k, k_gain)
    b, t, hq, d = q.shape
    hkv = k.shape[2]
    g = hq // hkv
    nblk = t // WINDOW
    qb = q.reshape(b, nblk, WINDOW, hkv, g, d)
    kpad = jnp.pad(k, ((0, 0), (WINDOW, 0), (0, 0), (0, 0)))
    vpad = jnp.pad(v, ((0, 0), (WINDOW, 0), (0, 0), (0, 0)))
    kb = jnp.concatenate([kpad[:, :t].reshape(b, nblk, WINDOW, hkv, d),
                          k.reshape(b, nblk, WINDOW, hkv, d)], axis=2)
    vb = jnp.concatenate([vpad[:, :t].reshape(b, nblk, WINDOW, hkv, d),
                          v.reshape(b, nblk, WINDOW, hkv, d)], axis=2)
    s = jnp.einsum('bnqhgd,bnkhd->bnhgqk', qb, kb).astype(jnp.float32) * (d ** -0.5)
    qi = jnp.arange(WINDOW)[:, None]
    kj = jnp.arange(2 * WINDOW)[None, :]
    dist = (qi + WINDOW - kj)
    band = (dist >= 0) & (dist < WINDOW)
    valid = band[None] & ((jnp.arange(nblk)[:, None, None] > 0) | (kj >= WINDOW)[None])
    bias = -slopes.reshape(hkv, g)[:, :, None, None] * dist.astype(jnp.float32)
    s = jnp.where(valid[None, :, None, None], s + bias[None, None], -jnp.inf)
    sink = jnp.broadcast_to(sinks.astype(jnp.float32).reshape(hkv, g)[None, None, :, :, None, None],
                            s.shape[:-1] + (1,))
    probs = jax.nn.softmax(jnp.concatenate([s, sink], axis=-1), axis=-1)[..., :-1]
    o = jnp.einsum('bnhgqk,bnkhd->bnqhgd', probs.astype(v.dtype), vb)
    return o.reshape(b, t, hq, d)


def attention_mixer(h, w_in, q_gain, k_gain, sinks, w_out):
    b, t, _ = h.shape
    proj = h @ w_in
    cuts = [SB_W, 2 * SB_W, 3 * SB_W, 3 * SB_W + SWA_QW, 3 * SB_W + SWA_QW + SWA_KVW]
    sq, sk, sv, bq, bk, bv = jnp.split(proj, cuts, axis=-1)
    heads = lambda z, n: z.reshape(b, t, n, HEAD_DIM)
    a_out = stick_breaking_attention(heads(sq, SB_HEADS).transpose(0, 2, 1, 3),
                                     heads(sk, SB_HEADS).transpose(0, 2, 1, 3),
                                     heads(sv, SB_HEADS).transpose(0, 2, 1, 3))
    b_out = swa_sink_attention(heads(bq, SWA_HEADS), heads(bk, SWA_KV_HEADS), heads(bv, SWA_KV_HEADS),
                               q_gain, k_gain, sinks, alibi_slopes(SWA_HEADS))
    o = jnp.concatenate([a_out.transpose(0, 2, 1, 3).reshape(b, t, SB_W),
                         b_out.reshape(b, t, SWA_QW)], axis=-1)
    return o @ w_out


def causal_depthwise_conv(x, w):
    kk, c = w.shape
    return lax.conv_general_dilated(x, w[:, None, :].astype(x.dtype), window_strides=(1,),
                                    padding=[(kk - 1, 0)], dimension_numbers=('NWC', 'WIO', 'NWC'),
                                    feature_group_count=c)


def chunk_gated_delta_rule(q, k, v, g, beta):
    b, t, h, dk = q.shape
    dv = v.shape[-1]
    c = GDN_CHUNK
    n = t // c
    chunks = lambda z: z.astype(jnp.float32).reshape(b, n, c, h, -1).transpose(0, 3, 1, 2, 4)
    q, k, v = chunks(q), chunks(k), chunks(v)
    g = g.astype(jnp.float32).reshape(b, n, c, h).transpose(0, 3, 1, 2)
    beta = beta.astype(jnp.float32).reshape(b, n, c, h).transpose(0, 3, 1, 2)
    gc = jnp.cumsum(g, axis=-1)
    idx = jnp.arange(c)
    lower_incl = idx[:, None] >= idx[None, :]
    strict = idx[:, None] > idx[None, :]
    decay = jnp.exp(jnp.where(lower_incl, gc[..., :, None] - gc[..., None, :], -jnp.inf))
    kbeta = k * beta[..., None]
    lmat = jnp.where(strict, jnp.einsum('bhncd,bhnsd->bhncs', kbeta, k) * decay, 0.0)
    tmat = lmat + jnp.eye(c, dtype=jnp.float32)
    rhs = jnp.concatenate([v * beta[..., None], kbeta * jnp.exp(gc)[..., None]], axis=-1)
    sol = lax.linalg.triangular_solve(tmat, rhs, left_side=True, lower=True, unit_diagonal=True)
    u, w = sol[..., :dv], sol[..., dv:]
    attn = jnp.where(lower_incl, jnp.einsum('bhncd,bhnsd->bhncs', q, k) * decay, 0.0)
    q_dec = q * jnp.exp(gc)[..., None]
    k_tail = k * jnp.exp(gc[..., -1:] - gc)[..., None]
    chunk_dec = jnp.exp(gc[..., -1])

    def step(state, inp):
        u_c, w_c, qd_c, a_c, kt_c, dec_c = inp
        v_new = u_c - jnp.einsum('bhcd,bhdv->bhcv', w_c, state)
        o_c = jnp.einsum('bhcd,bhdv->bhcv', qd_c, state) + jnp.einsum('bhcs,bhsv->bhcv', a_c, v_new)
        state = state * dec_c[..., None, None] + jnp.einsum('bhcd,bhcv->bhdv', kt_c, v_new)
        return state, o_c

    xs = tuple(jnp.moveaxis(z, 2, 0) for z in (u, w, q_dec, attn, k_tail, chunk_dec))
    s0 = jnp.zeros((b, h, dk, dv), jnp.float32)
    _, o = lax.scan(step, s0, xs)
    return o.transpose(1, 0, 3, 2, 4).reshape(b, t, h, dv)


def gdn_mixer(h, w_in, conv_w, a_log, dt_bias, out_gain, w_out):
    b, t, _ = h.shape
    proj = h @ w_in
    qkv, z, beta_logit, a = jnp.split(
        proj, [GDN_CONV_W, GDN_CONV_W + GDN_VW, GDN_CONV_W + GDN_VW + GDN_V_HEADS], axis=-1)
    qkv = jax.nn.silu(causal_depthwise_conv(qkv, conv_w))
    q, k, v = jnp.split(qkv, [GDN_KW, 2 * GDN_KW], axis=-1)
    q = l2norm(q.reshape(b, t, GDN_K_HEADS, GDN_HEAD_DIM)) * (GDN_HEAD_DIM ** -0.5)
    k = l2norm(k.reshape(b, t, GDN_K_HEADS, GDN_HEAD_DIM))
    rep = GDN_V_HEADS // GDN_K_HEADS
    q = jnp.repeat(q, rep, axis=2)
    k = jnp.repeat(k, rep, axis=2)
    v = v.reshape(b, t, GDN_V_HEADS, GDN_HEAD_DIM)
    beta = jax.nn.sigmoid(beta_logit.astype(jnp.float32))
    g = -jnp.exp(a_log.astype(jnp.float32)) * jax.nn.softplus(a.astype(jnp.float32) + dt_bias.astype(jnp.float32))
    o = chunk_gated_delta_rule(q, k, v, g, beta).astype(h.dtype)
    o = rmsnorm(o, out_gain) * jax.nn.silu(z.reshape(b, t, GDN_V_HEADS, GDN_HEAD_DIM))
    return o.reshape(b, t, GDN_VW) @ w_out


def setup_inputs(seed: int = 0) -> dict:
    key = jax.random.key(seed)
    ks = jax.random.split(key, 24)
    f32 = jnp.float32
    nrm = lambda kk, shape, fan_in: jax.random.normal(kk, shape, f32) * (fan_in ** -0.5)
    gain = lambda kk, shape: 1.0 + 0.02 * jax.random.normal(kk, shape, f32)
    dt = jnp.exp(jax.random.uniform(ks[15], (N_ODD, GDN_V_HEADS), f32, math.log(1e-3), math.log(0.1)))
    return {
        'x': jax.random.normal(ks[0], (BATCH, SEQ, D_MODEL), f32),
        'p': jax.random.normal(ks[1], (DEPTH, BATCH, SEQ, PLE_DIM), f32),
        'ffn_norm': gain(ks[2], (DEPTH, 2, D_MODEL)),
        'ffn_w_gate': nrm(ks[3], (DEPTH, 2, D_MODEL, D_FF), D_MODEL),
        'ffn_w_up': nrm(ks[4], (DEPTH, 2, D_MODEL, D_FF), D_MODEL),
        'ffn_w_down': nrm(ks[5], (DEPTH, 2, D_FF, D_MODEL), D_FF),
        'mix_norm': gain(ks[6], (DEPTH, D_MODEL)),
        'att_w_in': nrm(ks[7], (N_EVEN, D_MODEL, ATT_IN), D_MODEL),
        'att_q_norm': gain(ks[8], (N_EVEN, HEAD_DIM)),
        'att_k_norm': gain(ks[9], (N_EVEN, HEAD_DIM)),
        'att_sinks': 0.5 * jax.random.normal(ks[10], (N_EVEN, SWA_HEADS), f32),
        'att_w_out': nrm(ks[11], (N_EVEN, ATT_OUT, D_MODEL), ATT_OUT),
        'gdn_w_in': nrm(ks[12], (N_ODD, D_MODEL, GDN_IN), D_MODEL),
        'gdn_conv_w': nrm(ks[13], (N_ODD, GDN_CONV, GDN_CONV_W), GDN_CONV),
        'gdn_a_log': jnp.log(jax.random.uniform(ks[14], (N_ODD, GDN_V_HEADS), f32, 1.0, 16.0)),
        'gdn_dt_bias': dt + jnp.log(-jnp.expm1(-dt)),
        'gdn_out_norm': gain(ks[16], (N_ODD, GDN_HEAD_DIM)),
        'gdn_w_out': nrm(ks[17], (N_ODD, GDN_VW, D_MODEL), GDN_VW),
        'ple_norm': gain(ks[18], (DEPTH, D_MODEL)),
        'ple_w_gate': nrm(ks[19], (DEPTH, D_MODEL, D_MODEL), D_MODEL),
        'ple_w_proj': nrm(ks[20], (DEPTH, PLE_DIM, D_MODEL), PLE_DIM),
    }


def reference(x, p, ffn_norm, ffn_w_gate, ffn_w_up, ffn_w_down, mix_norm,
              att_w_in, att_q_norm, att_k_norm, att_sinks, att_w_out,
              gdn_w_in, gdn_conv_w, gdn_a_log, gdn_dt_bias, gdn_out_norm, gdn_w_out,
              ple_norm, ple_w_gate, ple_w_proj):
    h = x
    for i in range(DEPTH):
        h = h + 0.5 * swiglu(rmsnorm(h, ffn_norm[i, 0]), ffn_w_gate[i, 0], ffn_w_up[i, 0], ffn_w_down[i, 0])
        hn = rmsnorm(h, mix_norm[i])
        j = i // 2
        if i % 2 == 0:
            h = h + attention_mixer(hn, att_w_in[j], att_q_norm[j], att_k_norm[j], att_sinks[j], att_w_out[j])
        else:
            h = h + gdn_mixer(hn, gdn_w_in[j], gdn_conv_w[j], gdn_a_log[j], gdn_dt_bias[j],
                              gdn_out_norm[j], gdn_w_out[j])
        h = h + 0.5 * swiglu(rmsnorm(h, ffn_norm[i, 1]), ffn_w_gate[i, 1], ffn_w_up[i, 1], ffn_w_down[i, 1])
        gate = jax.nn.sigmoid(rmsnorm(h, ple_norm[i]) @ ple_w_gate[i])
        h = h + gate * (p[i] @ ple_w_proj[i])
    return h
```

```python
import numpy as np
import concourse.bass as bass
import concourse.mybir as mybir
from concourse.alu_op_type import AluOpType as ALU
from contextlib import ExitStack

F32 = mybir.dt.float32
BF16 = mybir.dt.bfloat16
AF = mybir.ActivationFunctionType
AX = mybir.AxisListType
P = 128


class Buf:
    __slots__ = ("name", "lastw", "reads", "dsem", "dcount", "excl")

    def __init__(self, name):
        self.name = name
        self.lastw = None
        self.reads = {}
        self.dsem = None
        self.dcount = 0
        self.excl = False


class K:
    def __init__(self, nc, es, safe_same=True):
        self.nc = nc
        self.es = es
        self.E = {}
        for n in ("tensor", "vector", "scalar", "gpsimd", "sync"):
            sem = es.enter_context(nc.semaphore("e_" + n))
            self.E[n] = dict(eng=getattr(nc, n), sem=sem, count=0, waited={}, name=n)
        self.safe_same = safe_same
        self.dma_sems = {}
        self.free_dsems = []
        self.nsem_alloc = 0
        self.bar_sem = es.enter_context(nc.semaphore("barrier"))
        self.bar_count = 0
        self.nbuf = 0
        self.final_events = []
        self.ninstr = 0

    def buf(self, name=None):
        self.nbuf += 1
        return Buf(name or ("b%d" % self.nbuf))

    def _wait(self, e, sem, val):
        key = id(sem)
        if e["waited"].get(key, 0) >= val:
            return
        e["eng"].wait_ge(sem, val)
        e["waited"][key] = val
        if getattr(self, "trace", None) is not None:
            self.trace.append((e["name"], "wait", [n for n, x in self.E.items() if x["sem"] is sem] or "dma", val))

    def _emit_waits(self, en, reads, writes):
        e = self.E[en]
        evs = []
        for b in reads:
            if b.lastw is not None:
                evs.append(b.lastw)
            if b.excl:
                evs.extend(ev for ev in b.reads.values() if ev[2] != en)
        for b in writes:
            if b.lastw is not None:
                evs.append(b.lastw)
            evs.extend(b.reads.values())
        for (sem, val, src) in evs:
            if src == en and (en == "tensor" or not self.safe_same):
                continue
            self._wait(e, sem, val)

    def _record(self, ev, reads, writes):
        for b in writes:
            b.lastw = ev
            b.reads = {}
        for b in reads:
            if b in writes:
                continue
            key = id(ev[0])
            old = b.reads.get(key)
            if old is None or old[1] < ev[1]:
                b.reads[key] = ev

    def op(self, en, fn, reads=(), writes=()):
        e = self.E[en]
        self._emit_waits(en, reads, writes)
        ins = fn(e["eng"])
        e["count"] += 1
        ins.then_inc(e["sem"], 1)
        ev = (e["sem"], e["count"], en)
        self._record(ev, reads, writes)
        self.ninstr += 1
        if getattr(self, "trace", None) is not None:
            self.trace.append((en, "op", e["count"], [b.name for b in reads], [b.name for b in writes]))
        return ev

    def dma(self, qn, pairs, reads=(), writes=(), final=False):
        e = self.E[qn]
        self._emit_waits(qn, reads, writes)
        owner = writes[0] if len(writes) else reads[0]
        if owner.dsem is None:
            if self.free_dsems:
                owner.dsem, owner.dcount = self.free_dsems.pop()
            else:
                self.nsem_alloc += 1
                owner.dsem = self.es.enter_context(self.nc.semaphore("d%d_%s" % (self.nsem_alloc, owner.name)))
        for (o, i) in pairs:
            ins = e["eng"].dma_start(out=o, in_=i)
            owner.dcount += 16
            ins.then_inc(owner.dsem, 16)
            self.ninstr += 1
        ev = (owner.dsem, owner.dcount, "dma")
        self.dma_sems[id(owner.dsem)] = [owner.dsem, owner.dcount]
        self._record(ev, reads, writes)
        if final:
            self.final_events.append(ev)
        return ev

    def barrier(self, collective_fn=None):
        g = self.E["gpsimd"]
        for n, e in self.E.items():
            if n != "gpsimd" and e["count"] > 0:
                self._wait(g, e["sem"], e["count"])
        if g["count"] > 0 and self.safe_same:
            self._wait(g, g["sem"], g["count"])
        for sem, val in self.dma_sems.values():
            self._wait(g, sem, val)
        fns = collective_fn if isinstance(collective_fn, (list, tuple)) else ([collective_fn] if collective_fn else [])
        if fns:
            for fn in fns:
                ins = fn(g["eng"])
                self.bar_count += 1
                ins.then_inc(self.bar_sem, 1)
        else:
            ins = g["eng"].nop()
            self.bar_count += 1
            ins.then_inc(self.bar_sem, 1)
        for n, e in self.E.items():
            self._wait(e, self.bar_sem, self.bar_count)
            for n2, e2 in self.E.items():
                e["waited"][id(e2["sem"])] = max(e["waited"].get(id(e2["sem"]), 0), e2["count"])
            for sem, val in self.dma_sems.values():
                e["waited"][id(sem)] = max(e["waited"].get(id(sem), 0), val)
        self.free_dsems.extend((sem, val) for sem, val in self.dma_sems.values())
        self.dma_sems = {}

    def finish(self):
        e = self.E["sync"]
        for (sem, val, src) in self.final_events:
            self._wait(e, sem, val)
EPS = 1e-6
TT = 512


class RR:
    def __init__(self, items):
        self.items = items
        self.i = 0

    def next(self):
        it = self.items[self.i % len(self.items)]
        self.i += 1
        return it


class RowProg:
    def __init__(self, nc, es, TOK, n_gain, NST=3, NWB=4, safe_same=True, k=None, pfx=""):
        self.nc = nc
        self.es = es
        k = self.k = k if k is not None else K(nc, es, safe_same=safe_same)
        self.TOK = TOK
        self.pfx = pfx
        self.NTT = TOK // TT
        NTT = self.NTT

        def alloc(name, shape, dt):
            return es.enter_context(nc.sbuf_tensor(pfx + "sb_" + name, shape, dt))

        self.hT = alloc("hT", [P, 8, TOK], F32)
        self.hTb = [[k.buf("hT%d_%d" % (c, t)) for t in range(NTT)] for c in range(8)]
        self.hn = alloc("hn", [P, 8, TOK], BF16)
        self.hnb = [[k.buf("hn%d_%d" % (c, t)) for t in range(NTT)] for c in range(8)]
        self.act = alloc("act", [P, 11, TOK], BF16)
        self.actb = [[k.buf("ac%d_%d" % (c, t)) for t in range(NTT)] for c in range(11)]
        wst = alloc("wst", [P, NST, 2048], F32)
        self.wst = RR([(k.buf("wst%d" % i), wst[:, i, :]) for i in range(NST)])
        wbf = alloc("wbf", [P, NWB, 2048], BF16)
        self.wbf = RR([(k.buf("wbf%d" % i), wbf[:, i, :]) for i in range(NWB)])
        sq = alloc("sq", [P, 2, TT], BF16)
        self.sq = RR([(k.buf("sq%d" % i), sq[:, i, :]) for i in range(2)])
        t32 = alloc("t32", [P, 4, TT], F32)
        self.t32 = RR([(k.buf("t32_%d" % i), t32[:, i, :]) for i in range(4)])
        self.ones = alloc("ones", [P, P], BF16)
        self.onesb = k.buf("ones")
        self.gains = alloc("gains", [P, n_gain * 8 + 1], F32)
        self.n_gain = n_gain
        self.gainsb = k.buf("gains")
        ps = [es.enter_context(nc.psum_tensor(pfx + "ps%d" % i, [P, TT], F32)) for i in range(7)]
        self.ps = RR([(k.buf("ps%d" % i), ps[i][:, :]) for i in range(7)])
        self.items = []
        k.op("vector", lambda e: e.memset(self.ones[:], 1.0), writes=[self.onesb])
        self.epsc = alloc("epsc", [P, 1], F32)
        self.epsap = self.epsc[:, 0:1]
        k.op("vector", lambda e: e.memset(self.epsc[:], EPS), writes=[self.onesb])

    def ts(self, tt):
        return slice(tt * TT, (tt + 1) * TT)

    def item(self, specs, fn):
        self.items.append((specs, fn))

    def load_gains(self, g_dram):
        self.k.dma("sync", [(self.gains[:], g_dram)], writes=[self.gainsb])

    def obuf(self, rc):
        if rc < 8:
            return self.hnb[rc], self.hn[:, rc, :]
        return self.actb[rc - 8], self.act[:, rc - 8, :]

    def _issue_load(self, spec):
        k = self.k
        w, r0, R, c0, ncols = spec
        assert R * ncols <= 2048
        sb, sap = self.wst.next()
        wb, wap = self.wbf.next()
        src = w[r0 * P:(r0 + R) * P, c0:c0 + ncols].rearrange("(r p) n -> p r n", p=P)
        dst = sap[:, 0:R * ncols].rearrange("p (r n) -> p r n", r=R)
        k.dma("sync", [(dst, src)], writes=[sb])
        k.op("gpsimd", lambda e: e.tensor_copy(out=wap[:, 0:R * ncols], in_=sap[:, 0:R * ncols]),
             reads=[sb], writes=[wb])
        return wb, wap[:, 0:R * ncols].rearrange("p (r n) -> p r n", r=R)

    def emit(self, lookahead=1, finish=True):
        items = self.items
        loaded = {}
        nl = 0
        for i, (specs, fn) in enumerate(items):
            while nl < len(items) and nl <= i + lookahead:
                loaded[nl] = [self._issue_load(s) for s in items[nl][0]]
                nl += 1
            fn(loaded.pop(i))
        if finish:
            self.k.finish()

    def load_h(self, xT_dram):
        def fn(_):
            for c in range(8):
                self.k.dma("sync", [(self.hT[:, c, :], xT_dram[c * P:(c + 1) * P, :])], writes=self.hTb[c])
        self.item([], fn)

    def store_h(self, out_dram, final=True):
        def fn(_):
            for c in range(8):
                self.k.dma("sync", [(out_dram[c * P:(c + 1) * P, :], self.hT[:, c, :])], reads=self.hTb[c], final=final)
        self.item([], fn)

    def load_sel(self, sel_dram):
        self.sel = self.es.enter_context(self.nc.sbuf_tensor(self.pfx + "sb_sel", [P, 2], F32))
        self.selb = self.k.buf("sel")
        self.k.dma("sync", [(self.sel[:], sel_dram)], writes=[self.selb])

    def hn_to_dram(self, gi, dst):
        self.norm(gi)

        def fn(_):
            for c in range(8):
                self.k.dma("sync", [(dst(c * P, (c + 1) * P), self.hn[:, c, :])], reads=self.hnb[c])
        self.item([], fn)

    def hn_from_dram(self, src):
        def fn(_):
            for c in range(8):
                self.k.dma("sync", [(self.hn[:, c, :], src(c * P, (c + 1) * P))], writes=self.hnb[c])
        self.item([], fn)

    def proj_fm(self, w, c0w, n_out, out_dram, row0, col0):
        k = self.k
        c0 = 0
        while c0 < n_out:
            ncols = min(256, n_out - c0)

            def fn(tiles, c0=c0, ncols=ncols):
                (wb, wap), = tiles
                j0 = 0
                while j0 < ncols:
                    m = min(P, ncols - j0)
                    for tt in range(self.NTT):
                        ts = self.ts(tt)
                        pb, pap = self.ps.next()
                        for c in range(8):
                            k.op("tensor", lambda e: e.matmul(pap[0:m, :], lhsT=wap[:, c, j0:j0 + m], rhs=self.hn[:, c, ts],
                                                              start=(c == 0), stop=(c == 7)),
                                 reads=[wb, self.hnb[c][tt]], writes=[pb])
                        eb, eap = self.t32.next()
                        if tt % 2 == 0:
                            k.op("scalar", lambda e: e.copy(out=eap[0:m, :], in_=pap[0:m, :]), reads=[pb], writes=[eb])
                        else:
                            k.op("vector", lambda e: e.tensor_copy(out=eap[0:m, :], in_=pap[0:m, :]), reads=[pb], writes=[eb])
                        r0 = row0 + c0 + j0
                        k.dma("sync", [(out_dram[r0:r0 + m, col0 + tt * TT:col0 + (tt + 1) * TT], eap[0:m, :])], reads=[eb])
                    j0 += m
            self.item([(w, 0, 8, c0w + c0, ncols)], fn)
            c0 += ncols

    def proj_tm(self, w, c0w, ncols, out_dram, tok0, ocol0):
        k = self.k

        def fn(tiles):
            (wb, wap), = tiles
            for tb in range(self.TOK // P):
                tt = (tb * P) // TT
                pb, pap = self.ps.next()
                for c in range(8):
                    k.op("tensor", lambda e: e.matmul(pap[:, 0:ncols], lhsT=self.hn[:, c, tb * P:(tb + 1) * P], rhs=wap[:, c, 0:ncols],
                                                      start=(c == 0), stop=(c == 7)),
                         reads=[wb, self.hnb[c][tt]], writes=[pb])
                eb, eap = self.t32.next()
                if tb % 2 == 0:
                    k.op("scalar", lambda e: e.copy(out=eap[:, 0:ncols], in_=pap[:, 0:ncols]), reads=[pb], writes=[eb])
                else:
                    k.op("vector", lambda e: e.tensor_copy(out=eap[:, 0:ncols], in_=pap[:, 0:ncols]), reads=[pb], writes=[eb])
                k.dma("sync", [(out_dram[tok0 + tb * P:tok0 + (tb + 1) * P, ocol0:ocol0 + ncols], eap[:, 0:ncols])], reads=[eb])
        self.item([(w, 0, 8, c0w, ncols)], fn)

    def _load_sel_chunk(self, G, rc):
        k = self.k
        T = self.TOK
        ab, aap = self.wst.next()
        k.dma("sync", [(aap[:, 0:T], G(rc)[:, 0:T])], writes=[ab])
        bb, bap = self.wst.next()
        k.dma("sync", [(bap[:, 0:T], G(rc)[:, T:2 * T])], writes=[bb])
        k.op("gpsimd", lambda e: e.tensor_scalar(out=aap[:, 0:T], in0=aap[:, 0:T], scalar1=self.sel[:, 0:1], scalar2=None, op0=ALU.mult),
             reads=[ab, self.selb], writes=[ab])
        k.op("gpsimd", lambda e: e.tensor_scalar(out=bap[:, 0:T], in0=bap[:, 0:T], scalar1=self.sel[:, 1:2], scalar2=None, op0=ALU.mult),
             reads=[bb, self.selb], writes=[bb])
        return ab, aap, bb, bap

    def mix_in_sel(self, G, nrc, w_out):
        k = self.k

        def fn(_):
            for rc in range(nrc):
                bufs, dap = self.obuf(rc)
                ab, aap, bb, bap = self._load_sel_chunk(G, rc)
                k.op("gpsimd", lambda e: e.tensor_tensor(out=dap, in0=aap[:, 0:self.TOK], in1=bap[:, 0:self.TOK], op=ALU.add),
                     reads=[ab, bb], writes=bufs)
        self.item([], fn)
        self._mix_matmuls(nrc, w_out)

    def _mix_matmuls(self, nrc, w_out):
        k = self.k
        for dc in range(8):
            def fn(tiles, dc=dc):
                (wb, wap), = tiles
                for tt in range(self.NTT):
                    ts = self.ts(tt)
                    pb, pap = self.ps.next()
                    for rc in range(nrc):
                        bufs, oap = self.obuf(rc)
                        k.op("tensor", lambda e: e.matmul(pap, lhsT=wap[:, rc, :], rhs=oap[:, ts],
                                                          start=(rc == 0), stop=(rc == nrc - 1)),
                             reads=[wb, bufs[tt]], writes=[pb])
                    k.op("vector", lambda e: e.tensor_tensor(out=self.hT[:, dc, ts], in0=pap, in1=self.hT[:, dc, ts], op=ALU.add),
                         reads=[pb, self.hTb[dc][tt]], writes=[self.hTb[dc][tt]])
            self.item([(w_out, 0, nrc, dc * P, P)], fn)

    def gdn_gate_mix_in_sel(self, G, zT_dram, w_out):
        k = self.k
        gcol = self.gains[:, self.n_gain * 8:self.n_gain * 8 + 1]

        def fn(_):
            for rc in range(16):
                bufs, dap = self.obuf(rc)
                ob, oap, bb, bap = self._load_sel_chunk(G, rc)
                k.op("gpsimd", lambda e: e.tensor_tensor(out=oap[:, 0:self.TOK], in0=oap[:, 0:self.TOK], in1=bap[:, 0:self.TOK], op=ALU.add),
                     reads=[ob, bb], writes=[ob])
                zb, zap = self.wst.next()
                k.dma("sync", [(zap[:, 0:self.TOK], zT_dram[rc * P:(rc + 1) * P, :])], writes=[zb])
                for tt in range(self.NTT):
                    ts = self.ts(tt)
                    qb, qap = self.sq.next()
                    k.op("scalar", lambda e: e.activation(out=qap, in_=oap[:, ts], func=AF.Square), reads=[ob], writes=[qb])
                    pb, pap = self.ps.next()
                    k.op("tensor", lambda e: e.matmul(pap, lhsT=self.ones[:], rhs=qap, start=True, stop=True),
                         reads=[qb, self.onesb], writes=[pb])
                    tb, tap = self.t32.next()
                    k.op("scalar", lambda e: e.activation(out=tap, in_=pap, func=AF.Sqrt, scale=1.0 / 128.0, bias=self.epsap),
                         reads=[pb, self.onesb], writes=[tb])
                    k.op("vector", lambda e: e.reciprocal(out=tap, in_=tap), reads=[tb], writes=[tb])
                    k.op("vector", lambda e: e.scalar_tensor_tensor(out=oap[:, ts], in0=oap[:, ts], scalar=gcol, in1=tap,
                                                                    op0=ALU.mult, op1=ALU.mult),
                         reads=[ob, tb, self.gainsb], writes=[ob])
                for tt in range(self.NTT):
                    ts = self.ts(tt)
                    k.op("scalar", lambda e: e.activation(out=zap[:, ts], in_=zap[:, ts], func=AF.Silu), reads=[zb], writes=[zb])
                    k.op("gpsimd", lambda e: e.tensor_tensor(out=dap[:, ts], in0=oap[:, ts], in1=zap[:, ts], op=ALU.mult),
                         reads=[ob, zb], writes=[bufs[tt]])
        self.item([], fn)
        self._mix_matmuls(16, w_out)

    def norm(self, gi):
        def fn(_):
            k = self.k
            for tt in range(self.NTT):
                ts = self.ts(tt)
                pb, pap = self.ps.next()
                for c in range(8):
                    qb, qap = self.sq.next()
                    k.op("scalar", lambda e: e.activation(out=qap, in_=self.hT[:, c, ts], func=AF.Square),
                         reads=[self.hTb[c][tt]], writes=[qb])
                    k.op("tensor", lambda e: e.matmul(pap, lhsT=self.ones[:], rhs=qap, start=(c == 0), stop=(c == 7)),
                         reads=[qb, self.onesb], writes=[pb])
                tb, tap = self.t32.next()
                k.op("scalar", lambda e: e.activation(out=tap, in_=pap, func=AF.Sqrt, scale=1.0 / 1024.0, bias=self.epsap),
                     reads=[pb, self.onesb], writes=[tb])
                rb, rap = self.t32.next()
                k.op("vector", lambda e: e.reciprocal(out=rap, in_=tap), reads=[tb], writes=[rb])
                for c in range(8):
                    k.op("vector", lambda e: e.scalar_tensor_tensor(
                        out=self.hn[:, c, ts], in0=self.hT[:, c, ts], scalar=self.gains[:, gi * 8 + c:gi * 8 + c + 1],
                        in1=rap, op0=ALU.mult, op1=ALU.mult),
                        reads=[self.hTb[c][tt], rb, self.gainsb], writes=[self.hnb[c][tt]])
        self.item([], fn)

    def ffn(self, gi, wg, wu, wd):
        self.norm(gi)
        k = self.k
        for half in range(2):
            f0 = half * 11
            groups = [(0, 2), (2, 2), (4, 2), (6, 2), (8, 2), (10, 1)]
            for (fl0, nfc) in groups:
                def fn(tiles, fl0=fl0, nfc=nfc):
                    (gb, gap), (ub, uap) = tiles
                    for j in range(nfc):
                        for tt in range(self.NTT):
                            ts = self.ts(tt)
                            pgb, pg = self.ps.next()
                            pub, pu = self.ps.next()
                            for c in range(8):
                                k.op("tensor", lambda e: e.matmul(pg, lhsT=gap[:, c, j * P:(j + 1) * P], rhs=self.hn[:, c, ts],
                                                                  start=(c == 0), stop=(c == 7)),
                                     reads=[gb, self.hnb[c][tt]], writes=[pgb])
                            for c in range(8):
                                k.op("tensor", lambda e: e.matmul(pu, lhsT=uap[:, c, j * P:(j + 1) * P], rhs=self.hn[:, c, ts],
                                                                  start=(c == 0), stop=(c == 7)),
                                     reads=[ub, self.hnb[c][tt]], writes=[pub])
                            sb, sap = self.t32.next()
                            k.op("scalar", lambda e: e.activation(out=sap, in_=pg, func=AF.Silu), reads=[pgb], writes=[sb])
                            k.op("vector", lambda e: e.tensor_tensor(out=self.act[:, fl0 + j, ts], in0=sap, in1=pu, op=ALU.mult),
                                 reads=[sb, pub], writes=[self.actb[fl0 + j][tt]])
                c0 = (f0 + fl0) * P
                self.item([(wg, 0, 8, c0, nfc * P), (wu, 0, 8, c0, nfc * P)], fn)
            for dc in range(8):
                def fn(tiles, dc=dc):
                    (wb, wap), = tiles
                    for tt in range(self.NTT):
                        ts = self.ts(tt)
                        pb, pap = self.ps.next()
                        for f in range(11):
                            k.op("tensor", lambda e: e.matmul(pap, lhsT=wap[:, f, :], rhs=self.act[:, f, ts],
                                                              start=(f == 0), stop=(f == 10)),
                                 reads=[wb, self.actb[f][tt]], writes=[pb])
                        k.op("vector", lambda e: e.scalar_tensor_tensor(
                            out=self.hT[:, dc, ts], in0=pap, scalar=0.5, in1=self.hT[:, dc, ts],
                            op0=ALU.mult, op1=ALU.add),
                            reads=[pb, self.hTb[dc][tt]], writes=[self.hTb[dc][tt]])
                self.item([(wd, f0, 11, dc * P, P)], fn)

    def proj_out(self, gi, w, n_out, out_dram):
        self.norm(gi)
        k = self.k
        c0 = 0
        while c0 < n_out:
            ncols = min(256, n_out - c0)

            def fn(tiles, c0=c0, ncols=ncols):
                (wb, wap), = tiles
                j0 = 0
                while j0 < ncols:
                    m = min(P, ncols - j0)
                    for tt in range(self.NTT):
                        ts = self.ts(tt)
                        pb, pap = self.ps.next()
                        for c in range(8):
                            k.op("tensor", lambda e: e.matmul(pap[0:m, :], lhsT=wap[:, c, j0:j0 + m], rhs=self.hn[:, c, ts],
                                                              start=(c == 0), stop=(c == 7)),
                                 reads=[wb, self.hnb[c][tt]], writes=[pb])
                        eb, eap = self.t32.next()
                        if tt % 2 == 0:
                            k.op("scalar", lambda e: e.copy(out=eap[0:m, :], in_=pap[0:m, :]), reads=[pb], writes=[eb])
                        else:
                            k.op("vector", lambda e: e.tensor_copy(out=eap[0:m, :], in_=pap[0:m, :]), reads=[pb], writes=[eb])
                        k.dma("sync", [(out_dram[c0 + j0:c0 + j0 + m, ts], eap[0:m, :])], reads=[eb], final=True)
                    j0 += m
            self.item([(w, 0, 8, c0, ncols)], fn)
            c0 += ncols

    def load_T_bf16(self, src_dram, nrc):
        def fn(_):
            k = self.k
            for rc in range(nrc):
                bufs, dap = self.obuf(rc)
                sb, sap = self.wst.next()
                k.dma("sync", [(sap[:, 0:self.TOK], src_dram[rc * P:(rc + 1) * P, :])], writes=[sb])
                k.op("gpsimd", lambda e: e.tensor_copy(out=dap, in_=sap[:, 0:self.TOK]), reads=[sb], writes=bufs)
        self.item([], fn)

    def mix_in(self, oT_dram, nrc, w_out):
        self.load_T_bf16(oT_dram, nrc)
        k = self.k
        for dc in range(8):
            def fn(tiles, dc=dc):
                (wb, wap), = tiles
                for tt in range(self.NTT):
                    ts = self.ts(tt)
                    pb, pap = self.ps.next()
                    for rc in range(nrc):
                        bufs, oap = self.obuf(rc)
                        k.op("tensor", lambda e: e.matmul(pap, lhsT=wap[:, rc, :], rhs=oap[:, ts],
                                                          start=(rc == 0), stop=(rc == nrc - 1)),
                             reads=[wb, bufs[tt]], writes=[pb])
                    k.op("vector", lambda e: e.tensor_tensor(out=self.hT[:, dc, ts], in0=pap, in1=self.hT[:, dc, ts], op=ALU.add),
                         reads=[pb, self.hTb[dc][tt]], writes=[self.hTb[dc][tt]])
            self.item([(w_out, 0, nrc, dc * P, P)], fn)

    def ple(self, gi, wpg, wpp, pT_dram):
        self.norm(gi)
        k = self.k

        def fnp(_):
            for rc in range(2):
                sb, sap = self.wst.next()
                k.dma("sync", [(sap[:, 0:self.TOK], pT_dram[rc * P:(rc + 1) * P, :])], writes=[sb])
                k.op("gpsimd", lambda e: e.tensor_copy(out=self.act[:, rc, :], in_=sap[:, 0:self.TOK]),
                     reads=[sb], writes=self.actb[rc])
        self.item([], fnp)
        for dc in range(8):
            def fn(tiles, dc=dc):
                (gb, gap), (pb_, pap_) = tiles
                for tt in range(self.NTT):
                    ts = self.ts(tt)
                    pgb, pg = self.ps.next()
                    ppb, pp = self.ps.next()
                    for c in range(8):
                        k.op("tensor", lambda e: e.matmul(pg, lhsT=gap[:, c, :], rhs=self.hn[:, c, ts],
                                                          start=(c == 0), stop=(c == 7)),
                             reads=[gb, self.hnb[c][tt]], writes=[pgb])
                    for c in range(2):
                        k.op("tensor", lambda e: e.matmul(pp, lhsT=pap_[:, c, :], rhs=self.act[:, c, ts],
                                                          start=(c == 0), stop=(c == 1)),
                             reads=[pb_, self.actb[c][tt]], writes=[ppb])
                    sb, sap = self.t32.next()
                    k.op("scalar", lambda e: e.activation(out=sap, in_=pg, func=AF.Sigmoid), reads=[pgb], writes=[sb])
                    mb, map_ = self.t32.next()
                    k.op("vector", lambda e: e.tensor_tensor(out=map_, in0=sap, in1=pp, op=ALU.mult),
                         reads=[sb, ppb], writes=[mb])
                    k.op("gpsimd", lambda e: e.tensor_tensor(out=self.hT[:, dc, ts], in0=map_, in1=self.hT[:, dc, ts], op=ALU.add),
                         reads=[mb, self.hTb[dc][tt]], writes=[self.hTb[dc][tt]])
            self.item([(wpg, 0, 8, dc * P, P), (wpp, 0, 2, dc * P, P)], fn)

    def gdn_gate_mix_in(self, oT_dram, zT_dram, w_out):
        k = self.k
        gcol = self.gains[:, self.n_gain * 8:self.n_gain * 8 + 1]

        def fn(_):
            for rc in range(16):
                bufs, dap = self.obuf(rc)
                ob, oap = self.wst.next()
                k.dma("sync", [(oap[:, 0:self.TOK], oT_dram[rc * P:(rc + 1) * P, :])], writes=[ob])
                zb, zap = self.wst.next()
                k.dma("sync", [(zap[:, 0:self.TOK], zT_dram[rc * P:(rc + 1) * P, :])], writes=[zb])
                rstd = []
                for tt in range(self.NTT):
                    ts = self.ts(tt)
                    qb, qap = self.sq.next()
                    k.op("scalar", lambda e: e.activation(out=qap, in_=oap[:, ts], func=AF.Square), reads=[ob], writes=[qb])
                    pb, pap = self.ps.next()
                    k.op("tensor", lambda e: e.matmul(pap, lhsT=self.ones[:], rhs=qap, start=True, stop=True),
                         reads=[qb, self.onesb], writes=[pb])
                    tb, tap = self.t32.next()
                    k.op("scalar", lambda e: e.activation(out=tap, in_=pap, func=AF.Sqrt, scale=1.0 / 128.0, bias=self.epsap),
                         reads=[pb, self.onesb], writes=[tb])
                    k.op("vector", lambda e: e.reciprocal(out=tap, in_=tap), reads=[tb], writes=[tb])
                    k.op("vector", lambda e: e.scalar_tensor_tensor(out=oap[:, ts], in0=oap[:, ts], scalar=gcol, in1=tap,
                                                                    op0=ALU.mult, op1=ALU.mult),
                         reads=[ob, tb, self.gainsb], writes=[ob])
                for tt in range(self.NTT):
                    ts = self.ts(tt)
                    k.op("scalar", lambda e: e.activation(out=zap[:, ts], in_=zap[:, ts], func=AF.Silu), reads=[zb], writes=[zb])
                    k.op("gpsimd", lambda e: e.tensor_tensor(out=dap[:, ts], in0=oap[:, ts], in1=zap[:, ts], op=ALU.mult),
                         reads=[ob, zb], writes=[bufs[tt]])
        self.item([], fn)
        for dc in range(8):
            def fn2(tiles, dc=dc):
                (wb, wap), = tiles
                for tt in range(self.NTT):
                    ts = self.ts(tt)
                    pb, pap = self.ps.next()
                    for rc in range(16):
                        bufs, oap = self.obuf(rc)
                        k.op("tensor", lambda e: e.matmul(pap, lhsT=wap[:, rc, :], rhs=oap[:, ts],
                                                          start=(rc == 0), stop=(rc == 15)),
                             reads=[wb, bufs[tt]], writes=[pb])
                    k.op("vector", lambda e: e.tensor_tensor(out=self.hT[:, dc, ts], in0=pap, in1=self.hT[:, dc, ts], op=ALU.add),
                         reads=[pb, self.hTb[dc][tt]], writes=[self.hTb[dc][tt]])
            self.item([(w_out, 0, 16, dc * P, P)], fn2)
SEQ = 4096
NBLK = 32
BIG = 30000.0
SHIFT = 8.0


class AttnProg:
    def __init__(self, nc, es, D, safe_same=True, k=None, pfx=""):
        self.nc = nc
        self.es = es
        k = self.k = k if k is not None else K(nc, es, safe_same=safe_same)
        self.D = D

        def alloc(name, shape, dt):
            return es.enter_context(nc.sbuf_tensor(pfx + "sa_" + name, shape, dt))

        def rr(name, shape, dt, n):
            t = alloc(name, [shape[0], n] + list(shape[1:]), dt)
            return RR([(k.buf("%s%d" % (name, i)), t[:, i]) for i in range(n)])

        self.alloc = alloc
        self.cst = alloc("cst", [P, 128 * 4 + 4 * 512 + 2 * 512], BF16)
        self.cst32 = alloc("cst32", [P, 128 * 4 + 4 * 512 + 2 * 512], F32)
        self.cb = k.buf("cst")
        self.small = alloc("small", [P, 8 + 512], F32)
        self.smallb = k.buf("small")
        self.stage = rr("stage", [P, SEQ], F32, 2)
        self.qT = rr("qT", [64, SEQ], BF16, 2)
        self.kT = rr("kT", [64, SEQ], BF16, 2)
        self.nkT = rr("nkT", [64, SEQ], BF16, 2)
        self.v = rr("v", [P, NBLK * 64], BF16, 2)
        self.e32 = rr("e32", [P, 512], F32, 2)
        self.sp = rr("sp", [P, 512], BF16, 3)
        self.w = rr("w", [P, 512], BF16, 2)
        self.ls = rr("ls", [P, 512], BF16, 2)
        self.ost = rr("ost", [64, 512], F32, 3)
        ps = [es.enter_context(nc.psum_tensor(pfx + "psa%d" % i, [P, 512], F32)) for i in range(8)]
        self.psz = RR([(k.buf("psz%d" % i), ps[i][:, :]) for i in range(2)])
        self.psc = RR([(k.buf("psc%d" % i), ps[2 + i][:, :]) for i in range(2)])
        self.pso = RR([(k.buf("pso%d" % i), ps[4 + i][:, :]) for i in range(2)])
        self.psd = RR([(k.buf("psd%d" % i), ps[6 + i][:, :]) for i in range(2)])

    def consts(self):
        k, D = self.k, self.D
        n = 128 * 4 + 4 * 512 + 2 * 512
        k.dma("sync", [(self.cst32[:], D["cst"])], writes=[self.cb])
        k.op("vector", lambda e: e.tensor_copy(out=self.cst[:], in_=self.cst32[:]), reads=[self.cb], writes=[self.cb])
        k.dma("sync", [(self.small[:], D["small"])], writes=[self.smallb])
        c = self.cst
        self.tri = c[:, 0:128]
        self.ident = c[:, 128:256]
        self.nident = c[:, 256:384]
        self.ones = c[:, 384:512]
        self.masks = [c[:, 512 + j * 512:512 + (j + 1) * 512] for j in range(4)]
        self.swab = [c[:, 2560 + j * 512:2560 + (j + 1) * 512] for j in range(2)]
        s = self.small
        self.gq = s[0:64, 0:1]
        self.gk = s[0:64, 1:2]
        self.one = s[:, 2:3]
        self.nshift = s[:, 3:4]
        self.eps = s[:, 4:5]
        self.eps64 = s[:, 5:6]
        self.sinks = s[0:64, 8:520]
        k.op("scalar", lambda e: e.activation(out=self.sinks, in_=self.sinks, func=AF.Exp, bias=self.nshift[0:64, :]),
             reads=[self.smallb], writes=[self.smallb])

    def load_sb_head(self, h):
        k, D = self.k, self.D
        qb, qap = self.qT.next()
        kb_, kap = self.kT.next()
        nb, nap = self.nkT.next()
        vb, vap = self.v.next()
        sb, sap = self.stage.next()
        k.dma("sync", [(sap[0:64, :], D["sqT"][h * 64:(h + 1) * 64, :])], writes=[sb])
        k.op("gpsimd", lambda e: e.tensor_scalar(out=qap, in0=sap[0:64, :], scalar1=0.125, scalar2=None, op0=ALU.mult),
             reads=[sb], writes=[qb])
        sb, sap = self.stage.next()
        k.dma("sync", [(sap[0:64, :], D["skT"][h * 64:(h + 1) * 64, :])], writes=[sb])
        k.op("gpsimd", lambda e: e.tensor_copy(out=kap, in_=sap[0:64, :]), reads=[sb], writes=[kb_])
        k.op("gpsimd", lambda e: e.tensor_scalar(out=nap, in0=sap[0:64, :], scalar1=-1.0, scalar2=None, op0=ALU.mult),
             reads=[sb], writes=[nb])
        sb, sap = self.stage.next()
        k.dma("sync", [(sap[:, 0:NBLK * 64].rearrange("p (b d) -> p b d", d=64), D["sv"][h])], writes=[sb])
        k.op("gpsimd", lambda e: e.tensor_copy(out=vap, in_=sap[:, 0:NBLK * 64]), reads=[sb], writes=[vb])
        return (qb, qap, kb_, kap, nb, nap, vb, vap)

    def sb_head(self, h, tiles):
        k, D = self.k, self.D
        (qb, qap, kb_, kap, nb, nap, vb, vap) = tiles
        cb = self.cb
        for qs in range(8):
            q_sl = slice(qs * 512, (qs + 1) * 512)
            pob, po = self.pso.next()
            lsum = None
            kbs = list(range(4 * qs + 3, -1, -1))
            for idx, kb in enumerate(kbs):
                k_sl = slice(kb * 128, (kb + 1) * 128)
                j = kb - 4 * qs
                diag = j >= 0
                pzb, pz = self.psz.next()
                k.op("tensor", lambda e: e.matmul(pz, lhsT=kap[:, k_sl], rhs=qap[:, q_sl], start=True, stop=not diag),
                     reads=[kb_, qb], writes=[pzb])
                if diag:
                    k.op("tensor", lambda e: e.matmul(pz, lhsT=self.nident, rhs=self.masks[j], start=False, stop=True),
                         reads=[cb], writes=[pzb])
                eb, eap = self.e32.next()
                k.op("scalar", lambda e: e.activation(out=eap, in_=pz, func=AF.Exp), reads=[pzb], writes=[eb])
                spb, spap = self.sp.next()
                k.op("scalar", lambda e: e.activation(out=spap, in_=eap, func=AF.Ln, bias=self.one),
                     reads=[eb, self.smallb], writes=[spb])
                pcb, pc = self.psc.next()
                k.op("tensor", lambda e: e.matmul(pc, lhsT=self.tri, rhs=spap, start=True, stop=False),
                     reads=[cb, spb], writes=[pcb])
                if lsum is not None:
                    k.op("tensor", lambda e: e.matmul(pc, lhsT=self.ones, rhs=lsum[1], start=False, stop=False),
                         reads=[cb, lsum[0]], writes=[pcb])
                if diag:
                    k.op("tensor", lambda e: e.matmul(pc, lhsT=self.ident, rhs=self.masks[j], start=False, stop=False),
                         reads=[cb], writes=[pcb])
                k.op("tensor", lambda e: e.matmul(pc, lhsT=nap[:, k_sl], rhs=qap[:, q_sl], start=False, stop=True),
                     reads=[nb, qb], writes=[pcb])
                wb, wap = self.w.next()
                k.op("scalar", lambda e: e.activation(out=wap, in_=pc, func=AF.Exp, scale=-1.0), reads=[pcb], writes=[wb])
                k.op("tensor", lambda e: e.matmul(po[0:64, :], lhsT=vap[:, kb * 64:(kb + 1) * 64], rhs=wap,
                                                  start=(idx == 0), stop=(idx == len(kbs) - 1)),
                     reads=[vb, wb], writes=[pob])
                if idx < len(kbs) - 1:
                    if lsum is None:
                        lsum = (spb, spap)
                    else:
                        lb, lap = self.ls.next()
                        k.op("gpsimd", lambda e: e.tensor_tensor(out=lap, in0=lsum[1], in1=spap, op=ALU.add),
                             reads=[lsum[0], spb], writes=[lb])
                        lsum = (lb, lap)
            ob, oap = self.ost.next()
            k.op("vector", lambda e: e.tensor_copy(out=oap, in_=po[0:64, :]), reads=[pob], writes=[ob])
            if callable(D["oT"]):
                dst = D["oT"](h * 64, (h + 1) * 64)[:, q_sl]
            else:
                dst = D["oT"][h * 64:(h + 1) * 64, q_sl]
            k.dma("sync", [(dst, oap)], reads=[ob], final=self.final)

    def swa(self):
        k, D = self.k, self.D
        cb = self.cb
        alloc = self.alloc
        qn = alloc("qn", [64, 4, SEQ], BF16)
        qnb = [k.buf("qn%d" % i) for i in range(4)]
        kn = alloc("kn", [64, SEQ], BF16)
        knb = k.buf("kn")
        sq = RR([(k.buf("ssq%d" % i), alloc("ssq%d" % i, [64, 512], BF16)[:, :]) for i in range(2)])
        t32 = RR([(k.buf("st32_%d" % i), alloc("st32_%d" % i, [64, 512], F32)[:, :]) for i in range(4)])
        vb, vap = self.v.next()
        sb, sap = self.stage.next()
        k.dma("sync", [(sap[:, 0:NBLK * 64].rearrange("p (b d) -> p b d", d=64), D["bv"])], writes=[sb])
        k.op("gpsimd", lambda e: e.tensor_copy(out=vap, in_=sap[:, 0:NBLK * 64]), reads=[sb], writes=[vb])

        def qknorm(src_dram, dst_ap, dst_buf, gain, epsap):
            sb, sap = self.stage.next()
            k.dma("sync", [(sap[0:64, :], src_dram)], writes=[sb])
            for tt in range(8):
                ts = slice(tt * 512, (tt + 1) * 512)
                qb_, qap_ = sq.next()
                k.op("scalar", lambda e: e.activation(out=qap_, in_=sap[0:64, ts], func=AF.Square), reads=[sb], writes=[qb_])
                pb, pap = self.psd.next()
                k.op("tensor", lambda e: e.matmul(pap[0:64, :], lhsT=self.ones[0:64, 0:64], rhs=qap_, start=True, stop=True),
                     reads=[qb_, cb], writes=[pb])
                tb, tap = t32.next()
                sc = 1.0 if epsap is self.eps64 else 1.0 / 64.0
                k.op("scalar", lambda e: e.activation(out=tap, in_=pap[0:64, :], func=AF.Sqrt, scale=sc, bias=epsap[0:64, :]),
                     reads=[pb, self.smallb], writes=[tb])
                rb, rap = t32.next()
                k.op("vector", lambda e: e.reciprocal(out=rap, in_=tap), reads=[tb], writes=[rb])
                k.op("vector", lambda e: e.scalar_tensor_tensor(out=dst_ap[:, ts], in0=sap[0:64, ts], scalar=gain, in1=rap,
                                                                op0=ALU.mult, op1=ALU.mult),
                     reads=[sb, rb, self.smallb], writes=[dst_buf])

        qknorm(D["bkT"], kn[:, :], knb, self.gk, self.eps)
        for hl in range(4):
            qknorm(D["bqT"][hl * 64:(hl + 1) * 64, :], qn[:, hl, :], qnb[hl], self.gq, self.eps64)

        pw = RR([(k.buf("pw%d" % i), alloc("pw%d" % i, [P, 512], BF16)[:, :]) for i in range(3)])
        ost = RR([(k.buf("so%d" % i), alloc("so%d" % i, [64, 4, 512], F32)) for i in range(2)])
        den = RR([(k.buf("dn%d" % i), alloc("dn%d" % i, [64, 512], F32)[:, :]) for i in range(2)])
        for qg in range(8):
            osb, osap = ost.next()
            for qi in range(4):
                qb = qg * 4 + qi
                q_sl = slice(qb * 128, (qb + 1) * 128)
                pob, po = self.pso.next()
                pdb, pd = self.psd.next()
                kbl = [qb] if qb == 0 else [qb - 1, qb]
                for ii, kb in enumerate(kbl):
                    k_sl = slice(kb * 128, (kb + 1) * 128)
                    which = 1 if kb == qb else 0
                    pzb, pz = self.psz.next()
                    k.op("tensor", lambda e: e.matmul(pz.rearrange("p (h q) -> p h q", h=4), lhsT=kn[:, k_sl], rhs=qn[:, :, q_sl], start=True, stop=False),
                         reads=[knb] + qnb, writes=[pzb])
                    k.op("tensor", lambda e: e.matmul(pz, lhsT=self.ident, rhs=self.swab[which], start=False, stop=True),
                         reads=[cb], writes=[pzb])
                    wb, wap = pw.next()
                    k.op("scalar", lambda e: e.activation(out=wap, in_=pz, func=AF.Exp, bias=self.nshift),
                         reads=[pzb, self.smallb], writes=[wb])
                    k.op("tensor", lambda e: e.matmul(po[0:64, :], lhsT=vap[:, kb * 64:(kb + 1) * 64], rhs=wap,
                                                      start=(ii == 0), stop=(ii == len(kbl) - 1)),
                         reads=[vb, wb], writes=[pob])
                    k.op("tensor", lambda e: e.matmul(pd[0:64, :], lhsT=self.ones[:, 0:64], rhs=wap,
                                                      start=(ii == 0), stop=(ii == len(kbl) - 1)),
                         reads=[cb, wb], writes=[pdb])
                db, dap = den.next()
                k.op("vector", lambda e: e.tensor_tensor(out=dap, in0=pd[0:64, :], in1=self.sinks, op=ALU.add),
                     reads=[pdb, self.smallb], writes=[db])
                k.op("vector", lambda e: e.reciprocal(out=dap, in_=dap), reads=[db], writes=[db])
                k.op("vector", lambda e: e.tensor_tensor(out=osap[:, :, qi * 128:(qi + 1) * 128],
                                                         in0=po[0:64, :].rearrange("p (h q) -> p h q", h=4),
                                                         in1=dap.rearrange("p (h q) -> p h q", h=4), op=ALU.mult),
                     reads=[pob, db], writes=[osb])
            if callable(D["oT"]):
                pairs = [(D["oT"]((4 + hl) * 64, (5 + hl) * 64)[:, qg * 512:(qg + 1) * 512], osap[:, hl, :]) for hl in range(4)]
            else:
                pairs = [(D["oT"][(4 + hl) * 64:(5 + hl) * 64, qg * 512:(qg + 1) * 512], osap[:, hl, :]) for hl in range(4)]
            k.dma("sync", pairs, reads=[osb], final=self.final)

    final = True

    def emit(self, finish=True):
        self.consts()
        nxt = self.load_sb_head(0)
        for h in range(4):
            cur = nxt
            if h < 3:
                nxt = self.load_sb_head(h + 1)
            self.sb_head(h, cur)
        self.swa()
        if finish:
            self.k.finish()


def attn_consts(half):
    n = 128 * 4 + 4 * 512 + 2 * 512
    c = np.zeros((128, n), np.float32)
    kk = np.arange(128)[:, None]
    qq = np.arange(128)[None, :]
    c[:, 0:128] = (kk >= qq)
    c[:, 128:256] = np.eye(128)
    c[:, 256:384] = -np.eye(128)
    c[:, 384:512] = 1.0
    ql = np.arange(512)[None, :]
    for j in range(4):
        c[:, 512 + j * 512:512 + (j + 1) * 512] = np.where(j * 128 + kk >= ql, BIG, 0.0)
    for hl in range(4):
        slope = 2.0 ** (-(4 * half + hl + 1))
        dist_prev = qq + 128 - kk
        dist_cur = qq - kk
        c[:, 2560 + hl * 128:2560 + (hl + 1) * 128] = np.where(kk > qq, -slope * dist_prev, -BIG)
        c[:, 3072 + hl * 128:3072 + (hl + 1) * 128] = np.where(kk <= qq, -slope * dist_cur, -BIG)
    return c
GC = 128
NCH = 32
GBIG = 30000.0
LN_QSCALE = -0.5 * 4.852030263919617


def run_streams(streams):
    streams = list(streams)
    while streams:
        for s in list(streams):
            try:
                next(s)
            except StopIteration:
                streams.remove(s)


class GdnProg:
    def __init__(self, nc, es, D, safe_same=True, inv_fp32=True, k=None, pfx="", fused=False):
        self.nc = nc
        self.es = es
        self.fused = fused
        k = self.k = k if k is not None else K(nc, es, safe_same=safe_same)
        self.D = D
        self.inv_fp32 = inv_fp32

        def alloc(name, shape, dt):
            return es.enter_context(nc.sbuf_tensor(pfx + "sg_" + name, shape, dt))

        def rr(name, shape, dt, n):
            t = alloc(name, [shape[0], n] + list(shape[1:]), dt)
            return RR([(k.buf("%s%d" % (name, i)), t[:, i]) for i in range(n)])

        self.alloc = alloc
        self.rr = rr
        self.c32 = alloc("c32", [P, 6 * 128], F32)
        self.cb = k.buf("c32")
        self.identb = alloc("identb", [P, P], BF16)
        self.onesb = alloc("onesb", [P, P], BF16)
        self.small = alloc("small", [P, 8], F32)
        self.convw = alloc("convw", [P, 64], F32)
        self.diagw = alloc("diagw", [P, 64, P], BF16)
        self.gt = alloc("gates", [P, 6, 256], F32)
        self.gb = k.buf("gates")
        self.xst = rr("xst", [P, 515], F32, 2)
        self.xb = rr("xb", [P, 515], BF16, 2)
        qT = alloc("qT", [P, 2, 4, 512], BF16)
        kT = alloc("kT", [P, 2, 4, 512], BF16)
        vt = alloc("vt", [P, 2, 4, 8, P], BF16)
        self.qT, self.kT, self.vt = qT, kT, vt
        self.qTb = [[k.buf("qT%d_%d" % (s, m)) for m in range(4)] for s in range(2)]
        self.kTb = [[k.buf("kT%d_%d" % (s, m)) for m in range(4)] for s in range(2)]
        self.vtb = [[k.buf("vt%d_%d" % (s, h)) for h in range(8)] for s in range(2)]
        self.e32 = rr("e32", [P, 512], F32, 3)
        self.y32 = rr("y32", [P, 512], F32, 2)
        self.yb = rr("yb", [P, 512], BF16, 2)
        self.sq = rr("sq", [P, 512], BF16, 2)
        self.r32 = rr("r32", [P, 512], F32, 2)
        self.egc = rr("egc", [P, 24], F32, 3)
        self.Xt = alloc("Xt", [P, 2, 8, P], BF16)
        self.AT = alloc("AT", [P, 2, 8, P], BF16)
        self.kh = alloc("kh", [P, 2, 8, P], BF16)
        self.Xtb = [[k.buf("Xt%d_%d" % (s, h)) for h in range(8)] for s in range(2)]
        self.ATb = [[k.buf("AT%d_%d" % (s, h)) for h in range(8)] for s in range(2)]
        self.khb = [[k.buf("kh%d_%d" % (s, h)) for h in range(8)] for s in range(2)]
        self.f32t = rr("f32t", [P, P], F32, 48)
        self.S = alloc("S", [P, 8, P], F32)
        self.Sbf = alloc("Sbf", [P, 8, P], BF16)
        self.Sb = [k.buf("S%d" % h) for h in range(8)]
        self.Sbfb = [k.buf("Sbf%d" % h) for h in range(8)]
        self.Rb = rr("R", [P, P], BF16, 8)
        self.vn = rr("vn", [P, P], BF16, 8)
        self.tmp = rr("tmp", [P, P], F32, 8)
        self.ost = alloc("ost", [P, 2, 8, P], F32)
        self.ostT = alloc("ostT", [P, 2, 8, P], F32)
        self.ostTb = [[k.buf("ostT%d_%d" % (s, h)) for h in range(8)] for s in range(2)]
        self.ostb = [[k.buf("ost%d_%d" % (s, h)) for h in range(8)] for s in range(2)]
        pc = [es.enter_context(nc.psum_tensor(pfx + "pgc%d" % i, [P, 512], F32)) for i in range(1)]
        self.pc = RR([(k.buf("pgc%d" % i), pc[i][:, :]) for i in range(1)])
        pb = es.enter_context(nc.psum_tensor(pfx + "pgb", [P, 1024], BF16))
        pbb = k.buf("pgb")
        pbb.excl = True
        self.pvt = (pbb, pb[:, 0:512])
        self.pkt = RR([(pbb, pb[:, 512 + i * 128:512 + (i + 1) * 128]) for i in range(3)])
        self.fence_ap = pb[0:1, 896:1024]
        self.pbb = pbb
        pq = [es.enter_context(nc.psum_tensor(pfx + "pgq%d" % i, [P, 512], F32)) for i in range(6)]
        bq = [k.buf("pgq%d" % i) for i in range(6)]
        for b in bq + [self.pc.items[0][0]]:
            b.excl = True
        self.pqb = RR([(bq[i], pq[i]) for i in range(3)])
        self.psqb = RR([(bq[i], pq[i]) for i in range(3, 6)])

    def fence(self, banks):
        self.k.op("tensor", lambda e: e.transpose(self.fence_ap, self.identb[:, 0:1], self.identb[:]),
                  reads=[self.cb], writes=[self.pbb] + list(banks))

    def phase0(self):
        k, D = self.k, self.D
        k.dma("sync", [(self.c32[:], D["c32"])], writes=[self.cb])
        c = self.c32
        self.LE = c[:, 0:128]
        self.GT = c[:, 128:256]
        self.MASKB = c[:, 256:384]
        self.I32 = c[:, 384:512]
        self.ONES32 = c[:, 512:640]
        self.STRICT = c[:, 640:768]
        k.op("vector", lambda e: e.tensor_copy(out=self.identb[:], in_=self.I32), reads=[self.cb], writes=[self.cb])
        k.op("vector", lambda e: e.tensor_copy(out=self.onesb[:], in_=self.ONES32), reads=[self.cb], writes=[self.cb])
        k.dma("sync", [(self.small[:], D["small"]), (self.convw[:], D["convw"])], writes=[self.cb])
        self.one = self.small[:, 0:1]
        self.eps = self.small[:, 1:2]
        self.lnq = self.small[:, 2:3]
        self.zero = self.small[:, 3:4]
        for i in range(64):
            k.op("gpsimd", lambda e: e.tensor_scalar(out=self.diagw[:, i, :], in0=self.identb[:], scalar1=self.convw[:, i:i + 1],
                                                     scalar2=None, op0=ALU.mult), reads=[self.cb], writes=[self.cb])
        g = self.gt
        if self.fused:
            gt = D["gtok"]
            k.dma("sync", [(g[:, 0, :].rearrange("p (c h) -> p c h", h=8), gt[:, 0:8].rearrange("(c p) h -> p c h", p=P)),
                           (g[:, 1, :].rearrange("p (c h) -> p c h", h=8), gt[:, 8:16].rearrange("(c p) h -> p c h", p=P)),
                           (g[:, 2:4, :], D["gconst"])], writes=[self.gb])
        else:
            k.dma("sync", [(g[:, 0:4, :], D["gates"])], writes=[self.gb])
        A, BL, DTB, ALOG, G, BETA = (g[:, i, :] for i in range(6))
        gb = [self.gb]
        k.op("vector", lambda e: e.tensor_tensor(out=A, in0=A, in1=DTB, op=ALU.add), reads=gb, writes=gb)
        k.op("scalar", lambda e: e.activation(out=A, in_=A, func=AF.Exp), reads=gb, writes=gb)
        k.op("scalar", lambda e: e.activation(out=A, in_=A, func=AF.Ln, bias=self.one), reads=gb + [self.cb], writes=gb)
        k.op("scalar", lambda e: e.activation(out=ALOG, in_=ALOG, func=AF.Exp), reads=gb, writes=gb)
        k.op("vector", lambda e: e.scalar_tensor_tensor(out=G, in0=A, scalar=-1.0, in1=ALOG, op0=ALU.mult, op1=ALU.mult),
             reads=gb, writes=gb)
        k.op("scalar", lambda e: e.activation(out=BL, in_=BL, func=AF.Exp, scale=-1.0), reads=gb, writes=gb)
        k.op("vector", lambda e: e.tensor_scalar(out=BL, in0=BL, scalar1=1.0, scalar2=None, op0=ALU.add), reads=gb, writes=gb)
        k.op("vector", lambda e: e.reciprocal(out=BETA, in_=BL), reads=gb, writes=gb)
        self.G, self.BETA = G, BETA
        k.op("vector", lambda e: e.memset(self.S[:], 0.0), writes=self.Sb)
        k.op("vector", lambda e: e.memset(self.Sbf[:], 0.0), writes=self.Sbfb)

    def prologue_rows(self, t, rows):
        k, D = self.k, self.D
        slot = t % 2
        cb = self.cb
        for r in rows:
            sb, sap = self.xst.next()
            if not self.fused:
                k.dma("sync", [(sap, D["xT"][r * P:(r + 1) * P, t * 512:t * 512 + 515])], writes=[sb])
            elif t == 0:
                k.op("gpsimd", lambda e: e.memset(sap[:, 0:3], 0.0), writes=[sb])
                k.dma("sync", [(sap[:, 3:515], D["xT"][r * P:(r + 1) * P, 0:512])], writes=[sb])
            else:
                k.dma("sync", [(sap, D["xT"][r * P:(r + 1) * P, t * 512 - 3:t * 512 + 512])], writes=[sb])
            xbb, xbap = self.xb.next()
            k.op("gpsimd", lambda e: e.tensor_copy(out=xbap, in_=sap), reads=[sb], writes=[xbb])
            yield
            pcb, pcap = self.pc.next()
            for j in range(4):
                k.op("tensor", lambda e: e.matmul(pcap, lhsT=self.diagw[:, r * 4 + j, :], rhs=xbap[:, j:j + 512],
                                                  start=(j == 0), stop=(j == 3)), reads=[cb, xbb], writes=[pcb])
            yield
            eb, eap = self.e32.next()
            k.op("scalar", lambda e: e.activation(out=eap, in_=pcap, func=AF.Exp, scale=-1.0), reads=[pcb], writes=[eb])
            yield
            k.op("scalar", lambda e: e.activation(out=eap, in_=eap, func=AF.Ln, bias=self.one), reads=[eb, cb], writes=[eb])
            yield
            k.op("scalar", lambda e: e.activation(out=eap, in_=eap, func=AF.Exp, scale=-1.0), reads=[eb], writes=[eb])
            yield
            if r < 8:
                yb_, yap = self.y32.next()
            else:
                yb_, yap = self.yb.next()
            k.op("vector", lambda e: e.tensor_tensor(out=yap, in0=pcap, in1=eap, op=ALU.mult), reads=[pcb, eb], writes=[yb_])
            yield
            if r < 8:
                m = r % 4
                qb_, qap_ = self.sq.next()
                k.op("scalar", lambda e: e.activation(out=qap_, in_=yap, func=AF.Square), reads=[yb_], writes=[qb_])
                yield
                pnb, pnap = self.pc.next()
                k.op("tensor", lambda e: e.matmul(pnap, lhsT=self.onesb[:], rhs=qap_, start=True, stop=True),
                     reads=[qb_, cb], writes=[pnb])
                yield
                rb, rap = self.r32.next()
                k.op("scalar", lambda e: e.activation(out=rap, in_=pnap, func=AF.Ln, bias=self.eps), reads=[pnb, cb], writes=[rb])
                yield
                bias = self.lnq if r < 4 else self.zero
                k.op("scalar", lambda e: e.activation(out=rap, in_=rap, func=AF.Exp, scale=-0.5, bias=bias), reads=[rb, cb], writes=[rb])
                yield
                if r < 4:
                    dst, dbuf = self.qT[:, slot, m, :], self.qTb[slot][m]
                else:
                    dst, dbuf = self.kT[:, slot, m, :], self.kTb[slot][m]
                k.op("vector", lambda e: e.tensor_tensor(out=dst, in0=yap, in1=rap, op=ALU.mult), reads=[yb_, rb], writes=[dbuf])
                yield
            else:
                hv = r - 8
                pvb, pvap = self.pvt
                for cc in range(4):
                    k.op("tensor", lambda e: e.transpose(pvap[:, cc * P:(cc + 1) * P], yap[:, cc * P:(cc + 1) * P], self.identb[:]),
                         reads=[yb_, cb], writes=[pvb])
                yield
                k.op("vector", lambda e: e.tensor_copy(out=self.vt[:, slot, :, hv, :], in_=pvap.rearrange("p (c d) -> p c d", c=4)),
                     reads=[pvb], writes=[self.vtb[slot][hv]])
                yield

    def pre(self, c):
        k = self.k
        cb = self.cb
        t, c4 = c // 4, c % 4
        slot = t % 2
        cs = c % 2
        csl = slice(c4 * P, (c4 + 1) * P)
        egb, egap = self.egc.next()
        self.egcur = getattr(self, "egcur", {})
        self.egcur[c] = (egb, egap)
        pb, pap = self.pqb.next()
        k.op("tensor", lambda e: e.matmul(pap[:, 0:8], lhsT=self.LE, rhs=self.G[:, c * 8:(c + 1) * 8], start=True, stop=True),
             reads=[cb, self.gb], writes=[pb])
        k.op("tensor", lambda e: e.matmul(pap[:, 8:16], lhsT=self.ONES32, rhs=self.G[:, c * 8:(c + 1) * 8], start=True, stop=True),
             reads=[cb, self.gb], writes=[pb])
        self.fence([pb])
        k.op("scalar", lambda e: e.activation(out=egap[:, 0:16], in_=pap[:, 0:16], func=AF.Exp), reads=[pb], writes=[egb])
        k.op("vector", lambda e: e.tensor_scalar(out=egap[:, 16:24], in0=egap[:, 0:8], scalar1=-1.0, scalar2=None, op0=ALU.mult),
             reads=[egb], writes=[egb])
        yield
        for grp in range(2):
            heads = list(range(4 * grp, 4 * grp + 4))
            kheads = [2 * grp, 2 * grp + 1]
            Gm = {}
            for h in heads:
                Gm[h] = self.f32t.next()
                k.op("gpsimd", lambda e: e.tensor_scalar(out=Gm[h][1], in0=self.GT, scalar1=self.G[:, c * 8 + h:c * 8 + h + 1],
                                                         scalar2=None, op0=ALU.mult), reads=[cb, self.gb], writes=[Gm[h][0]])
            yield
            pZ = {}
            bk = self.pqb.next()
            for h in heads:
                pZ[h] = (bk[0], bk[1][:, (h % 4) * P:(h % 4 + 1) * P])
                k.op("tensor", lambda e: e.matmul(pZ[h][1], lhsT=Gm[h][1], rhs=self.LE, start=True, stop=False),
                     reads=[Gm[h][0], cb], writes=[pZ[h][0]])
                k.op("tensor", lambda e: e.matmul(pZ[h][1], lhsT=self.I32, rhs=self.MASKB, start=False, stop=True),
                     reads=[cb], writes=[pZ[h][0]])
            self.fence([bk[0]])
            yield
            Dm = {}
            for h in heads:
                Dm[h] = self.f32t.next()
                k.op("scalar", lambda e: e.activation(out=Dm[h][1], in_=pZ[h][1], func=AF.Exp), reads=[pZ[h][0]], writes=[Dm[h][0]])
            yield
            pKQ, pKK, pkt, Dms = {}, {}, {}, {}
            bk = self.pqb.next()
            for m in kheads:
                kTc = self.kT[:, slot, m, csl]
                qTc = self.qT[:, slot, m, csl]
                pKQ[m] = (bk[0], bk[1][:, (m % 2) * P:(m % 2 + 1) * P])
                k.op("tensor", lambda e: e.matmul(pKQ[m][1], lhsT=kTc, rhs=qTc, start=True, stop=True),
                     reads=[self.kTb[slot][m], self.qTb[slot][m]], writes=[pKQ[m][0]])
                pKK[m] = (bk[0], bk[1][:, (2 + m % 2) * P:(3 + m % 2) * P])
                k.op("tensor", lambda e: e.matmul(pKK[m][1], lhsT=kTc, rhs=kTc, start=True, stop=True),
                     reads=[self.kTb[slot][m]], writes=[pKK[m][0]])
                pkt[m] = self.pkt.next()
                k.op("tensor", lambda e: e.transpose(pkt[m][1], kTc, self.identb[:]), reads=[self.kTb[slot][m], cb], writes=[pkt[m][0]])
            for h in heads:
                Dms[h] = self.f32t.next()
                k.op("gpsimd", lambda e: e.tensor_tensor(out=Dms[h][1], in0=Dm[h][1], in1=self.STRICT, op=ALU.mult),
                     reads=[Dm[h][0], cb], writes=[Dms[h][0]])
            yield
            Pt, Q_, W = {}, {}, {}
            for h in heads:
                m = h // 2
                k.op("vector", lambda e: e.tensor_tensor(out=self.AT[:, cs, h, :], in0=pKQ[m][1], in1=Dm[h][1], op=ALU.mult),
                     reads=[pKQ[m][0], Dm[h][0]], writes=[self.ATb[cs][h]])
                k.op("vector", lambda e: e.tensor_scalar(out=self.kh[:, cs, h, :], in0=pkt[m][1], scalar1=Dm[h][1][:, 127:128],
                                                         scalar2=None, op0=ALU.mult),
                     reads=[pkt[m][0], Dm[h][0]], writes=[self.khb[cs][h]])
                Pt[h] = self.f32t.next()
                k.op("vector", lambda e: e.scalar_tensor_tensor(out=Pt[h][1], in0=pKK[m][1], scalar=self.BETA[:, c * 8 + h:c * 8 + h + 1],
                                                                in1=Dms[h][1], op0=ALU.mult, op1=ALU.mult),
                     reads=[pKK[m][0], self.gb, Dms[h][0]], writes=[Pt[h][0]])
            yield
            pT = {}
            bk = self.pqb.next()
            for h in heads:
                pT[h] = (bk[0], bk[1][:, (h % 4) * P:(h % 4 + 1) * P])
                k.op("tensor", lambda e: e.transpose(pT[h][1], Pt[h][1], self.I32), reads=[Pt[h][0], cb], writes=[pT[h][0]])
            self.fence([bk[0]])
            yield
            for h in heads:
                Q_[h] = self.f32t.next()
                k.op("scalar", lambda e: e.copy(out=Q_[h][1], in_=pT[h][1]), reads=[pT[h][0]], writes=[Q_[h][0]])
                W[h] = self.f32t.next()
                k.op("gpsimd", lambda e: e.tensor_tensor(out=W[h][1], in0=self.I32, in1=Pt[h][1], op=ALU.subtract),
                     reads=[cb, Pt[h][0]], writes=[W[h][0]])
            yield
            pP, pQ = {}, {}
            bkP = self.pqb.next()
            bkQ = self.pqb.next()
            for h in heads:
                pQ[h] = (bkQ[0], bkQ[1][:, (h % 4) * P:(h % 4 + 1) * P])
                k.op("tensor", lambda e: e.matmul(pQ[h][1], lhsT=Pt[h][1], rhs=Q_[h][1], start=True, stop=True),
                     reads=[Q_[h][0], Pt[h][0]], writes=[pQ[h][0]])
            for h in heads:
                pP[h] = (bkP[0], bkP[1][:, (h % 4) * P:(h % 4 + 1) * P])
                k.op("tensor", lambda e: e.matmul(pP[h][1], lhsT=Q_[h][1], rhs=Pt[h][1], start=True, stop=True),
                     reads=[Q_[h][0], Pt[h][0]], writes=[pP[h][0]])
            self.fence([bkP[0], bkQ[0]])
            yield
            nP, nQ = {}, {}
            for h in heads:
                nP[h] = self.f32t.next()
                k.op("scalar", lambda e: e.copy(out=nP[h][1], in_=pP[h][1]), reads=[pP[h][0]], writes=[nP[h][0]])
                nQ[h] = self.f32t.next()
                k.op("vector", lambda e: e.tensor_copy(out=nQ[h][1], in_=pQ[h][1]), reads=[pQ[h][0]], writes=[nQ[h][0]])
            Pt, Q_ = nP, nQ
            yield
            for step in range(1, 7):
                pW, pP, pQ = {}, {}, {}
                bkW = self.pqb.next()
                for h in heads:
                    pW[h] = (bkW[0], bkW[1][:, (h % 4) * P:(h % 4 + 1) * P])
                    k.op("tensor", lambda e: e.matmul(pW[h][1], lhsT=Q_[h][1], rhs=W[h][1], start=True, stop=True),
                         reads=[Q_[h][0], W[h][0]], writes=[pW[h][0]])
                if step <= 5:
                    bkQ = self.pqb.next()
                    for h in heads:
                        pQ[h] = (bkQ[0], bkQ[1][:, (h % 4) * P:(h % 4 + 1) * P])
                        k.op("tensor", lambda e: e.matmul(pQ[h][1], lhsT=Pt[h][1], rhs=Q_[h][1], start=True, stop=True),
                             reads=[Q_[h][0], Pt[h][0]], writes=[pQ[h][0]])
                if step <= 4:
                    bkP = self.pqb.next()
                    for h in heads:
                        pP[h] = (bkP[0], bkP[1][:, (h % 4) * P:(h % 4 + 1) * P])
                        k.op("tensor", lambda e: e.matmul(pP[h][1], lhsT=Q_[h][1], rhs=Pt[h][1], start=True, stop=True),
                             reads=[Q_[h][0], Pt[h][0]], writes=[pP[h][0]])
                self.fence([bkW[0]] + ([bkQ[0]] if step <= 5 else []) + ([bkP[0]] if step <= 4 else []))
                yield
                nW, nP, nQ = {}, {}, {}
                for h in heads:
                    if step < 6:
                        nW[h] = self.f32t.next()
                        k.op("vector", lambda e: e.tensor_tensor(out=nW[h][1], in0=pW[h][1], in1=W[h][1], op=ALU.add),
                             reads=[pW[h][0], W[h][0]], writes=[nW[h][0]])
                    else:
                        k.op("vector", lambda e: e.tensor_tensor(out=self.Xt[:, cs, h, :], in0=pW[h][1], in1=W[h][1], op=ALU.add),
                             reads=[pW[h][0], W[h][0]], writes=[self.Xtb[cs][h]])
                    if step <= 5:
                        nQ[h] = self.f32t.next()
                        k.op("scalar", lambda e: e.copy(out=nQ[h][1], in_=pQ[h][1]), reads=[pQ[h][0]], writes=[nQ[h][0]])
                    if step <= 4:
                        nP[h] = self.f32t.next()
                        k.op("scalar", lambda e: e.copy(out=nP[h][1], in_=pP[h][1]), reads=[pP[h][0]], writes=[nP[h][0]])
                W, Pt, Q_ = nW, nP, nQ
                yield

    def seq(self, c):
        for grp in range(2):
            yield from self.seq_grp(c, grp)

    def seq_grp(self, c, grp):
        k, D = self.k, self.D
        t, c4 = c // 4, c % 4
        slot = t % 2
        cs = c % 2
        csl = slice(c4 * P, (c4 + 1) * P)
        heads = list(range(4 * grp, 4 * grp + 4))
        egb, egap = self.egcur[c]
        p1, pa = {}, {}
        bk1 = self.psqb.next()
        bka = self.psqb.next()
        for h in heads:
            m = h // 2
            p1[h] = (bk1[0], bk1[1][:, (h % 4) * P:(h % 4 + 1) * P])
            k.op("tensor", lambda e: e.matmul(p1[h][1], lhsT=self.kT[:, slot, m, csl], rhs=self.Sbf[:, h, :], start=True, stop=True),
                 reads=[self.kTb[slot][m], self.Sbfb[h]], writes=[p1[h][0]])
            pa[h] = (bka[0], bka[1][:, (h % 4) * P:(h % 4 + 1) * P])
            k.op("tensor", lambda e: e.matmul(pa[h][1], lhsT=self.qT[:, slot, m, csl], rhs=self.Sbf[:, h, :], start=True, stop=True),
                 reads=[self.qTb[slot][m], self.Sbfb[h]], writes=[pa[h][0]])
        yield
        R, tmp = {}, {}
        for h in heads:
            R[h] = self.Rb.next()
            k.op("vector", lambda e: e.scalar_tensor_tensor(out=R[h][1], in0=p1[h][1], scalar=egap[:, 16 + h:17 + h],
                                                            in1=self.vt[:, slot, c4, h, :], op0=ALU.mult, op1=ALU.add),
                 reads=[p1[h][0], egb, self.vtb[slot][h]], writes=[R[h][0]])
            tmp[h] = self.tmp.next()
            k.op("scalar", lambda e: e.activation(out=tmp[h][1], in_=pa[h][1], func=AF.Copy, scale=egap[:, h:h + 1]),
                 reads=[pa[h][0], egb], writes=[tmp[h][0]])
        yield
        p2 = {}
        bk2 = self.psqb.next()
        for h in heads:
            p2[h] = (bk2[0], bk2[1][:, (h % 4) * P:(h % 4 + 1) * P])
            k.op("tensor", lambda e: e.matmul(p2[h][1], lhsT=self.Xt[:, cs, h, :], rhs=R[h][1], start=True, stop=True),
                 reads=[self.Xtb[cs][h], R[h][0]], writes=[p2[h][0]])
        yield
        vn = {}
        for h in heads:
            vn[h] = self.vn.next()
            k.op("scalar", lambda e: e.activation(out=vn[h][1], in_=p2[h][1], func=AF.Copy, scale=self.BETA[:, c * 8 + h:c * 8 + h + 1]),
                 reads=[p2[h][0], self.gb], writes=[vn[h][0]])
        yield
        pb_, p3 = {}, {}
        bkb = self.psqb.next()
        bk3 = self.psqb.next()
        for h in heads:
            pb_[h] = (bkb[0], bkb[1][:, (h % 4) * P:(h % 4 + 1) * P])
            k.op("tensor", lambda e: e.matmul(pb_[h][1], lhsT=self.AT[:, cs, h, :], rhs=vn[h][1], start=True, stop=True),
                 reads=[self.ATb[cs][h], vn[h][0]], writes=[pb_[h][0]])
            p3[h] = (bk3[0], bk3[1][:, (h % 4) * P:(h % 4 + 1) * P])
            k.op("tensor", lambda e: e.matmul(p3[h][1], lhsT=self.kh[:, cs, h, :], rhs=vn[h][1], start=True, stop=True),
                 reads=[self.khb[cs][h], vn[h][0]], writes=[p3[h][0]])
        yield
        for h in heads:
            k.op("vector", lambda e: e.tensor_tensor(out=self.ost[:, cs, h, :], in0=pb_[h][1], in1=tmp[h][1], op=ALU.add),
                 reads=[pb_[h][0], tmp[h][0]], writes=[self.ostb[cs][h]])
            k.op("vector", lambda e: e.scalar_tensor_tensor(out=self.S[:, h, :], in0=self.S[:, h, :], scalar=egap[:, 8 + h:9 + h],
                                                            in1=p3[h][1], op0=ALU.mult, op1=ALU.add),
                 reads=[self.Sb[h], egb, p3[h][0]], writes=[self.Sb[h]])
        yield
        for h in heads:
            k.op("gpsimd", lambda e: e.tensor_copy(out=self.Sbf[:, h, :], in_=self.S[:, h, :]), reads=[self.Sb[h]], writes=[self.Sbfb[h]])
        if not self.fused:
            if grp == 1:
                k.dma("sync", [(D["o_tok"][c * P:(c + 1) * P, :], self.ost[:, cs, :, :].rearrange("p h d -> p (h d)"))],
                      reads=self.ostb[cs], final=True)
            yield
            return
        bkt = self.psqb.next()
        for h in heads:
            k.op("tensor", lambda e: e.transpose(bkt[1][:, (h % 4) * P:(h % 4 + 1) * P], self.ost[:, cs, h, :], self.I32),
                 reads=[self.ostb[cs][h], self.cb], writes=[bkt[0]])
        self.fence([bkt[0]])
        yield
        for h in heads:
            k.op("scalar", lambda e: e.copy(out=self.ostT[:, cs, h, :], in_=bkt[1][:, (h % 4) * P:(h % 4 + 1) * P]),
                 reads=[bkt[0]], writes=[self.ostTb[cs][h]])
        if grp == 1:
            k.dma("sync", [(D["oT"](h * P, (h + 1) * P)[:, c * P:(c + 1) * P], self.ostT[:, cs, h, :]) for h in range(8)],
                  reads=self.ostTb[cs])
        yield

    def emit(self, nchunks=NCH, stop=None, finish=True):
        self.phase0()
        if stop == "phase0":
            self.k.finish(); return
        run_streams([self.prologue_rows(0, range(16) if stop != "pro1" else [0, 8])])
        if stop in ("pro", "pro1"):
            self.k.finish(); return
        if stop is not None and stop.startswith("pre"):
            n = int(stop[3:] or 1000)
            g = self.pre(0)
            for _ in range(n):
                try:
                    next(g)
                except StopIteration:
                    break
            self.k.finish(); return
        run_streams([self.pre(0)])
        pro = []
        ntiles = (nchunks + 3) // 4
        for c in range(nchunks):
            t = c // 4
            if c % 4 == 0 and t + 1 < ntiles:
                pro = [self.prologue_rows(t + 1, range(16))]
            main = [self.seq(c)]
            if c + 1 < nchunks and (c % 4 != 3):
                main.append(self.pre(c + 1))
            streams = list(main)
            while streams:
                for s_ in list(streams):
                    try:
                        next(s_)
                    except StopIteration:
                        streams.remove(s_)
                for s_ in list(pro):
                    try:
                        next(s_)
                        next(s_)
                    except StopIteration:
                        pro.remove(s_)
            if c % 4 == 3 and c + 1 < nchunks:
                run_streams(pro)
                pro = []
                run_streams([self.pre(c + 1)])
        if finish:
            self.k.finish()


def gdn_consts():
    c = np.zeros((128, 768), np.float32)
    p = np.arange(128)[:, None]
    i = np.arange(128)[None, :]
    c[:, 0:128] = (p <= i)
    c[:, 128:256] = (p > i)
    c[:, 256:384] = np.where(i < p, -GBIG, 0.0)
    c[:, 384:512] = np.eye(128)
    c[:, 512:640] = 1.0
    c[:, 640:768] = (p < i)
    return c
from concourse.bass_utils import run_bass_kernel_spmd

NCORES = 8
_DBG = {}
TOKC = 2048
PAIRS = [[0, 1], [2, 3], [4, 5], [6, 7]]


def _dram(nc, name, shape, kind="ExternalInput", dt=None):
    return nc.dram_tensor(name, list(shape), dt or F32, kind=kind).ap()


def _scratch(nc, name, shape, dt=None):
    return nc.dram_tensor(name, list(shape), dt or F32)


class Chunked:
    def __init__(self, nc, name, rows, cols, rc, dt=None, gathered=True):
        self.rc, self.rows, self.cols = rc, rows, cols
        self.n = rows // rc
        self.src = [nc.dram_tensor("%s_%d" % (name, j), [rc, cols], dt or F32) for j in range(self.n)]
        self.dst = [nc.dram_tensor("G%s_%d" % (name, j), [2 * rc, cols], dt or F32) for j in range(self.n)] if gathered else []

    def own(self, r0, r1):
        j = r0 // self.rc
        assert (r1 - 1) // self.rc == j
        return self.src[j].ap()[r0 - j * self.rc:r1 - j * self.rc, :]

    def gat(self, rank, r0, r1):
        j = r0 // self.rc
        assert (r1 - 1) // self.rc == j
        return self.dst[j].ap()[rank * self.rc + r0 - j * self.rc:rank * self.rc + r1 - j * self.rc, :]

    def gathers(self):
        return [(lambda g, s=s, d=d: g.collective_compute("AllGather", ALU.bypass, replica_groups=PAIRS,
                                                         ins=[s.ap().opt()], outs=[d.ap().opt()]))
                for s, d in zip(self.src, self.dst)]


def _gain_layout(vecs, extra=None):
    g = np.stack(vecs).astype(np.float32).reshape(len(vecs), 8, 128).transpose(2, 0, 1).reshape(128, len(vecs) * 8)
    col = np.zeros((128, 1), np.float32) if extra is None else np.asarray(extra, np.float32).reshape(128, 1)
    return np.ascontiguousarray(np.concatenate([g, col], 1))


def build_fused(stop=None, dump=None):
    nc = bass.Bass("TRN2", target_bir_lowering=False)
    I = {}
    def inp(name, shape):
        I[name] = _dram(nc, name, shape)
        return I[name]
    xT = inp("xT", [1024, TOKC])
    sel = inp("sel", [128, 2])
    gA = inp("gA", [128, 17]); gB = inp("gB", [128, 33]); gC = inp("gC", [128, 17])
    wg = [inp("wg%d" % i, [1024, 2816]) for i in range(4)]
    wu = [inp("wu%d" % i, [1024, 2816]) for i in range(4)]
    wd = [inp("wd%d" % i, [2816, 1024]) for i in range(4)]
    w_att_in = inp("w_att_in", [1024, 1152])
    w_att_out = inp("w_att_out", [1024, 1024])
    w_gdn_in = inp("w_gdn_in", [1024, 2064])
    w_gdn_z = inp("w_gdn_z", [1024, 2048])
    w_gdn_out = inp("w_gdn_out", [2048, 1024])
    wpg = [inp("wpg%d" % i, [1024, 1024]) for i in range(2)]
    wpp = [inp("wpp%d" % i, [256, 1024]) for i in range(2)]
    pT = [inp("pT%d" % i, [256, TOKC]) for i in range(2)]
    a_cst = inp("a_cst", [128, 3584]); a_small = inp("a_small", [128, 520])
    g_c32 = inp("g_c32", [128, 768]); g_small = inp("g_small", [128, 8]); g_convw = inp("g_convw", [128, 64])
    g_gconst = inp("g_gconst", [128, 2, 256])
    outT = _dram(nc, "outT", [1024, TOKC], "ExternalOutput")
    hsp = _scratch(nc, "hsp", [1024, TOKC])
    hnA = Chunked(nc, "hnA", 1024, TOKC, 512, BF16)
    projA = _scratch(nc, "projA", [832, 4096]); vtokA = _scratch(nc, "vtokA", [4096, 320])
    oA = Chunked(nc, "oA", 512, 4096, 128)
    zB = _scratch(nc, "zB", [2048, TOKC])
    hnB = Chunked(nc, "hnB", 1024, TOKC, 512, BF16)
    projB = _scratch(nc, "projB", [2048, 4096]); gtokB = _scratch(nc, "gtokB", [4096, 16])
    oB = Chunked(nc, "oB", 1024, 4096, 128)
    SCR = dict(hsp=hsp, projA=projA, vtokA=vtokA, zB=zB, projB=projB, gtokB=gtokB,
               GhnA0=hnA.dst[0], GoA0=oA.dst[0], GoB0=oB.dst[0], oB0=oB.src[0], oA0=oA.src[0])

    dumps = {}

    def maybe_stop(k, name):
        if stop != name:
            return False
        if dump:
            src = SCR[dump]
            o = nc.dram_tensor("dbg", list(src.shape), src.dtype, kind="ExternalOutput").ap()
            b = k.buf("dbg")
            k.dma("sync", [(o, src.ap())], reads=[b], final=True)
        k.finish()
        return True

    with ExitStack() as es:
        k = K(nc, es)
        with ExitStack() as pes:
            rp = RowProg(nc, pes, TOKC, 2, k=k, pfx="p1")
            rp.load_gains(gA)
            rp.load_h(xT)
            rp.ffn(0, wg[0], wu[0], wd[0])
            rp.hn_to_dram(1, hnA.own)
            rp.store_h(hsp.ap(), final=False)
            rp.emit(finish=False)
            k.barrier(hnA.gathers() if stop != "p1nocc" else None)
        if maybe_stop(k, "p1") or maybe_stop(k, "p1nocc"):
            return nc
        with ExitStack() as pes:
            rp = RowProg(nc, pes, TOKC, 2, k=k, pfx="p2")
            for r in range(2):
                rp.hn_from_dram(lambda r0, r1, r=r: hnA.gat(r, r0, r1))
                rp.proj_fm(w_att_in, 0, 832, projA.ap(), 0, r * TOKC)
                rp.proj_tm(w_att_in, 832, 256, vtokA.ap(), r * TOKC, 0)
                rp.proj_tm(w_att_in, 1088, 64, vtokA.ap(), r * TOKC, 256)
            rp.emit(finish=False)
            k.barrier()
        if maybe_stop(k, "p2"):
            return nc
        with ExitStack() as pes:
            pa, va = projA.ap(), vtokA.ap()
            D = dict(sqT=pa[0:256, :], skT=pa[256:512, :], bqT=pa[512:768, :], bkT=pa[768:832, :],
                     sv=[va[:, hl * 64:(hl + 1) * 64].rearrange("(b p) d -> p b d", p=P) for hl in range(4)],
                     bv=va[:, 256:320].rearrange("(b p) d -> p b d", p=P),
                     cst=a_cst, small=a_small, oT=oA.own)
            ap_ = AttnProg(nc, pes, D, k=k, pfx="p3")
            ap_.final = False
            ap_.emit(finish=False)
            k.barrier(oA.gathers())
        if maybe_stop(k, "p3"):
            return nc
        with ExitStack() as pes:
            rp = RowProg(nc, pes, TOKC, 4, k=k, pfx="p4")
            rp.load_gains(gB)
            rp.load_sel(sel)
            rp.load_h(hsp.ap())
            rp.mix_in_sel(lambda rc: oA.gat(rc // 4, (rc % 4) * P, (rc % 4 + 1) * P), 8, w_att_out)
            rp.ffn(0, wg[1], wu[1], wd[1])
            rp.ple(1, wpg[0], wpp[0], pT[0])
            rp.ffn(2, wg[2], wu[2], wd[2])
            rp.hn_to_dram(3, hnB.own)
            rp.proj_fm(w_gdn_z, 0, 2048, zB.ap(), 0, 0)
            rp.store_h(hsp.ap(), final=False)
            rp.emit(finish=False)
            k.barrier(hnB.gathers())
        if maybe_stop(k, "p4"):
            return nc
        with ExitStack() as pes:
            rp = RowProg(nc, pes, TOKC, 2, k=k, pfx="p5")
            for r in range(2):
                rp.hn_from_dram(lambda r0, r1, r=r: hnB.gat(r, r0, r1))
                rp.proj_fm(w_gdn_in, 0, 2048, projB.ap(), 0, r * TOKC)
                rp.proj_tm(w_gdn_in, 2048, 16, gtokB.ap(), r * TOKC, 0)
            rp.emit(finish=False)
            k.barrier()
        if maybe_stop(k, "p5"):
            return nc
        with ExitStack() as pes:
            D = dict(xT=projB.ap(), gtok=gtokB.ap(), gconst=g_gconst, convw=g_convw, small=g_small, c32=g_c32, oT=oB.own)
            gp = GdnProg(nc, pes, D, k=k, pfx="p6", fused=True)
            gp.emit(NCH, finish=False)
            k.barrier(oB.gathers())
        if maybe_stop(k, "p6"):
            return nc
        with ExitStack() as pes:
            rp = RowProg(nc, pes, TOKC, 2, k=k, pfx="p7")
            rp.load_gains(gC)
            rp.load_sel(sel)
            rp.load_h(hsp.ap())
            rp.gdn_gate_mix_in_sel(lambda rc: oB.gat(rc // 8, (rc % 8) * P, (rc % 8 + 1) * P), zB.ap(), w_gdn_out)
            rp.ffn(0, wg[3], wu[3], wd[3])
            rp.ple(1, wpg[1], wpp[1], pT[1])
            rp.store_h(outT, final=True)
            rp.emit(finish=True)
    return nc


def kernel(x, p, ffn_norm, ffn_w_gate, ffn_w_up, ffn_w_down, mix_norm,
           att_w_in, att_q_norm, att_k_norm, att_sinks, att_w_out,
           gdn_w_in, gdn_conv_w, gdn_a_log, gdn_dt_bias, gdn_out_norm, gdn_w_out,
           ple_norm, ple_w_gate, ple_w_proj):
    f = lambda a: np.ascontiguousarray(np.asarray(a, dtype=np.float32))
    x = f(x).reshape(-1, 1024)
    p = f(p).reshape(2, -1, 256)
    ffn_norm, mix_norm, ple_norm = f(ffn_norm), f(mix_norm), f(ple_norm)
    wg, wu, wd = f(ffn_w_gate), f(ffn_w_up), f(ffn_w_down)
    att_w_in, att_w_out = f(att_w_in)[0], f(att_w_out)[0]
    gdn_w_in, gdn_w_out = f(gdn_w_in)[0], f(gdn_w_out)[0]
    conv_w, a_log, dt_bias = f(gdn_conv_w)[0], f(gdn_a_log)[0], f(gdn_dt_bias)[0]
    qg, kg, sinks = f(att_q_norm)[0], f(att_k_norm)[0], f(att_sinks)[0]
    tok = lambda c: slice(c * TOKC, (c + 1) * TOKC)
    ar = np.arange

    shared = {}
    for i, (l, j) in enumerate([(0, 0), (0, 1), (1, 0), (1, 1)]):
        shared["wg%d" % i] = wg[l, j]
        shared["wu%d" % i] = wu[l, j]
        shared["wd%d" % i] = wd[l, j]
    for i in range(2):
        shared["wpg%d" % i] = f(ple_w_gate)[i]
        shared["wpp%d" % i] = f(ple_w_proj)[i]
    shared["gA"] = _gain_layout([ffn_norm[0, 0], mix_norm[0]])
    shared["gB"] = _gain_layout([ffn_norm[0, 1], ple_norm[0], ffn_norm[1, 0], mix_norm[1]])
    shared["gC"] = _gain_layout([ffn_norm[1, 1], ple_norm[1]], extra=f(gdn_out_norm)[0])
    rows = np.concatenate([ar(0, 256), 512 + ar(0, 256), ar(256, 512), 512 + ar(256, 512)])
    shared["w_att_out"] = np.ascontiguousarray(att_w_out[rows])
    shared["w_gdn_z"] = np.ascontiguousarray(gdn_w_in[:, 4096:6144])
    shared["w_gdn_out"] = gdn_w_out
    shared["g_c32"] = gdn_consts()
    sm = np.zeros((128, 8), np.float32)
    sm[:, 0] = 1.0
    sm[:, 1] = 1e-6
    sm[:, 2] = LN_QSCALE
    shared["g_small"] = sm

    maps = []
    for c in range(NCORES):
        b, half = c // 2, c % 2
        m = dict(shared)
        m["xT"] = np.ascontiguousarray(x[tok(c)].T)
        m["pT0"] = np.ascontiguousarray(p[0][tok(c)].T)
        m["pT1"] = np.ascontiguousarray(p[1][tok(c)].T)
        s = np.zeros((128, 2), np.float32)
        s[:, half] = 1.0
        m["sel"] = s
        h4 = half * 256
        cols = np.concatenate([ar(h4, h4 + 256), 512 + ar(h4, h4 + 256), 1536 + ar(h4, h4 + 256),
                               2048 + ar(half * 64, half * 64 + 64), 1024 + ar(h4, h4 + 256), 2176 + ar(half * 64, half * 64 + 64)])
        m["w_att_in"] = np.ascontiguousarray(att_w_in[:, cols])
        cols = np.concatenate([ar(half * 512, half * 512 + 512), 1024 + ar(half * 512, half * 512 + 512),
                               2048 + ar(half * 1024, half * 1024 + 1024),
                               6160 + ar(half * 8, half * 8 + 8), 6144 + ar(half * 8, half * 8 + 8)])
        m["w_gdn_in"] = np.ascontiguousarray(gdn_w_in[:, cols])
        m["a_cst"] = attn_consts(half)
        sma = np.zeros((128, 520), np.float32)
        sma[0:64, 0] = qg
        sma[0:64, 1] = kg
        sma[:, 2] = 1.0
        sma[:, 3] = -SHIFT
        sma[:, 4] = 1e-6
        sma[:, 5] = 64e-6
        for hl in range(4):
            sma[0:64, 8 + hl * 128:8 + (hl + 1) * 128] = sinks[4 * half + hl]
        m["a_small"] = sma
        chs = np.concatenate([ar((4 * half + mm) * 128, (4 * half + mm + 1) * 128) for mm in range(4)] +
                             [1024 + ar((4 * half + mm) * 128, (4 * half + mm + 1) * 128) for mm in range(4)] +
                             [2048 + ar((8 * half + hv) * 128, (8 * half + hv + 1) * 128) for hv in range(8)])
        cw = conv_w[:, chs]
        m["g_convw"] = np.ascontiguousarray(cw.reshape(4, 16, 128).transpose(2, 1, 0).reshape(128, 64))
        gc = np.zeros((128, 2, 256), np.float32)
        gc[:, 0, :] = np.tile(dt_bias[8 * half:8 * half + 8], 32)[None, :]
        gc[:, 1, :] = np.tile(a_log[8 * half:8 * half + 8], 32)[None, :]
        m["g_gconst"] = gc
        maps.append(m)

    nc = build_fused(stop=_DBG.get("stop"), dump=_DBG.get("dump"))
    res = run_bass_kernel_spmd(nc, maps, core_ids=list(range(NCORES))).results
    if _DBG.get("stop"):
        _DBG["res"] = res
        return None
    out = np.concatenate([res[c]["outT"].T for c in range(NCORES)], 0)
    return np.ascontiguousarray(out.reshape(4, 4096, 1024).astype(np.float32))
```

```python
import numpy as np
import concourse.bass as bass
import concourse.mybir as mybir
from concourse.alu_op_type import AluOpType as ALU
from contextlib import ExitStack

F32 = mybir.dt.float32
BF16 = mybir.dt.bfloat16
AF = mybir.ActivationFunctionType
AX = mybir.AxisListType
P = 128


class Buf:
    __slots__ = ("name", "lastw", "reads", "dsem", "dcount", "excl")

    def __init__(self, name):
        self.name = name
        self.lastw = None
        self.reads = {}
        self.dsem = None
        self.dcount = 0
        self.excl = False


class K:
    def __init__(self, nc, es, safe_same=True):
        self.nc = nc
        self.es = es
        self.E = {}
        for n in ("tensor", "vector", "scalar", "gpsimd", "sync"):
            sem = es.enter_context(nc.semaphore("e_" + n))
            self.E[n] = dict(eng=getattr(nc, n), sem=sem, count=0, waited={}, name=n)
        self.safe_same = safe_same
        self.dma_sems = {}
        self.free_dsems = []
        self.nsem_alloc = 0
        self.bar_sem = es.enter_context(nc.semaphore("barrier"))
        self.bar_count = 0
        self.cc_sem = es.enter_context(nc.semaphore("ccsem"))
        self.cc_count = 0
        self.nbuf = 0
        self.final_events = []
        self.ninstr = 0

    def buf(self, name=None):
        self.nbuf += 1
        return Buf(name or ("b%d" % self.nbuf))

    def _wait(self, e, sem, val):
        key = id(sem)
        if e["waited"].get(key, 0) >= val:
            return
        e["eng"].wait_ge(sem, val)
        e["waited"][key] = val
        if getattr(self, "trace", None) is not None:
            self.trace.append((e["name"], "wait", [n for n, x in self.E.items() if x["sem"] is sem] or "dma", val))

    def _emit_waits(self, en, reads, writes):
        e = self.E[en]
        evs = []
        for b in reads:
            if b.lastw is not None:
                evs.append(b.lastw)
            if b.excl:
                evs.extend(ev for ev in b.reads.values() if ev[2] != en)
        for b in writes:
            if b.lastw is not None:
                evs.append(b.lastw)
            evs.extend(b.reads.values())
        for (sem, val, src) in evs:
            if src == en and (en == "tensor" or not self.safe_same):
                continue
            self._wait(e, sem, val)

    def _record(self, ev, reads, writes):
        for b in writes:
            b.lastw = ev
            b.reads = {}
        for b in reads:
            if b in writes:
                continue
            key = id(ev[0])
            old = b.reads.get(key)
            if old is None or old[1] < ev[1]:
                b.reads[key] = ev

    def op(self, en, fn, reads=(), writes=()):
        e = self.E[en]
        self._emit_waits(en, reads, writes)
        ins = fn(e["eng"])
        e["count"] += 1
        ins.then_inc(e["sem"], 1)
        ev = (e["sem"], e["count"], en)
        self._record(ev, reads, writes)
        self.ninstr += 1
        if getattr(self, "trace", None) is not None:
            self.trace.append((en, "op", e["count"], [b.name for b in reads], [b.name for b in writes]))
        return ev

    def dma(self, qn, pairs, reads=(), writes=(), final=False):
        e = self.E[qn]
        self._emit_waits(qn, reads, writes)
        owner = writes[0] if len(writes) else reads[0]
        if owner.dsem is None:
            if self.free_dsems:
                owner.dsem, owner.dcount = self.free_dsems.pop()
            else:
                self.nsem_alloc += 1
                owner.dsem = self.es.enter_context(self.nc.semaphore("d%d_%s" % (self.nsem_alloc, owner.name)))
        for (o, i) in pairs:
            ins = e["eng"].dma_start(out=o, in_=i)
            owner.dcount += 16
            ins.then_inc(owner.dsem, 16)
            self.ninstr += 1
        ev = (owner.dsem, owner.dcount, "dma")
        self.dma_sems[id(owner.dsem)] = [owner.dsem, owner.dcount]
        self._record(ev, reads, writes)
        if final:
            self.final_events.append(ev)
        return ev

    def barrier(self, collective_fn=None):
        g = self.E["gpsimd"]
        for n, e in self.E.items():
            if n != "gpsimd" and e["count"] > 0:
                self._wait(g, e["sem"], e["count"])
        if g["count"] > 0 and self.safe_same:
            self._wait(g, g["sem"], g["count"])
        for sem, val in self.dma_sems.values():
            self._wait(g, sem, val)
        fns = collective_fn if isinstance(collective_fn, (list, tuple)) else ([collective_fn] if collective_fn else [])
        ins = g["eng"].nop()
        self.bar_count += 1
        ins.then_inc(self.bar_sem, 1)
        for n, e in self.E.items():
            self._wait(e, self.bar_sem, self.bar_count)
            for n2, e2 in self.E.items():
                e["waited"][id(e2["sem"])] = max(e["waited"].get(id(e2["sem"]), 0), e2["count"])
            for sem, val in self.dma_sems.values():
                e["waited"][id(sem)] = max(e["waited"].get(id(sem), 0), val)
        events = []
        for fn in fns:
            ins = fn(g["eng"])
            self.cc_count += 1
            ins.then_inc(self.cc_sem, 1)
            events.append((self.cc_sem, self.cc_count, "cc"))
        self.free_dsems.extend((sem, val) for sem, val in self.dma_sems.values())
        self.dma_sems = {}
        return events

    def finish(self):
        e = self.E["sync"]
        for (sem, val, src) in self.final_events:
            self._wait(e, sem, val)
EPS = 1e-6
TT = 512


class RR:
    def __init__(self, items):
        self.items = items
        self.i = 0

    def next(self):
        it = self.items[self.i % len(self.items)]
        self.i += 1
        return it


class RowProg:
    def __init__(self, nc, es, TOK, n_gain, NST=3, NWB=4, safe_same=True, k=None, pfx=""):
        self.nc = nc
        self.es = es
        k = self.k = k if k is not None else K(nc, es, safe_same=safe_same)
        self.TOK = TOK
        self.pfx = pfx
        self.NTT = TOK // TT
        NTT = self.NTT

        def alloc(name, shape, dt):
            return es.enter_context(nc.sbuf_tensor(pfx + "sb_" + name, shape, dt))

        self.hT = alloc("hT", [P, 8, TOK], F32)
        self.hTb = [[k.buf("hT%d_%d" % (c, t)) for t in range(NTT)] for c in range(8)]
        self.hn = alloc("hn", [P, 8, TOK], BF16)
        self.hnb = [[k.buf("hn%d_%d" % (c, t)) for t in range(NTT)] for c in range(8)]
        self.act = alloc("act", [P, 11, TOK], BF16)
        self.actb = [[k.buf("ac%d_%d" % (c, t)) for t in range(NTT)] for c in range(11)]
        wst = alloc("wst", [P, NST, 2048], F32)
        self.wst = RR([(k.buf("wst%d" % i), wst[:, i, :]) for i in range(NST)])
        wbf = alloc("wbf", [P, NWB, 2048], BF16)
        self.wbf = RR([(k.buf("wbf%d" % i), wbf[:, i, :]) for i in range(NWB)])
        sq = alloc("sq", [P, 2, TT], BF16)
        self.sq = RR([(k.buf("sq%d" % i), sq[:, i, :]) for i in range(2)])
        t32 = alloc("t32", [P, 4, TT], F32)
        self.t32 = RR([(k.buf("t32_%d" % i), t32[:, i, :]) for i in range(4)])
        self.ones = alloc("ones", [P, P], BF16)
        self.onesb = k.buf("ones")
        self.gains = alloc("gains", [P, n_gain * 8 + 1], F32)
        self.n_gain = n_gain
        self.gainsb = k.buf("gains")
        ps = [es.enter_context(nc.psum_tensor(pfx + "ps%d" % i, [P, TT], F32)) for i in range(7)]
        self.ps = RR([(k.buf("ps%d" % i), ps[i][:, :]) for i in range(7)])
        self.items = []
        k.op("vector", lambda e: e.memset(self.ones[:], 1.0), writes=[self.onesb])
        self.epsc = alloc("epsc", [P, 1], F32)
        self.epsap = self.epsc[:, 0:1]
        k.op("vector", lambda e: e.memset(self.epsc[:], EPS), writes=[self.onesb])

    def ts(self, tt):
        return slice(tt * TT, (tt + 1) * TT)

    def item(self, specs, fn):
        self.items.append((specs, fn))

    def load_gains(self, g_dram):
        self.k.dma("sync", [(self.gains[:], g_dram)], writes=[self.gainsb])

    def obuf(self, rc):
        if rc < 8:
            return self.hnb[rc], self.hn[:, rc, :]
        return self.actb[rc - 8], self.act[:, rc - 8, :]

    def _issue_load(self, spec):
        k = self.k
        w, r0, R, c0, ncols = spec
        assert R * ncols <= 2048
        sb, sap = self.wst.next()
        wb, wap = self.wbf.next()
        src = w[r0 * P:(r0 + R) * P, c0:c0 + ncols].rearrange("(r p) n -> p r n", p=P)
        dst = sap[:, 0:R * ncols].rearrange("p (r n) -> p r n", r=R)
        k.dma("sync", [(dst, src)], writes=[sb])
        k.op("gpsimd", lambda e: e.tensor_copy(out=wap[:, 0:R * ncols], in_=sap[:, 0:R * ncols]),
             reads=[sb], writes=[wb])
        return wb, wap[:, 0:R * ncols].rearrange("p (r n) -> p r n", r=R)

    def emit(self, lookahead=1, finish=True):
        items = self.items
        loaded = {}
        nl = 0
        for i, (specs, fn) in enumerate(items):
            while nl < len(items) and nl <= i + lookahead:
                loaded[nl] = [self._issue_load(s) for s in items[nl][0]]
                nl += 1
            fn(loaded.pop(i))
        if finish:
            self.k.finish()

    def load_h(self, xT_dram):
        def fn(_):
            for c in range(8):
                self.k.dma("sync", [(self.hT[:, c, :], xT_dram[c * P:(c + 1) * P, :])], writes=self.hTb[c])
        self.item([], fn)

    def store_h(self, out_dram, final=True):
        def fn(_):
            for c in range(8):
                self.k.dma("sync", [(out_dram[c * P:(c + 1) * P, :], self.hT[:, c, :])], reads=self.hTb[c], final=final)
        self.item([], fn)

    def load_sel(self, sel_dram):
        self.sel = self.es.enter_context(self.nc.sbuf_tensor(self.pfx + "sb_sel", [P, 2], F32))
        self.selb = self.k.buf("sel")
        self.k.dma("sync", [(self.sel[:], sel_dram)], writes=[self.selb])

    def hn_to_dram(self, gi, dst):
        self.norm(gi)

        def fn(_):
            for c in range(8):
                self.k.dma("sync", [(dst(c * P, (c + 1) * P), self.hn[:, c, :])], reads=self.hnb[c])
        self.item([], fn)

    def hn_from_dram(self, src):
        def fn(_):
            for c in range(8):
                sap_, sbuf_ = src(c * P, (c + 1) * P)
                self.k.dma("sync", [(self.hn[:, c, :], sap_)], reads=[sbuf_], writes=self.hnb[c])
        self.item([], fn)

    def proj_fm(self, w, c0w, n_out, out_dram, row0, col0):
        k = self.k
        c0 = 0
        while c0 < n_out:
            ncols = min(256, n_out - c0)

            def fn(tiles, c0=c0, ncols=ncols):
                (wb, wap), = tiles
                j0 = 0
                while j0 < ncols:
                    m = min(P, ncols - j0)
                    for tt in range(self.NTT):
                        ts = self.ts(tt)
                        pb, pap = self.ps.next()
                        for c in range(8):
                            k.op("tensor", lambda e: e.matmul(pap[0:m, :], lhsT=wap[:, c, j0:j0 + m], rhs=self.hn[:, c, ts],
                                                              start=(c == 0), stop=(c == 7)),
                                 reads=[wb, self.hnb[c][tt]], writes=[pb])
                        eb, eap = self.t32.next()
                        if tt % 2 == 0:
                            k.op("scalar", lambda e: e.copy(out=eap[0:m, :], in_=pap[0:m, :]), reads=[pb], writes=[eb])
                        else:
                            k.op("vector", lambda e: e.tensor_copy(out=eap[0:m, :], in_=pap[0:m, :]), reads=[pb], writes=[eb])
                        r0 = row0 + c0 + j0
                        k.dma("sync", [(out_dram[r0:r0 + m, col0 + tt * TT:col0 + (tt + 1) * TT], eap[0:m, :])], reads=[eb])
                    j0 += m
            self.item([(w, 0, 8, c0w + c0, ncols)], fn)
            c0 += ncols

    def proj_tm(self, w, c0w, ncols, out_dram, tok0, ocol0):
        k = self.k

        def fn(tiles):
            (wb, wap), = tiles
            for tb in range(self.TOK // P):
                tt = (tb * P) // TT
                pb, pap = self.ps.next()
                for c in range(8):
                    k.op("tensor", lambda e: e.matmul(pap[:, 0:ncols], lhsT=self.hn[:, c, tb * P:(tb + 1) * P], rhs=wap[:, c, 0:ncols],
                                                      start=(c == 0), stop=(c == 7)),
                         reads=[wb, self.hnb[c][tt]], writes=[pb])
                eb, eap = self.t32.next()
                if tb % 2 == 0:
                    k.op("scalar", lambda e: e.copy(out=eap[:, 0:ncols], in_=pap[:, 0:ncols]), reads=[pb], writes=[eb])
                else:
                    k.op("vector", lambda e: e.tensor_copy(out=eap[:, 0:ncols], in_=pap[:, 0:ncols]), reads=[pb], writes=[eb])
                k.dma("sync", [(out_dram[tok0 + tb * P:tok0 + (tb + 1) * P, ocol0:ocol0 + ncols], eap[:, 0:ncols])], reads=[eb])
        self.item([(w, 0, 8, c0w, ncols)], fn)

    def _load_sel_chunk(self, G, rc):
        k = self.k
        T = self.TOK
        ab, aap = self.wst.next()
        gap_, gbuf_ = G(rc)
        k.dma("sync", [(aap[:, 0:T], gap_[:, 0:T])], reads=[gbuf_], writes=[ab])
        bb, bap = self.wst.next()
        k.dma("sync", [(bap[:, 0:T], gap_[:, T:2 * T])], reads=[gbuf_], writes=[bb])
        k.op("gpsimd", lambda e: e.tensor_scalar(out=aap[:, 0:T], in0=aap[:, 0:T], scalar1=self.sel[:, 0:1], scalar2=None, op0=ALU.mult),
             reads=[ab, self.selb], writes=[ab])
        k.op("gpsimd", lambda e: e.tensor_scalar(out=bap[:, 0:T], in0=bap[:, 0:T], scalar1=self.sel[:, 1:2], scalar2=None, op0=ALU.mult),
             reads=[bb, self.selb], writes=[bb])
        return ab, aap, bb, bap

    def mix_in_sel(self, G, nrc, w_out):
        k = self.k

        def fn(_):
            for rc in range(nrc):
                bufs, dap = self.obuf(rc)
                ab, aap, bb, bap = self._load_sel_chunk(G, rc)
                k.op("gpsimd", lambda e: e.tensor_tensor(out=dap, in0=aap[:, 0:self.TOK], in1=bap[:, 0:self.TOK], op=ALU.add),
                     reads=[ab, bb], writes=bufs)
        self.item([], fn)
        self._mix_matmuls(nrc, w_out)

    def _mix_matmuls(self, nrc, w_out):
        k = self.k
        for dc in range(8):
            def fn(tiles, dc=dc):
                (wb, wap), = tiles
                for tt in range(self.NTT):
                    ts = self.ts(tt)
                    pb, pap = self.ps.next()
                    for rc in range(nrc):
                        bufs, oap = self.obuf(rc)
                        k.op("tensor", lambda e: e.matmul(pap, lhsT=wap[:, rc, :], rhs=oap[:, ts],
                                                          start=(rc == 0), stop=(rc == nrc - 1)),
                             reads=[wb, bufs[tt]], writes=[pb])
                    k.op("vector", lambda e: e.tensor_tensor(out=self.hT[:, dc, ts], in0=pap, in1=self.hT[:, dc, ts], op=ALU.add),
                         reads=[pb, self.hTb[dc][tt]], writes=[self.hTb[dc][tt]])
            self.item([(w_out, 0, nrc, dc * P, P)], fn)

    def gdn_gate_mix_in_sel(self, G, zT_dram, w_out):
        k = self.k
        gcol = self.gains[:, self.n_gain * 8:self.n_gain * 8 + 1]

        def fn(_):
            for rc in range(16):
                bufs, dap = self.obuf(rc)
                ob, oap, bb, bap = self._load_sel_chunk(G, rc)
                k.op("gpsimd", lambda e: e.tensor_tensor(out=oap[:, 0:self.TOK], in0=oap[:, 0:self.TOK], in1=bap[:, 0:self.TOK], op=ALU.add),
                     reads=[ob, bb], writes=[ob])
                zb, zap = self.wst.next()
                k.dma("sync", [(zap[:, 0:self.TOK], zT_dram[rc * P:(rc + 1) * P, :])], writes=[zb])
                for tt in range(self.NTT):
                    ts = self.ts(tt)
                    qb, qap = self.sq.next()
                    k.op("scalar", lambda e: e.activation(out=qap, in_=oap[:, ts], func=AF.Square), reads=[ob], writes=[qb])
                    pb, pap = self.ps.next()
                    k.op("tensor", lambda e: e.matmul(pap, lhsT=self.ones[:], rhs=qap, start=True, stop=True),
                         reads=[qb, self.onesb], writes=[pb])
                    tb, tap = self.t32.next()
                    k.op("scalar", lambda e: e.activation(out=tap, in_=pap, func=AF.Ln, scale=1.0 / 128.0, bias=self.epsap),
                         reads=[pb, self.onesb], writes=[tb])
                    k.op("scalar", lambda e: e.activation(out=tap, in_=tap, func=AF.Exp, scale=-0.5), reads=[tb], writes=[tb])
                    k.op("vector", lambda e: e.scalar_tensor_tensor(out=oap[:, ts], in0=oap[:, ts], scalar=gcol, in1=tap,
                                                                    op0=ALU.mult, op1=ALU.mult),
                         reads=[ob, tb, self.gainsb], writes=[ob])
                for tt in range(self.NTT):
                    ts = self.ts(tt)
                    k.op("scalar", lambda e: e.activation(out=zap[:, ts], in_=zap[:, ts], func=AF.Silu), reads=[zb], writes=[zb])
                    k.op("gpsimd", lambda e: e.tensor_tensor(out=dap[:, ts], in0=oap[:, ts], in1=zap[:, ts], op=ALU.mult),
                         reads=[ob, zb], writes=[bufs[tt]])
        self.item([], fn)
        self._mix_matmuls(16, w_out)

    def norm(self, gi):
        def fn(_):
            k = self.k
            for tt in range(self.NTT):
                ts = self.ts(tt)
                pb, pap = self.ps.next()
                for c in range(8):
                    qb, qap = self.sq.next()
                    k.op("scalar", lambda e: e.activation(out=qap, in_=self.hT[:, c, ts], func=AF.Square),
                         reads=[self.hTb[c][tt]], writes=[qb])
                    k.op("tensor", lambda e: e.matmul(pap, lhsT=self.ones[:], rhs=qap, start=(c == 0), stop=(c == 7)),
                         reads=[qb, self.onesb], writes=[pb])
                tb, tap = self.t32.next()
                k.op("scalar", lambda e: e.activation(out=tap, in_=pap, func=AF.Ln, scale=1.0 / 1024.0, bias=self.epsap),
                     reads=[pb, self.onesb], writes=[tb])
                rb, rap = self.t32.next()
                k.op("scalar", lambda e: e.activation(out=rap, in_=tap, func=AF.Exp, scale=-0.5), reads=[tb], writes=[rb])
                for c in range(8):
                    k.op("vector", lambda e: e.scalar_tensor_tensor(
                        out=self.hn[:, c, ts], in0=self.hT[:, c, ts], scalar=self.gains[:, gi * 8 + c:gi * 8 + c + 1],
                        in1=rap, op0=ALU.mult, op1=ALU.mult),
                        reads=[self.hTb[c][tt], rb, self.gainsb], writes=[self.hnb[c][tt]])
        self.item([], fn)

    def ffn(self, gi, wg, wu, wd):
        self.norm(gi)
        k = self.k
        for half in range(2):
            f0 = half * 11
            groups = [(0, 2), (2, 2), (4, 2), (6, 2), (8, 2), (10, 1)]
            for (fl0, nfc) in groups:
                def fn(tiles, fl0=fl0, nfc=nfc):
                    (gb, gap), (ub, uap) = tiles
                    for j in range(nfc):
                        for tt in range(self.NTT):
                            ts = self.ts(tt)
                            pgb, pg = self.ps.next()
                            pub, pu = self.ps.next()
                            for c in range(8):
                                k.op("tensor", lambda e: e.matmul(pg, lhsT=gap[:, c, j * P:(j + 1) * P], rhs=self.hn[:, c, ts],
                                                                  start=(c == 0), stop=(c == 7)),
                                     reads=[gb, self.hnb[c][tt]], writes=[pgb])
                            for c in range(8):
                                k.op("tensor", lambda e: e.matmul(pu, lhsT=uap[:, c, j * P:(j + 1) * P], rhs=self.hn[:, c, ts],
                                                                  start=(c == 0), stop=(c == 7)),
                                     reads=[ub, self.hnb[c][tt]], writes=[pub])
                            sb, sap = self.t32.next()
                            k.op("scalar", lambda e: e.activation(out=sap, in_=pg, func=AF.Silu), reads=[pgb], writes=[sb])
                            k.op("vector", lambda e: e.tensor_tensor(out=self.act[:, fl0 + j, ts], in0=sap, in1=pu, op=ALU.mult),
                                 reads=[sb, pub], writes=[self.actb[fl0 + j][tt]])
                c0 = (f0 + fl0) * P
                self.item([(wg, 0, 8, c0, nfc * P), (wu, 0, 8, c0, nfc * P)], fn)
            for dc in range(8):
                def fn(tiles, dc=dc):
                    (wb, wap), = tiles
                    for tt in range(self.NTT):
                        ts = self.ts(tt)
                        pb, pap = self.ps.next()
                        for f in range(11):
                            k.op("tensor", lambda e: e.matmul(pap, lhsT=wap[:, f, :], rhs=self.act[:, f, ts],
                                                              start=(f == 0), stop=(f == 10)),
                                 reads=[wb, self.actb[f][tt]], writes=[pb])
                        k.op("vector", lambda e: e.scalar_tensor_tensor(
                            out=self.hT[:, dc, ts], in0=pap, scalar=0.5, in1=self.hT[:, dc, ts],
                            op0=ALU.mult, op1=ALU.add),
                            reads=[pb, self.hTb[dc][tt]], writes=[self.hTb[dc][tt]])
                self.item([(wd, f0, 11, dc * P, P)], fn)

    def proj_out(self, gi, w, n_out, out_dram):
        self.norm(gi)
        k = self.k
        c0 = 0
        while c0 < n_out:
            ncols = min(256, n_out - c0)

            def fn(tiles, c0=c0, ncols=ncols):
                (wb, wap), = tiles
                j0 = 0
                while j0 < ncols:
                    m = min(P, ncols - j0)
                    for tt in range(self.NTT):
                        ts = self.ts(tt)
                        pb, pap = self.ps.next()
                        for c in range(8):
                            k.op("tensor", lambda e: e.matmul(pap[0:m, :], lhsT=wap[:, c, j0:j0 + m], rhs=self.hn[:, c, ts],
                                                              start=(c == 0), stop=(c == 7)),
                                 reads=[wb, self.hnb[c][tt]], writes=[pb])
                        eb, eap = self.t32.next()
                        if tt % 2 == 0:
                            k.op("scalar", lambda e: e.copy(out=eap[0:m, :], in_=pap[0:m, :]), reads=[pb], writes=[eb])
                        else:
                            k.op("vector", lambda e: e.tensor_copy(out=eap[0:m, :], in_=pap[0:m, :]), reads=[pb], writes=[eb])
                        k.dma("sync", [(out_dram[c0 + j0:c0 + j0 + m, ts], eap[0:m, :])], reads=[eb], final=True)
                    j0 += m
            self.item([(w, 0, 8, c0, ncols)], fn)
            c0 += ncols

    def load_T_bf16(self, src_dram, nrc):
        def fn(_):
            k = self.k
            for rc in range(nrc):
                bufs, dap = self.obuf(rc)
                sb, sap = self.wst.next()
                k.dma("sync", [(sap[:, 0:self.TOK], src_dram[rc * P:(rc + 1) * P, :])], writes=[sb])
                k.op("gpsimd", lambda e: e.tensor_copy(out=dap, in_=sap[:, 0:self.TOK]), reads=[sb], writes=bufs)
        self.item([], fn)

    def mix_in(self, oT_dram, nrc, w_out):
        self.load_T_bf16(oT_dram, nrc)
        k = self.k
        for dc in range(8):
            def fn(tiles, dc=dc):
                (wb, wap), = tiles
                for tt in range(self.NTT):
                    ts = self.ts(tt)
                    pb, pap = self.ps.next()
                    for rc in range(nrc):
                        bufs, oap = self.obuf(rc)
                        k.op("tensor", lambda e: e.matmul(pap, lhsT=wap[:, rc, :], rhs=oap[:, ts],
                                                          start=(rc == 0), stop=(rc == nrc - 1)),
                             reads=[wb, bufs[tt]], writes=[pb])
                    k.op("vector", lambda e: e.tensor_tensor(out=self.hT[:, dc, ts], in0=pap, in1=self.hT[:, dc, ts], op=ALU.add),
                         reads=[pb, self.hTb[dc][tt]], writes=[self.hTb[dc][tt]])
            self.item([(w_out, 0, nrc, dc * P, P)], fn)

    def ple(self, gi, wpg, wpp, pT_dram):
        self.norm(gi)
        k = self.k

        def fnp(_):
            for rc in range(2):
                sb, sap = self.wst.next()
                k.dma("sync", [(sap[:, 0:self.TOK], pT_dram[rc * P:(rc + 1) * P, :])], writes=[sb])
                k.op("gpsimd", lambda e: e.tensor_copy(out=self.act[:, rc, :], in_=sap[:, 0:self.TOK]),
                     reads=[sb], writes=self.actb[rc])
        self.item([], fnp)
        for dc in range(8):
            def fn(tiles, dc=dc):
                (gb, gap), (pb_, pap_) = tiles
                for tt in range(self.NTT):
                    ts = self.ts(tt)
                    pgb, pg = self.ps.next()
                    ppb, pp = self.ps.next()
                    for c in range(8):
                        k.op("tensor", lambda e: e.matmul(pg, lhsT=gap[:, c, :], rhs=self.hn[:, c, ts],
                                                          start=(c == 0), stop=(c == 7)),
                             reads=[gb, self.hnb[c][tt]], writes=[pgb])
                    for c in range(2):
                        k.op("tensor", lambda e: e.matmul(pp, lhsT=pap_[:, c, :], rhs=self.act[:, c, ts],
                                                          start=(c == 0), stop=(c == 1)),
                             reads=[pb_, self.actb[c][tt]], writes=[ppb])
                    sb, sap = self.t32.next()
                    k.op("scalar", lambda e: e.activation(out=sap, in_=pg, func=AF.Sigmoid), reads=[pgb], writes=[sb])
                    mb, map_ = self.t32.next()
                    k.op("vector", lambda e: e.tensor_tensor(out=map_, in0=sap, in1=pp, op=ALU.mult),
                         reads=[sb, ppb], writes=[mb])
                    k.op("gpsimd", lambda e: e.tensor_tensor(out=self.hT[:, dc, ts], in0=map_, in1=self.hT[:, dc, ts], op=ALU.add),
                         reads=[mb, self.hTb[dc][tt]], writes=[self.hTb[dc][tt]])
            self.item([(wpg, 0, 8, dc * P, P), (wpp, 0, 2, dc * P, P)], fn)

    def gdn_gate_mix_in(self, oT_dram, zT_dram, w_out):
        k = self.k
        gcol = self.gains[:, self.n_gain * 8:self.n_gain * 8 + 1]

        def fn(_):
            for rc in range(16):
                bufs, dap = self.obuf(rc)
                ob, oap = self.wst.next()
                k.dma("sync", [(oap[:, 0:self.TOK], oT_dram[rc * P:(rc + 1) * P, :])], writes=[ob])
                zb, zap = self.wst.next()
                k.dma("sync", [(zap[:, 0:self.TOK], zT_dram[rc * P:(rc + 1) * P, :])], writes=[zb])
                rstd = []
                for tt in range(self.NTT):
                    ts = self.ts(tt)
                    qb, qap = self.sq.next()
                    k.op("scalar", lambda e: e.activation(out=qap, in_=oap[:, ts], func=AF.Square), reads=[ob], writes=[qb])
                    pb, pap = self.ps.next()
                    k.op("tensor", lambda e: e.matmul(pap, lhsT=self.ones[:], rhs=qap, start=True, stop=True),
                         reads=[qb, self.onesb], writes=[pb])
                    tb, tap = self.t32.next()
                    k.op("scalar", lambda e: e.activation(out=tap, in_=pap, func=AF.Ln, scale=1.0 / 128.0, bias=self.epsap),
                         reads=[pb, self.onesb], writes=[tb])
                    k.op("scalar", lambda e: e.activation(out=tap, in_=tap, func=AF.Exp, scale=-0.5), reads=[tb], writes=[tb])
                    k.op("vector", lambda e: e.scalar_tensor_tensor(out=oap[:, ts], in0=oap[:, ts], scalar=gcol, in1=tap,
                                                                    op0=ALU.mult, op1=ALU.mult),
                         reads=[ob, tb, self.gainsb], writes=[ob])
                for tt in range(self.NTT):
                    ts = self.ts(tt)
                    k.op("scalar", lambda e: e.activation(out=zap[:, ts], in_=zap[:, ts], func=AF.Silu), reads=[zb], writes=[zb])
                    k.op("gpsimd", lambda e: e.tensor_tensor(out=dap[:, ts], in0=oap[:, ts], in1=zap[:, ts], op=ALU.mult),
                         reads=[ob, zb], writes=[bufs[tt]])
        self.item([], fn)
        for dc in range(8):
            def fn2(tiles, dc=dc):
                (wb, wap), = tiles
                for tt in range(self.NTT):
                    ts = self.ts(tt)
                    pb, pap = self.ps.next()
                    for rc in range(16):
                        bufs, oap = self.obuf(rc)
                        k.op("tensor", lambda e: e.matmul(pap, lhsT=wap[:, rc, :], rhs=oap[:, ts],
                                                          start=(rc == 0), stop=(rc == 15)),
                             reads=[wb, bufs[tt]], writes=[pb])
                    k.op("vector", lambda e: e.tensor_tensor(out=self.hT[:, dc, ts], in0=pap, in1=self.hT[:, dc, ts], op=ALU.add),
                         reads=[pb, self.hTb[dc][tt]], writes=[self.hTb[dc][tt]])
            self.item([(w_out, 0, 16, dc * P, P)], fn2)
SEQ = 4096
NBLK = 32
BIG = 30000.0
SHIFT = 8.0


class AttnProg:
    final = True

    def __init__(self, nc, es, D, safe_same=True, k=None, pfx=""):
        self.nc = nc
        self.es = es
        k = self.k = k if k is not None else K(nc, es, safe_same=safe_same)
        self.D = D

        def alloc(name, shape, dt):
            return es.enter_context(nc.sbuf_tensor(pfx + "sa_" + name, shape, dt))

        def rr(name, shape, dt, n):
            t = alloc(name, [shape[0], n] + list(shape[1:]), dt)
            return RR([(k.buf("%s%d" % (name, i)), t[:, i]) for i in range(n)])

        self.alloc = alloc
        self.cst = alloc("cst", [P, 128 * 4 + 4 * 512 + 2 * 512], BF16)
        self.cb = k.buf("cst")
        self.small = alloc("small", [P, 8 + 512], F32)
        self.smallb = k.buf("small")
        self.stage = rr("stage", [P, SEQ], F32, 2)
        self.qT = rr("qT", [64, SEQ], BF16, 4)
        self.kT = rr("kT", [64, SEQ], BF16, 4)
        self.v = rr("v", [P, NBLK * 64], BF16, 5)
        self.e32 = [rr("e32_%d" % s, [P, 512], F32, 1) for s in range(4)]
        self.sp = [rr("sp_%d" % s, [P, 512], BF16, 2) for s in range(4)]
        self.w = [rr("w_%d" % s, [P, 512], BF16, 1) for s in range(4)]
        self.ls = [rr("ls_%d" % s, [P, 512], BF16, 2) for s in range(4)]
        self.ost = rr("ost", [64, 512], F32, 2)
        ps = [es.enter_context(nc.psum_tensor(pfx + "psa%d" % i, [P, 512], F32)) for i in range(8)]
        self.psb = [(k.buf("psa%d" % i), ps[i][:, :]) for i in range(8)]
        for b, _ in self.psb:
            b.excl = True

    def consts(self):
        k, D = self.k, self.D
        sb, sap = self.stage.next()
        n = 128 * 4 + 4 * 512 + 2 * 512
        k.dma("sync", [(sap[:, 0:n], D["cst"])], writes=[sb])
        k.op("vector", lambda e: e.tensor_copy(out=self.cst[:], in_=sap[:, 0:n]), reads=[sb], writes=[self.cb])
        k.dma("sync", [(self.small[:], D["small"])], writes=[self.smallb])
        c = self.cst
        self.tri = c[:, 0:128]
        self.ident = c[:, 128:256]
        self.nident = c[:, 256:384]
        self.ones = c[:, 384:512]
        self.masks = [c[:, 512 + j * 512:512 + (j + 1) * 512] for j in range(4)]
        self.swab = [c[:, 2560 + j * 512:2560 + (j + 1) * 512] for j in range(2)]
        s = self.small
        self.gq = s[0:64, 0:1]
        self.gk = s[0:64, 1:2]
        self.one = s[:, 2:3]
        self.nshift = s[:, 3:4]
        self.eps = s[:, 4:5]
        self.ln8 = s[:, 6:7]
        self.zero = s[:, 7:8]
        self.sinks = s[0:64, 8:520]
        k.op("scalar", lambda e: e.activation(out=self.sinks, in_=self.sinks, func=AF.Exp, bias=self.nshift[0:64, :]),
             reads=[self.smallb], writes=[self.smallb])

    def load_sb_head(self, h):
        k, D = self.k, self.D
        qb, qap = self.qT.next()
        kb_, kap = self.kT.next()
        vb, vap = self.v.next()
        sb, sap = self.stage.next()
        k.dma("sync", [(sap[0:64, :], D["sqT"][h * 64:(h + 1) * 64, :])], writes=[sb])
        k.op("scalar", lambda e: e.activation(out=qap, in_=sap[0:64, :], func=AF.Copy, scale=0.125), reads=[sb], writes=[qb])
        sb, sap = self.stage.next()
        k.dma("sync", [(sap[0:64, :], D["skT"][h * 64:(h + 1) * 64, :])], writes=[sb])
        k.op("vector", lambda e: e.tensor_copy(out=kap, in_=sap[0:64, :]), reads=[sb], writes=[kb_])
        sb, sap = self.stage.next()
        k.dma("sync", [(sap[:, 0:NBLK * 64].rearrange("p (b d) -> p b d", d=64), D["sv"][h])], writes=[sb])
        k.op("gpsimd", lambda e: e.tensor_copy(out=vap, in_=sap[:, 0:NBLK * 64]), reads=[sb], writes=[vb])
        return (qb, qap, kb_, kap, vb, vap)

    def sb_stream(self, s, h, tiles):
        k, D = self.k, self.D
        (qb, qap, kb_, kap, vb, vap) = tiles
        cb = self.cb
        pzb, pz = self.psb[2 * s]
        pob, po = self.psb[2 * s + 1]
        for qs in range(8):
            q_sl = slice(qs * 512, (qs + 1) * 512)
            lsum = None
            kbs = list(range(4 * qs + 3, -1, -1))
            for idx, kb in enumerate(kbs):
                k_sl = slice(kb * 128, (kb + 1) * 128)
                j = kb - 4 * qs
                diag = j >= 0
                k.op("tensor", lambda e: e.matmul(pz, lhsT=kap[:, k_sl], rhs=qap[:, q_sl], start=True, stop=not diag),
                     reads=[kb_, qb], writes=[pzb])
                if diag:
                    k.op("tensor", lambda e: e.matmul(pz, lhsT=self.nident, rhs=self.masks[j], start=False, stop=True),
                         reads=[cb], writes=[pzb])
                yield
                eb, eap = self.e32[s].next()
                k.op("scalar", lambda e: e.activation(out=eap, in_=pz, func=AF.Exp), reads=[pzb], writes=[eb])
                yield
                spb, spap = self.sp[s].next()
                k.op("scalar", lambda e: e.activation(out=spap, in_=eap, func=AF.Ln, bias=self.one),
                     reads=[eb, self.smallb], writes=[spb])
                yield
                k.op("tensor", lambda e: e.matmul(pz, lhsT=self.tri, rhs=spap, start=True, stop=(lsum is None)),
                     reads=[cb, spb], writes=[pzb])
                if lsum is not None:
                    k.op("tensor", lambda e: e.matmul(pz, lhsT=self.ones, rhs=lsum[1], start=False, stop=True),
                         reads=[cb, lsum[0]], writes=[pzb])
                yield
                k.op("scalar", lambda e: e.activation(out=pz, in_=pz, func=AF.Exp, scale=-1.0), reads=[pzb], writes=[pzb])
                yield
                wb, wap = self.w[s].next()
                k.op("vector", lambda e: e.tensor_tensor(out=wap, in0=pz, in1=eap, op=ALU.mult), reads=[pzb, eb], writes=[wb])
                if idx < len(kbs) - 1:
                    if lsum is None:
                        lsum = (spb, spap)
                    else:
                        lb, lap = self.ls[s].next()
                        k.op("vector", lambda e: e.tensor_tensor(out=lap, in0=lsum[1], in1=spap, op=ALU.add),
                             reads=[lsum[0], spb], writes=[lb])
                        lsum = (lb, lap)
                yield
                k.op("tensor", lambda e: e.matmul(po[0:64, :], lhsT=vap[:, kb * 64:(kb + 1) * 64], rhs=wap,
                                                  start=(idx == 0), stop=(idx == len(kbs) - 1)),
                     reads=[vb, wb], writes=[pob])
                yield
            ob, oap = self.ost.next()
            k.op("vector", lambda e: e.tensor_copy(out=oap, in_=po[0:64, :]), reads=[pob], writes=[ob])
            if callable(D["oT"]):
                dst = D["oT"](h * 64, (h + 1) * 64)[:, q_sl]
            else:
                dst = D["oT"][h * 64:(h + 1) * 64, q_sl]
            k.dma("sync", [(dst, oap)], reads=[ob], final=self.final)
            yield

    def swa(self):
        k, D = self.k, self.D
        cb = self.cb
        alloc = self.alloc
        qslots = [self.qT.next() for _ in range(4)]
        knb, kn = self.kT.next()
        sq = RR([(k.buf("ssq%d" % i), alloc("ssq%d" % i, [64, 512], BF16)[:, :]) for i in range(2)])
        t32 = RR([(k.buf("st32_%d" % i), alloc("st32_%d" % i, [64, 512], F32)[:, :]) for i in range(3)])
        vb, vap = self.v.next()
        sb, sap = self.stage.next()
        k.dma("sync", [(sap[:, 0:NBLK * 64].rearrange("p (b d) -> p b d", d=64), D["bv"])], writes=[sb])
        k.op("gpsimd", lambda e: e.tensor_copy(out=vap, in_=sap[:, 0:NBLK * 64]), reads=[sb], writes=[vb])
        pnorm = RR([self.psb[6], self.psb[7]])
        pz_rr = RR([self.psb[0], self.psb[1]])
        po_rr = RR([self.psb[2], self.psb[3]])
        pd_rr = RR([self.psb[4], self.psb[5]])

        def qknorm(src_dram, dst_ap, dst_buf, gain, lnbias):
            sb, sap = self.stage.next()
            k.dma("sync", [(sap[0:64, :], src_dram)], writes=[sb])
            for tt in range(8):
                ts = slice(tt * 512, (tt + 1) * 512)
                qb_, qap_ = sq.next()
                k.op("scalar", lambda e: e.activation(out=qap_, in_=sap[0:64, ts], func=AF.Square), reads=[sb], writes=[qb_])
                pb, pap = pnorm.next()
                k.op("tensor", lambda e: e.matmul(pap[0:64, :], lhsT=self.ones[0:64, 0:64], rhs=qap_, start=True, stop=True),
                     reads=[qb_, cb], writes=[pb])
                tb, tap = t32.next()
                k.op("scalar", lambda e: e.activation(out=tap, in_=pap[0:64, :], func=AF.Ln, scale=1.0 / 64.0, bias=self.eps[0:64, :]),
                     reads=[pb, self.smallb], writes=[tb])
                k.op("scalar", lambda e: e.activation(out=tap, in_=tap, func=AF.Exp, scale=-0.5, bias=lnbias[0:64, :]),
                     reads=[tb, self.smallb], writes=[tb])
                k.op("vector", lambda e: e.scalar_tensor_tensor(out=dst_ap[:, ts], in0=sap[0:64, ts], scalar=gain, in1=tap,
                                                                op0=ALU.mult, op1=ALU.mult),
                     reads=[sb, tb, self.smallb], writes=[dst_buf])

        qknorm(D["bkT"], kn, knb, self.gk, self.zero)
        for hl in range(4):
            qknorm(D["bqT"][hl * 64:(hl + 1) * 64, :], qslots[hl][1], qslots[hl][0], self.gq, self.ln8)
        qnb = [qslots[hl][0] for hl in range(4)]

        pw = RR([(k.buf("pw%d" % i), alloc("pw%d" % i, [P, 512], BF16)[:, :]) for i in range(3)])
        ost = RR([(k.buf("so%d" % i), alloc("so%d" % i, [64, 4, 512], F32)) for i in range(2)])
        den = RR([(k.buf("dn%d" % i), alloc("dn%d" % i, [64, 512], F32)[:, :]) for i in range(2)])
        for qg in range(8):
            osb, osap = ost.next()
            for qi in range(4):
                qb = qg * 4 + qi
                q_sl = slice(qb * 128, (qb + 1) * 128)
                pob, po = po_rr.next()
                pdb, pd = pd_rr.next()
                kbl = [qb] if qb == 0 else [qb - 1, qb]
                for ii, kb in enumerate(kbl):
                    k_sl = slice(kb * 128, (kb + 1) * 128)
                    which = 1 if kb == qb else 0
                    pzb, pz = pz_rr.next()
                    for hl in range(4):
                        k.op("tensor", lambda e: e.matmul(pz[:, hl * 128:(hl + 1) * 128], lhsT=kn[:, k_sl], rhs=qslots[hl][1][:, q_sl],
                                                          start=(hl == 0), stop=False, skip_group_check=True),
                             reads=[knb, qnb[hl]], writes=[pzb])
                    k.op("tensor", lambda e: e.matmul(pz, lhsT=self.ident, rhs=self.swab[which], start=False, stop=True,
                                                      skip_group_check=True),
                         reads=[cb], writes=[pzb])
                    wb, wap = pw.next()
                    k.op("scalar", lambda e: e.activation(out=wap, in_=pz, func=AF.Exp, bias=self.nshift),
                         reads=[pzb, self.smallb], writes=[wb])
                    k.op("tensor", lambda e: e.matmul(po[0:64, :], lhsT=vap[:, kb * 64:(kb + 1) * 64], rhs=wap,
                                                      start=(ii == 0), stop=(ii == len(kbl) - 1)),
                         reads=[vb, wb], writes=[pob])
                    k.op("tensor", lambda e: e.matmul(pd[0:64, :], lhsT=self.ones[:, 0:64], rhs=wap,
                                                      start=(ii == 0), stop=(ii == len(kbl) - 1)),
                         reads=[cb, wb], writes=[pdb])
                db, dap = den.next()
                k.op("vector", lambda e: e.tensor_tensor(out=dap, in0=pd[0:64, :], in1=self.sinks, op=ALU.add),
                     reads=[pdb, self.smallb], writes=[db])
                k.op("vector", lambda e: e.reciprocal(out=dap, in_=dap), reads=[db], writes=[db])
                k.op("vector", lambda e: e.tensor_tensor(out=osap[:, :, qi * 128:(qi + 1) * 128],
                                                         in0=po[0:64, :].rearrange("p (h q) -> p h q", h=4),
                                                         in1=dap.rearrange("p (h q) -> p h q", h=4), op=ALU.mult),
                     reads=[pob, db], writes=[osb])
            if callable(D["oT"]):
                pairs = [(D["oT"]((4 + hl) * 64, (5 + hl) * 64)[:, qg * 512:(qg + 1) * 512], osap[:, hl, :]) for hl in range(4)]
            else:
                pairs = [(D["oT"][(4 + hl) * 64:(5 + hl) * 64, qg * 512:(qg + 1) * 512], osap[:, hl, :]) for hl in range(4)]
            k.dma("sync", pairs, reads=[osb], final=self.final)

    def emit(self, finish=True):
        self.consts()
        tiles = [self.load_sb_head(h) for h in range(4)]
        streams = [self.sb_stream(s, s, tiles[s]) for s in range(4)]
        while streams:
            for g in list(streams):
                try:
                    next(g)
                except StopIteration:
                    streams.remove(g)
        self.swa()
        if finish:
            self.k.finish()


def attn_consts(half):
    n = 128 * 4 + 4 * 512 + 2 * 512
    c = np.zeros((128, n), np.float32)
    kk = np.arange(128)[:, None]
    qq = np.arange(128)[None, :]
    c[:, 0:128] = (kk >= qq)
    c[:, 128:256] = np.eye(128)
    c[:, 256:384] = -np.eye(128)
    c[:, 384:512] = 1.0
    ql = np.arange(512)[None, :]
    for j in range(4):
        c[:, 512 + j * 512:512 + (j + 1) * 512] = np.where(j * 128 + kk >= ql, BIG, 0.0)
    for hl in range(4):
        slope = 2.0 ** (-(4 * half + hl + 1))
        dist_prev = qq + 128 - kk
        dist_cur = qq - kk
        c[:, 2560 + hl * 128:2560 + (hl + 1) * 128] = np.where(kk > qq, -slope * dist_prev, -BIG)
        c[:, 3072 + hl * 128:3072 + (hl + 1) * 128] = np.where(kk <= qq, -slope * dist_cur, -BIG)
    return c
GC = 128
NCH = 32
GBIG = 30000.0
LN_QSCALE = -0.5 * 4.852030263919617


def run_streams(streams):
    streams = list(streams)
    while streams:
        for s in list(streams):
            try:
                next(s)
            except StopIteration:
                streams.remove(s)


class GdnProg:
    def __init__(self, nc, es, D, safe_same=True, inv_fp32=True, k=None, pfx="", fused=False):
        self.nc = nc
        self.es = es
        self.fused = fused
        k = self.k = k if k is not None else K(nc, es, safe_same=safe_same)
        self.D = D
        self.inv_fp32 = inv_fp32

        def alloc(name, shape, dt):
            return es.enter_context(nc.sbuf_tensor(pfx + "sg_" + name, shape, dt))

        def rr(name, shape, dt, n):
            t = alloc(name, [shape[0], n] + list(shape[1:]), dt)
            return RR([(k.buf("%s%d" % (name, i)), t[:, i]) for i in range(n)])

        self.alloc = alloc
        self.rr = rr
        self.c32 = alloc("c32", [P, 6 * 128], F32)
        self.cb = k.buf("c32")
        self.identb = alloc("identb", [P, P], BF16)
        self.onesb = alloc("onesb", [P, P], BF16)
        self.small = alloc("small", [P, 8], F32)
        self.convw = alloc("convw", [P, 64], F32)
        self.diagw = alloc("diagw", [P, 64, P], BF16)
        self.gt = alloc("gates", [P, 6, 256], F32)
        self.gb = k.buf("gates")
        self.xst = rr("xst", [P, 515], F32, 2)
        self.xb = rr("xb", [P, 515], BF16, 2)
        qT = alloc("qT", [P, 2, 4, 512], BF16)
        kT = alloc("kT", [P, 2, 4, 512], BF16)
        vt = alloc("vt", [P, 2, 4, 8, P], BF16)
        self.qT, self.kT, self.vt = qT, kT, vt
        self.qTb = [[k.buf("qT%d_%d" % (s, m)) for m in range(4)] for s in range(2)]
        self.kTb = [[k.buf("kT%d_%d" % (s, m)) for m in range(4)] for s in range(2)]
        self.vtb = [[k.buf("vt%d_%d" % (s, h)) for h in range(8)] for s in range(2)]
        self.e32 = rr("e32", [P, 512], F32, 3)
        self.y32 = rr("y32", [P, 512], F32, 2)
        self.yb = rr("yb", [P, 512], BF16, 2)
        self.sq = rr("sq", [P, 512], BF16, 2)
        self.r32 = rr("r32", [P, 512], F32, 2)
        self.egc = rr("egc", [P, 24], F32, 3)
        self.Xt = alloc("Xt", [P, 2, 8, P], BF16)
        self.AT = alloc("AT", [P, 2, 8, P], BF16)
        self.kh = alloc("kh", [P, 2, 8, P], BF16)
        self.Xtb = [[k.buf("Xt%d_%d" % (s, h)) for h in range(8)] for s in range(2)]
        self.ATb = [[k.buf("AT%d_%d" % (s, h)) for h in range(8)] for s in range(2)]
        self.khb = [[k.buf("kh%d_%d" % (s, h)) for h in range(8)] for s in range(2)]
        self.f32t = rr("f32t", [P, P], F32, 48)
        self.S = alloc("S", [P, 8, P], F32)
        self.Sbf = alloc("Sbf", [P, 8, P], BF16)
        self.Sb = [k.buf("S%d" % h) for h in range(8)]
        self.Sbfb = [k.buf("Sbf%d" % h) for h in range(8)]
        self.Rb = rr("R", [P, P], BF16, 8)
        self.vn = rr("vn", [P, P], BF16, 8)
        self.tmp = rr("tmp", [P, P], F32, 8)
        self.ost = alloc("ost", [P, 2, 8, P], F32)
        self.ostT = alloc("ostT", [P, 2, 8, P], F32)
        self.ostTb = [[k.buf("ostT%d_%d" % (s, h)) for h in range(8)] for s in range(2)]
        self.ostb = [[k.buf("ost%d_%d" % (s, h)) for h in range(8)] for s in range(2)]
        pc = [es.enter_context(nc.psum_tensor(pfx + "pgc%d" % i, [P, 512], F32)) for i in range(1)]
        self.pc = RR([(k.buf("pgc%d" % i), pc[i][:, :]) for i in range(1)])
        pb = es.enter_context(nc.psum_tensor(pfx + "pgb", [P, 1024], BF16))
        pbb = k.buf("pgb")
        pbb.excl = True
        self.pvt = (pbb, pb[:, 0:512])
        self.pkt = RR([(pbb, pb[:, 512 + i * 128:512 + (i + 1) * 128]) for i in range(3)])
        self.fence_ap = pb[0:1, 896:1024]
        self.pbb = pbb
        pq = [es.enter_context(nc.psum_tensor(pfx + "pgq%d" % i, [P, 512], F32)) for i in range(6)]
        bq = [k.buf("pgq%d" % i) for i in range(6)]
        for b in bq + [self.pc.items[0][0]]:
            b.excl = True
        self.pqb = RR([(bq[i], pq[i]) for i in range(3)])
        self.psqb = RR([(bq[i], pq[i]) for i in range(3, 6)])

    def fence(self, banks):
        self.k.op("tensor", lambda e: e.transpose(self.fence_ap, self.identb[:, 0:1], self.identb[:]),
                  reads=[self.cb], writes=[self.pbb] + list(banks))

    def phase0(self):
        k, D = self.k, self.D
        k.dma("sync", [(self.c32[:], D["c32"])], writes=[self.cb])
        c = self.c32
        self.LE = c[:, 0:128]
        self.GT = c[:, 128:256]
        self.MASKB = c[:, 256:384]
        self.I32 = c[:, 384:512]
        self.ONES32 = c[:, 512:640]
        self.STRICT = c[:, 640:768]
        k.op("vector", lambda e: e.tensor_copy(out=self.identb[:], in_=self.I32), reads=[self.cb], writes=[self.cb])
        k.op("vector", lambda e: e.tensor_copy(out=self.onesb[:], in_=self.ONES32), reads=[self.cb], writes=[self.cb])
        k.dma("sync", [(self.small[:], D["small"]), (self.convw[:], D["convw"])], writes=[self.cb])
        self.one = self.small[:, 0:1]
        self.eps = self.small[:, 1:2]
        self.lnq = self.small[:, 2:3]
        self.zero = self.small[:, 3:4]
        for i in range(64):
            k.op("gpsimd", lambda e: e.tensor_scalar(out=self.diagw[:, i, :], in0=self.identb[:], scalar1=self.convw[:, i:i + 1],
                                                     scalar2=None, op0=ALU.mult), reads=[self.cb], writes=[self.cb])
        g = self.gt
        if self.fused:
            gt = D["gtok"]
            k.dma("sync", [(g[:, 0, :].rearrange("p (c h) -> p c h", h=8), gt[:, 0:8].rearrange("(c p) h -> p c h", p=P)),
                           (g[:, 1, :].rearrange("p (c h) -> p c h", h=8), gt[:, 8:16].rearrange("(c p) h -> p c h", p=P)),
                           (g[:, 2:4, :], D["gconst"])], writes=[self.gb])
        else:
            k.dma("sync", [(g[:, 0:4, :], D["gates"])], writes=[self.gb])
        A, BL, DTB, ALOG, G, BETA = (g[:, i, :] for i in range(6))
        gb = [self.gb]
        k.op("vector", lambda e: e.tensor_tensor(out=A, in0=A, in1=DTB, op=ALU.add), reads=gb, writes=gb)
        k.op("scalar", lambda e: e.activation(out=A, in_=A, func=AF.Exp), reads=gb, writes=gb)
        k.op("scalar", lambda e: e.activation(out=A, in_=A, func=AF.Ln, bias=self.one), reads=gb + [self.cb], writes=gb)
        k.op("scalar", lambda e: e.activation(out=ALOG, in_=ALOG, func=AF.Exp), reads=gb, writes=gb)
        k.op("vector", lambda e: e.scalar_tensor_tensor(out=G, in0=A, scalar=-1.0, in1=ALOG, op0=ALU.mult, op1=ALU.mult),
             reads=gb, writes=gb)
        k.op("scalar", lambda e: e.activation(out=BL, in_=BL, func=AF.Exp, scale=-1.0), reads=gb, writes=gb)
        k.op("vector", lambda e: e.tensor_scalar(out=BL, in0=BL, scalar1=1.0, scalar2=None, op0=ALU.add), reads=gb, writes=gb)
        k.op("vector", lambda e: e.reciprocal(out=BETA, in_=BL), reads=gb, writes=gb)
        self.G, self.BETA = G, BETA
        k.op("vector", lambda e: e.memset(self.S[:], 0.0), writes=self.Sb)
        k.op("vector", lambda e: e.memset(self.Sbf[:], 0.0), writes=self.Sbfb)

    def prologue_rows(self, t, rows):
        k, D = self.k, self.D
        slot = t % 2
        cb = self.cb
        for r in rows:
            sb, sap = self.xst.next()
            if not self.fused:
                k.dma("sync", [(sap, D["xT"][r * P:(r + 1) * P, t * 512:t * 512 + 515])], writes=[sb])
            elif t == 0:
                k.op("gpsimd", lambda e: e.memset(sap[:, 0:3], 0.0), writes=[sb])
                k.dma("sync", [(sap[:, 3:515], D["xT"][r * P:(r + 1) * P, 0:512])], writes=[sb])
            else:
                k.dma("sync", [(sap, D["xT"][r * P:(r + 1) * P, t * 512 - 3:t * 512 + 512])], writes=[sb])
            xbb, xbap = self.xb.next()
            k.op("gpsimd", lambda e: e.tensor_copy(out=xbap, in_=sap), reads=[sb], writes=[xbb])
            yield
            pcb, pcap = self.pc.next()
            for j in range(4):
                k.op("tensor", lambda e: e.matmul(pcap, lhsT=self.diagw[:, r * 4 + j, :], rhs=xbap[:, j:j + 512],
                                                  start=(j == 0), stop=(j == 3)), reads=[cb, xbb], writes=[pcb])
            yield
            eb, eap = self.e32.next()
            k.op("scalar", lambda e: e.activation(out=eap, in_=pcap, func=AF.Exp, scale=-1.0), reads=[pcb], writes=[eb])
            yield
            k.op("scalar", lambda e: e.activation(out=eap, in_=eap, func=AF.Ln, bias=self.one), reads=[eb, cb], writes=[eb])
            yield
            k.op("scalar", lambda e: e.activation(out=eap, in_=eap, func=AF.Exp, scale=-1.0), reads=[eb], writes=[eb])
            yield
            if r < 8:
                yb_, yap = self.y32.next()
            else:
                yb_, yap = self.yb.next()
            k.op("vector", lambda e: e.tensor_tensor(out=yap, in0=pcap, in1=eap, op=ALU.mult), reads=[pcb, eb], writes=[yb_])
            yield
            if r < 8:
                m = r % 4
                qb_, qap_ = self.sq.next()
                k.op("scalar", lambda e: e.activation(out=qap_, in_=yap, func=AF.Square), reads=[yb_], writes=[qb_])
                yield
                pnb, pnap = self.pc.next()
                k.op("tensor", lambda e: e.matmul(pnap, lhsT=self.onesb[:], rhs=qap_, start=True, stop=True),
                     reads=[qb_, cb], writes=[pnb])
                yield
                rb, rap = self.r32.next()
                k.op("scalar", lambda e: e.activation(out=rap, in_=pnap, func=AF.Ln, bias=self.eps), reads=[pnb, cb], writes=[rb])
                yield
                bias = self.lnq if r < 4 else self.zero
                k.op("scalar", lambda e: e.activation(out=rap, in_=rap, func=AF.Exp, scale=-0.5, bias=bias), reads=[rb, cb], writes=[rb])
                yield
                if r < 4:
                    dst, dbuf = self.qT[:, slot, m, :], self.qTb[slot][m]
                else:
                    dst, dbuf = self.kT[:, slot, m, :], self.kTb[slot][m]
                k.op("vector", lambda e: e.tensor_tensor(out=dst, in0=yap, in1=rap, op=ALU.mult), reads=[yb_, rb], writes=[dbuf])
                yield
            else:
                hv = r - 8
                pvb, pvap = self.pvt
                for cc in range(4):
                    k.op("tensor", lambda e: e.transpose(pvap[:, cc * P:(cc + 1) * P], yap[:, cc * P:(cc + 1) * P], self.identb[:]),
                         reads=[yb_, cb], writes=[pvb])
                yield
                k.op("vector", lambda e: e.tensor_copy(out=self.vt[:, slot, :, hv, :], in_=pvap.rearrange("p (c d) -> p c d", c=4)),
                     reads=[pvb], writes=[self.vtb[slot][hv]])
                yield

    def pre(self, c):
        k = self.k
        cb = self.cb
        t, c4 = c // 4, c % 4
        slot = t % 2
        cs = c % 2
        csl = slice(c4 * P, (c4 + 1) * P)
        egb, egap = self.egc.next()
        self.egcur = getattr(self, "egcur", {})
        self.egcur[c] = (egb, egap)
        pb, pap = self.pqb.next()
        k.op("tensor", lambda e: e.matmul(pap[:, 0:8], lhsT=self.LE, rhs=self.G[:, c * 8:(c + 1) * 8], start=True, stop=True),
             reads=[cb, self.gb], writes=[pb])
        k.op("tensor", lambda e: e.matmul(pap[:, 8:16], lhsT=self.ONES32, rhs=self.G[:, c * 8:(c + 1) * 8], start=True, stop=True),
             reads=[cb, self.gb], writes=[pb])
        self.fence([pb])
        k.op("scalar", lambda e: e.activation(out=egap[:, 0:16], in_=pap[:, 0:16], func=AF.Exp), reads=[pb], writes=[egb])
        k.op("vector", lambda e: e.tensor_scalar(out=egap[:, 16:24], in0=egap[:, 0:8], scalar1=-1.0, scalar2=None, op0=ALU.mult),
             reads=[egb], writes=[egb])
        yield
        for grp in range(2):
            heads = list(range(4 * grp, 4 * grp + 4))
            kheads = [2 * grp, 2 * grp + 1]
            Gm = {}
            for h in heads:
                Gm[h] = self.f32t.next()
                k.op("gpsimd", lambda e: e.tensor_scalar(out=Gm[h][1], in0=self.GT, scalar1=self.G[:, c * 8 + h:c * 8 + h + 1],
                                                         scalar2=None, op0=ALU.mult), reads=[cb, self.gb], writes=[Gm[h][0]])
            yield
            pZ = {}
            bk = self.pqb.next()
            for h in heads:
                pZ[h] = (bk[0], bk[1][:, (h % 4) * P:(h % 4 + 1) * P])
                k.op("tensor", lambda e: e.matmul(pZ[h][1], lhsT=Gm[h][1], rhs=self.LE, start=True, stop=False),
                     reads=[Gm[h][0], cb], writes=[pZ[h][0]])
                k.op("tensor", lambda e: e.matmul(pZ[h][1], lhsT=self.I32, rhs=self.MASKB, start=False, stop=True),
                     reads=[cb], writes=[pZ[h][0]])
            self.fence([bk[0]])
            yield
            Dm = {}
            for h in heads:
                Dm[h] = self.f32t.next()
                k.op("scalar", lambda e: e.activation(out=Dm[h][1], in_=pZ[h][1], func=AF.Exp), reads=[pZ[h][0]], writes=[Dm[h][0]])
            yield
            pKQ, pKK, pkt, Dms = {}, {}, {}, {}
            bk = self.pqb.next()
            for m in kheads:
                kTc = self.kT[:, slot, m, csl]
                qTc = self.qT[:, slot, m, csl]
                pKQ[m] = (bk[0], bk[1][:, (m % 2) * P:(m % 2 + 1) * P])
                k.op("tensor", lambda e: e.matmul(pKQ[m][1], lhsT=kTc, rhs=qTc, start=True, stop=True),
                     reads=[self.kTb[slot][m], self.qTb[slot][m]], writes=[pKQ[m][0]])
                pKK[m] = (bk[0], bk[1][:, (2 + m % 2) * P:(3 + m % 2) * P])
                k.op("tensor", lambda e: e.matmul(pKK[m][1], lhsT=kTc, rhs=kTc, start=True, stop=True),
                     reads=[self.kTb[slot][m]], writes=[pKK[m][0]])
                pkt[m] = self.pkt.next()
                k.op("tensor", lambda e: e.transpose(pkt[m][1], kTc, self.identb[:]), reads=[self.kTb[slot][m], cb], writes=[pkt[m][0]])
            for h in heads:
                Dms[h] = self.f32t.next()
                k.op("gpsimd", lambda e: e.tensor_tensor(out=Dms[h][1], in0=Dm[h][1], in1=self.STRICT, op=ALU.mult),
                     reads=[Dm[h][0], cb], writes=[Dms[h][0]])
            yield
            Pt, Q_, W = {}, {}, {}
            for h in heads:
                m = h // 2
                k.op("vector", lambda e: e.tensor_tensor(out=self.AT[:, cs, h, :], in0=pKQ[m][1], in1=Dm[h][1], op=ALU.mult),
                     reads=[pKQ[m][0], Dm[h][0]], writes=[self.ATb[cs][h]])
                k.op("vector", lambda e: e.tensor_scalar(out=self.kh[:, cs, h, :], in0=pkt[m][1], scalar1=Dm[h][1][:, 127:128],
                                                         scalar2=None, op0=ALU.mult),
                     reads=[pkt[m][0], Dm[h][0]], writes=[self.khb[cs][h]])
                Pt[h] = self.f32t.next()
                k.op("vector", lambda e: e.scalar_tensor_tensor(out=Pt[h][1], in0=pKK[m][1], scalar=self.BETA[:, c * 8 + h:c * 8 + h + 1],
                                                                in1=Dms[h][1], op0=ALU.mult, op1=ALU.mult),
                     reads=[pKK[m][0], self.gb, Dms[h][0]], writes=[Pt[h][0]])
            yield
            pT = {}
            bk = self.pqb.next()
            for h in heads:
                pT[h] = (bk[0], bk[1][:, (h % 4) * P:(h % 4 + 1) * P])
                k.op("tensor", lambda e: e.transpose(pT[h][1], Pt[h][1], self.I32), reads=[Pt[h][0], cb], writes=[pT[h][0]])
            self.fence([bk[0]])
            yield
            for h in heads:
                Q_[h] = self.f32t.next()
                k.op("scalar", lambda e: e.copy(out=Q_[h][1], in_=pT[h][1]), reads=[pT[h][0]], writes=[Q_[h][0]])
                W[h] = self.f32t.next()
                k.op("gpsimd", lambda e: e.tensor_tensor(out=W[h][1], in0=self.I32, in1=Pt[h][1], op=ALU.subtract),
                     reads=[cb, Pt[h][0]], writes=[W[h][0]])
            yield
            pP, pQ = {}, {}
            bkP = self.pqb.next()
            bkQ = self.pqb.next()
            for h in heads:
                pQ[h] = (bkQ[0], bkQ[1][:, (h % 4) * P:(h % 4 + 1) * P])
                k.op("tensor", lambda e: e.matmul(pQ[h][1], lhsT=Pt[h][1], rhs=Q_[h][1], start=True, stop=True),
                     reads=[Q_[h][0], Pt[h][0]], writes=[pQ[h][0]])
            for h in heads:
                pP[h] = (bkP[0], bkP[1][:, (h % 4) * P:(h % 4 + 1) * P])
                k.op("tensor", lambda e: e.matmul(pP[h][1], lhsT=Q_[h][1], rhs=Pt[h][1], start=True, stop=True),
                     reads=[Q_[h][0], Pt[h][0]], writes=[pP[h][0]])
            self.fence([bkP[0], bkQ[0]])
            yield
            nP, nQ = {}, {}
            for h in heads:
                nP[h] = self.f32t.next()
                k.op("scalar", lambda e: e.copy(out=nP[h][1], in_=pP[h][1]), reads=[pP[h][0]], writes=[nP[h][0]])
                nQ[h] = self.f32t.next()
                k.op("vector", lambda e: e.tensor_copy(out=nQ[h][1], in_=pQ[h][1]), reads=[pQ[h][0]], writes=[nQ[h][0]])
            Pt, Q_ = nP, nQ
            yield
            for step in range(1, 7):
                pW, pP, pQ = {}, {}, {}
                bkW = self.pqb.next()
                for h in heads:
                    pW[h] = (bkW[0], bkW[1][:, (h % 4) * P:(h % 4 + 1) * P])
                    k.op("tensor", lambda e: e.matmul(pW[h][1], lhsT=Q_[h][1], rhs=W[h][1], start=True, stop=True),
                         reads=[Q_[h][0], W[h][0]], writes=[pW[h][0]])
                if step <= 5:
                    bkQ = self.pqb.next()
                    for h in heads:
                        pQ[h] = (bkQ[0], bkQ[1][:, (h % 4) * P:(h % 4 + 1) * P])
                        k.op("tensor", lambda e: e.matmul(pQ[h][1], lhsT=Pt[h][1], rhs=Q_[h][1], start=True, stop=True),
                             reads=[Q_[h][0], Pt[h][0]], writes=[pQ[h][0]])
                if step <= 4:
                    bkP = self.pqb.next()
                    for h in heads:
                        pP[h] = (bkP[0], bkP[1][:, (h % 4) * P:(h % 4 + 1) * P])
                        k.op("tensor", lambda e: e.matmul(pP[h][1], lhsT=Q_[h][1], rhs=Pt[h][1], start=True, stop=True),
                             reads=[Q_[h][0], Pt[h][0]], writes=[pP[h][0]])
                self.fence([bkW[0]] + ([bkQ[0]] if step <= 5 else []) + ([bkP[0]] if step <= 4 else []))
                yield
                nW, nP, nQ = {}, {}, {}
                for h in heads:
                    if step < 6:
                        nW[h] = self.f32t.next()
                        k.op("vector", lambda e: e.tensor_tensor(out=nW[h][1], in0=pW[h][1], in1=W[h][1], op=ALU.add),
                             reads=[pW[h][0], W[h][0]], writes=[nW[h][0]])
                    else:
                        k.op("vector", lambda e: e.tensor_tensor(out=self.Xt[:, cs, h, :], in0=pW[h][1], in1=W[h][1], op=ALU.add),
                             reads=[pW[h][0], W[h][0]], writes=[self.Xtb[cs][h]])
                    if step <= 5:
                        nQ[h] = self.f32t.next()
                        k.op("scalar", lambda e: e.copy(out=nQ[h][1], in_=pQ[h][1]), reads=[pQ[h][0]], writes=[nQ[h][0]])
                    if step <= 4:
                        nP[h] = self.f32t.next()
                        k.op("scalar", lambda e: e.copy(out=nP[h][1], in_=pP[h][1]), reads=[pP[h][0]], writes=[nP[h][0]])
                W, Pt, Q_ = nW, nP, nQ
                yield

    def seq(self, c):
        for grp in range(2):
            yield from self.seq_grp(c, grp)

    def seq_grp(self, c, grp):
        k, D = self.k, self.D
        t, c4 = c // 4, c % 4
        slot = t % 2
        cs = c % 2
        csl = slice(c4 * P, (c4 + 1) * P)
        heads = list(range(4 * grp, 4 * grp + 4))
        egb, egap = self.egcur[c]
        p1, pa = {}, {}
        bk1 = self.psqb.next()
        bka = self.psqb.next()
        for h in heads:
            m = h // 2
            p1[h] = (bk1[0], bk1[1][:, (h % 4) * P:(h % 4 + 1) * P])
            k.op("tensor", lambda e: e.matmul(p1[h][1], lhsT=self.kT[:, slot, m, csl], rhs=self.Sbf[:, h, :], start=True, stop=True),
                 reads=[self.kTb[slot][m], self.Sbfb[h]], writes=[p1[h][0]])
            pa[h] = (bka[0], bka[1][:, (h % 4) * P:(h % 4 + 1) * P])
            k.op("tensor", lambda e: e.matmul(pa[h][1], lhsT=self.qT[:, slot, m, csl], rhs=self.Sbf[:, h, :], start=True, stop=True),
                 reads=[self.qTb[slot][m], self.Sbfb[h]], writes=[pa[h][0]])
        yield
        R, tmp = {}, {}
        for h in heads:
            R[h] = self.Rb.next()
            k.op("vector", lambda e: e.scalar_tensor_tensor(out=R[h][1], in0=p1[h][1], scalar=egap[:, 16 + h:17 + h],
                                                            in1=self.vt[:, slot, c4, h, :], op0=ALU.mult, op1=ALU.add),
                 reads=[p1[h][0], egb, self.vtb[slot][h]], writes=[R[h][0]])
            tmp[h] = self.tmp.next()
            k.op("scalar", lambda e: e.activation(out=tmp[h][1], in_=pa[h][1], func=AF.Copy, scale=egap[:, h:h + 1]),
                 reads=[pa[h][0], egb], writes=[tmp[h][0]])
        yield
        p2 = {}
        bk2 = self.psqb.next()
        for h in heads:
            p2[h] = (bk2[0], bk2[1][:, (h % 4) * P:(h % 4 + 1) * P])
            k.op("tensor", lambda e: e.matmul(p2[h][1], lhsT=self.Xt[:, cs, h, :], rhs=R[h][1], start=True, stop=True),
                 reads=[self.Xtb[cs][h], R[h][0]], writes=[p2[h][0]])
        yield
        vn = {}
        for h in heads:
            vn[h] = self.vn.next()
            k.op("scalar", lambda e: e.activation(out=vn[h][1], in_=p2[h][1], func=AF.Copy, scale=self.BETA[:, c * 8 + h:c * 8 + h + 1]),
                 reads=[p2[h][0], self.gb], writes=[vn[h][0]])
        yield
        pb_, p3 = {}, {}
        bkb = self.psqb.next()
        bk3 = self.psqb.next()
        for h in heads:
            pb_[h] = (bkb[0], bkb[1][:, (h % 4) * P:(h % 4 + 1) * P])
            k.op("tensor", lambda e: e.matmul(pb_[h][1], lhsT=self.AT[:, cs, h, :], rhs=vn[h][1], start=True, stop=True),
                 reads=[self.ATb[cs][h], vn[h][0]], writes=[pb_[h][0]])
            p3[h] = (bk3[0], bk3[1][:, (h % 4) * P:(h % 4 + 1) * P])
            k.op("tensor", lambda e: e.matmul(p3[h][1], lhsT=self.kh[:, cs, h, :], rhs=vn[h][1], start=True, stop=True),
                 reads=[self.khb[cs][h], vn[h][0]], writes=[p3[h][0]])
        yield
        for h in heads:
            k.op("vector", lambda e: e.tensor_tensor(out=self.ost[:, cs, h, :], in0=pb_[h][1], in1=tmp[h][1], op=ALU.add),
                 reads=[pb_[h][0], tmp[h][0]], writes=[self.ostb[cs][h]])
            k.op("vector", lambda e: e.scalar_tensor_tensor(out=self.S[:, h, :], in0=self.S[:, h, :], scalar=egap[:, 8 + h:9 + h],
                                                            in1=p3[h][1], op0=ALU.mult, op1=ALU.add),
                 reads=[self.Sb[h], egb, p3[h][0]], writes=[self.Sb[h]])
        yield
        for h in heads:
            k.op("gpsimd", lambda e: e.tensor_copy(out=self.Sbf[:, h, :], in_=self.S[:, h, :]), reads=[self.Sb[h]], writes=[self.Sbfb[h]])
        if not self.fused:
            if grp == 1:
                k.dma("sync", [(D["o_tok"][c * P:(c + 1) * P, :], self.ost[:, cs, :, :].rearrange("p h d -> p (h d)"))],
                      reads=self.ostb[cs], final=True)
            yield
            return
        bkt = self.psqb.next()
        for h in heads:
            k.op("tensor", lambda e: e.transpose(bkt[1][:, (h % 4) * P:(h % 4 + 1) * P], self.ost[:, cs, h, :], self.I32),
                 reads=[self.ostb[cs][h], self.cb], writes=[bkt[0]])
        self.fence([bkt[0]])
        yield
        for h in heads:
            k.op("scalar", lambda e: e.copy(out=self.ostT[:, cs, h, :], in_=bkt[1][:, (h % 4) * P:(h % 4 + 1) * P]),
                 reads=[bkt[0]], writes=[self.ostTb[cs][h]])
        if grp == 1:
            k.dma("sync", [(D["oT"](h * P, (h + 1) * P)[:, c * P:(c + 1) * P], self.ostT[:, cs, h, :]) for h in range(8)],
                  reads=self.ostTb[cs])
        yield

    def emit(self, nchunks=NCH, stop=None, finish=True):
        self.phase0()
        if stop == "phase0":
            self.k.finish(); return
        run_streams([self.prologue_rows(0, range(16) if stop != "pro1" else [0, 8])])
        if stop in ("pro", "pro1"):
            self.k.finish(); return
        if stop is not None and stop.startswith("pre"):
            n = int(stop[3:] or 1000)
            g = self.pre(0)
            for _ in range(n):
                try:
                    next(g)
                except StopIteration:
                    break
            self.k.finish(); return
        run_streams([self.pre(0)])
        pro = []
        ntiles = (nchunks + 3) // 4
        for c in range(nchunks):
            t = c // 4
            if c % 4 == 0 and t + 1 < ntiles:
                pro = [self.prologue_rows(t + 1, range(16))]
            main = [self.seq(c)]
            if c + 1 < nchunks and (c % 4 != 3):
                main.append(self.pre(c + 1))
            streams = list(main)
            while streams:
                for s_ in list(streams):
                    try:
                        next(s_)
                    except StopIteration:
                        streams.remove(s_)
                for s_ in list(pro):
                    try:
                        next(s_)
                        next(s_)
                    except StopIteration:
                        pro.remove(s_)
            if c % 4 == 3 and c + 1 < nchunks:
                run_streams(pro)
                pro = []
                run_streams([self.pre(c + 1)])
        if finish:
            self.k.finish()


def gdn_consts():
    c = np.zeros((128, 768), np.float32)
    p = np.arange(128)[:, None]
    i = np.arange(128)[None, :]
    c[:, 0:128] = (p <= i)
    c[:, 128:256] = (p > i)
    c[:, 256:384] = np.where(i < p, -GBIG, 0.0)
    c[:, 384:512] = np.eye(128)
    c[:, 512:640] = 1.0
    c[:, 640:768] = (p < i)
    return c
from concourse.bass_utils import run_bass_kernel_spmd

NCORES = 8
_DBG = {}
TOKC = 2048
PAIRS = [[0, 1], [2, 3], [4, 5], [6, 7]]


def _dram(nc, name, shape, kind="ExternalInput", dt=None):
    return nc.dram_tensor(name, list(shape), dt or F32, kind=kind).ap()


def _scratch(nc, name, shape, dt=None):
    return nc.dram_tensor(name, list(shape), dt or F32)


class Chunked:
    def __init__(self, nc, name, rows, cols, rc, dt=None, gathered=True):
        self.rc, self.rows, self.cols = rc, rows, cols
        self.n = rows // rc
        self.src = [nc.dram_tensor("%s_%d" % (name, j), [rc, cols], dt or F32) for j in range(self.n)]
        self.dst = [nc.dram_tensor("G%s_%d" % (name, j), [2 * rc, cols], dt or F32) for j in range(self.n)] if gathered else []
        self.gb = [Buf("G%s_%d" % (name, j)) for j in range(self.n)]

    def set_events(self, events):
        for b, ev in zip(self.gb, events):
            b.lastw = ev
            b.reads = {}

    def gbuf(self, r0):
        return self.gb[r0 // self.rc]

    def own(self, r0, r1):
        j = r0 // self.rc
        assert (r1 - 1) // self.rc == j
        return self.src[j].ap()[r0 - j * self.rc:r1 - j * self.rc, :]

    def gat(self, rank, r0, r1):
        j = r0 // self.rc
        assert (r1 - 1) // self.rc == j
        return self.dst[j].ap()[rank * self.rc + r0 - j * self.rc:rank * self.rc + r1 - j * self.rc, :]

    def gathers(self):
        return [(lambda g, s=s, d=d: g.collective_compute("AllGather", ALU.bypass, replica_groups=PAIRS,
                                                         ins=[s.ap().opt()], outs=[d.ap().opt()]))
                for s, d in zip(self.src, self.dst)]


def _gain_layout(vecs, extra=None):
    g = np.stack(vecs).astype(np.float32).reshape(len(vecs), 8, 128).transpose(2, 0, 1).reshape(128, len(vecs) * 8)
    col = np.zeros((128, 1), np.float32) if extra is None else np.asarray(extra, np.float32).reshape(128, 1)
    return np.ascontiguousarray(np.concatenate([g, col], 1))


def build_fused(stop=None, dump=None):
    nc = bass.Bass("TRN2", target_bir_lowering=False)
    I = {}
    def inp(name, shape):
        I[name] = _dram(nc, name, shape)
        return I[name]
    xT = inp("xT", [1024, TOKC])
    sel = inp("sel", [128, 2])
    gA = inp("gA", [128, 17]); gB = inp("gB", [128, 33]); gC = inp("gC", [128, 17])
    wg = [inp("wg%d" % i, [1024, 2816]) for i in range(4)]
    wu = [inp("wu%d" % i, [1024, 2816]) for i in range(4)]
    wd = [inp("wd%d" % i, [2816, 1024]) for i in range(4)]
    w_att_in = inp("w_att_in", [1024, 1152])
    w_att_out = inp("w_att_out", [1024, 1024])
    w_gdn_in = inp("w_gdn_in", [1024, 2064])
    w_gdn_z = inp("w_gdn_z", [1024, 2048])
    w_gdn_out = inp("w_gdn_out", [2048, 1024])
    wpg = [inp("wpg%d" % i, [1024, 1024]) for i in range(2)]
    wpp = [inp("wpp%d" % i, [256, 1024]) for i in range(2)]
    pT = [inp("pT%d" % i, [256, TOKC]) for i in range(2)]
    a_cst = inp("a_cst", [128, 3584]); a_small = inp("a_small", [128, 520])
    g_c32 = inp("g_c32", [128, 768]); g_small = inp("g_small", [128, 8]); g_convw = inp("g_convw", [128, 64])
    g_gconst = inp("g_gconst", [128, 2, 256])
    outT = _dram(nc, "outT", [1024, TOKC], "ExternalOutput")
    hsp = _scratch(nc, "hsp", [1024, TOKC])
    hnA = Chunked(nc, "hnA", 1024, TOKC, 512, BF16)
    projA = _scratch(nc, "projA", [832, 4096]); vtokA = _scratch(nc, "vtokA", [4096, 320])
    oA = Chunked(nc, "oA", 512, 4096, 128)
    zB = _scratch(nc, "zB", [2048, TOKC])
    hnB = Chunked(nc, "hnB", 1024, TOKC, 512, BF16)
    projB = _scratch(nc, "projB", [2048, 4096]); gtokB = _scratch(nc, "gtokB", [4096, 16])
    oB = Chunked(nc, "oB", 1024, 4096, 128)
    SCR = dict(hsp=hsp, projA=projA, vtokA=vtokA, zB=zB, projB=projB, gtokB=gtokB,
               GhnA0=hnA.dst[0], GoA0=oA.dst[0], GoB0=oB.dst[0], oB0=oB.src[0], oA0=oA.src[0])

    dumps = {}

    def maybe_stop(k, name):
        if stop != name:
            return False
        if dump:
            src = SCR[dump]
            o = nc.dram_tensor("dbg", list(src.shape), src.dtype, kind="ExternalOutput").ap()
            b = k.buf("dbg")
            k.dma("sync", [(o, src.ap())], reads=[b], final=True)
        k.finish()
        return True

    with ExitStack() as es:
        k = K(nc, es)
        with ExitStack() as pes:
            rp = RowProg(nc, pes, TOKC, 2, k=k, pfx="p1")
            rp.load_gains(gA)
            rp.load_h(xT)
            rp.ffn(0, wg[0], wu[0], wd[0])
            rp.hn_to_dram(1, hnA.own)
            rp.store_h(hsp.ap(), final=False)
            rp.emit(finish=False)
            hnA.set_events(k.barrier(hnA.gathers()))
        if maybe_stop(k, "p1") or maybe_stop(k, "p1nocc"):
            return nc
        with ExitStack() as pes:
            rp = RowProg(nc, pes, TOKC, 2, k=k, pfx="p2")
            for r in range(2):
                rp.hn_from_dram(lambda r0, r1, r=r: (hnA.gat(r, r0, r1), hnA.gbuf(r0)))
                rp.proj_fm(w_att_in, 0, 832, projA.ap(), 0, r * TOKC)
                rp.proj_tm(w_att_in, 832, 256, vtokA.ap(), r * TOKC, 0)
                rp.proj_tm(w_att_in, 1088, 64, vtokA.ap(), r * TOKC, 256)
            rp.emit(finish=False)
            k.barrier()
        if maybe_stop(k, "p2"):
            return nc
        with ExitStack() as pes:
            pa, va = projA.ap(), vtokA.ap()
            D = dict(sqT=pa[0:256, :], skT=pa[256:512, :], bqT=pa[512:768, :], bkT=pa[768:832, :],
                     sv=[va[:, hl * 64:(hl + 1) * 64].rearrange("(b p) d -> p b d", p=P) for hl in range(4)],
                     bv=va[:, 256:320].rearrange("(b p) d -> p b d", p=P),
                     cst=a_cst, small=a_small, oT=oA.own)
            ap_ = AttnProg(nc, pes, D, k=k, pfx="p3")
            ap_.final = False
            ap_.emit(finish=False)
            oA.set_events(k.barrier(oA.gathers()))
        if maybe_stop(k, "p3"):
            return nc
        with ExitStack() as pes:
            rp = RowProg(nc, pes, TOKC, 4, k=k, pfx="p4")
            rp.load_gains(gB)
            rp.load_sel(sel)
            rp.load_h(hsp.ap())
            rp.mix_in_sel(lambda rc: (oA.gat(rc // 4, (rc % 4) * P, (rc % 4 + 1) * P), oA.gbuf((rc % 4) * P)), 8, w_att_out)
            rp.ffn(0, wg[1], wu[1], wd[1])
            rp.ple(1, wpg[0], wpp[0], pT[0])
            rp.ffn(2, wg[2], wu[2], wd[2])
            rp.hn_to_dram(3, hnB.own)
            rp.proj_fm(w_gdn_z, 0, 2048, zB.ap(), 0, 0)
            rp.store_h(hsp.ap(), final=False)
            rp.emit(finish=False)
            hnB.set_events(k.barrier(hnB.gathers()))
        if maybe_stop(k, "p4"):
            return nc
        with ExitStack() as pes:
            rp = RowProg(nc, pes, TOKC, 2, k=k, pfx="p5")
            for r in range(2):
                rp.hn_from_dram(lambda r0, r1, r=r: (hnB.gat(r, r0, r1), hnB.gbuf(r0)))
                rp.proj_fm(w_gdn_in, 0, 2048, projB.ap(), 0, r * TOKC)
                rp.proj_tm(w_gdn_in, 2048, 16, gtokB.ap(), r * TOKC, 0)
            rp.emit(finish=False)
            k.barrier()
        if maybe_stop(k, "p5"):
            return nc
        with ExitStack() as pes:
            D = dict(xT=projB.ap(), gtok=gtokB.ap(), gconst=g_gconst, convw=g_convw, small=g_small, c32=g_c32, oT=oB.own)
            gp = GdnProg(nc, pes, D, k=k, pfx="p6", fused=True)
            gp.emit(NCH, finish=False)
            oB.set_events(k.barrier(oB.gathers()))
        if maybe_stop(k, "p6"):
            return nc
        with ExitStack() as pes:
            rp = RowProg(nc, pes, TOKC, 2, k=k, pfx="p7")
            rp.load_gains(gC)
            rp.load_sel(sel)
            rp.load_h(hsp.ap())
            rp.gdn_gate_mix_in_sel(lambda rc: (oB.gat(rc // 8, (rc % 8) * P, (rc % 8 + 1) * P), oB.gbuf((rc % 8) * P)), zB.ap(), w_gdn_out)
            rp.ffn(0, wg[3], wu[3], wd[3])
            rp.ple(1, wpg[1], wpp[1], pT[1])
            rp.store_h(outT, final=True)
            rp.emit(finish=True)
    return nc


def kernel(x, p, ffn_norm, ffn_w_gate, ffn_w_up, ffn_w_down, mix_norm,
           att_w_in, att_q_norm, att_k_norm, att_sinks, att_w_out,
           gdn_w_in, gdn_conv_w, gdn_a_log, gdn_dt_bias, gdn_out_norm, gdn_w_out,
           ple_norm, ple_w_gate, ple_w_proj):
    f = lambda a: np.ascontiguousarray(np.asarray(a, dtype=np.float32))
    x = f(x).reshape(-1, 1024)
    p = f(p).reshape(2, -1, 256)
    ffn_norm, mix_norm, ple_norm = f(ffn_norm), f(mix_norm), f(ple_norm)
    wg, wu, wd = f(ffn_w_gate), f(ffn_w_up), f(ffn_w_down)
    att_w_in, att_w_out = f(att_w_in)[0], f(att_w_out)[0]
    gdn_w_in, gdn_w_out = f(gdn_w_in)[0], f(gdn_w_out)[0]
    conv_w, a_log, dt_bias = f(gdn_conv_w)[0], f(gdn_a_log)[0], f(gdn_dt_bias)[0]
    qg, kg, sinks = f(att_q_norm)[0], f(att_k_norm)[0], f(att_sinks)[0]
    tok = lambda c: slice(c * TOKC, (c + 1) * TOKC)
    ar = np.arange

    shared = {}
    for i, (l, j) in enumerate([(0, 0), (0, 1), (1, 0), (1, 1)]):
        shared["wg%d" % i] = wg[l, j]
        shared["wu%d" % i] = wu[l, j]
        shared["wd%d" % i] = wd[l, j]
    for i in range(2):
        shared["wpg%d" % i] = f(ple_w_gate)[i]
        shared["wpp%d" % i] = f(ple_w_proj)[i]
    shared["gA"] = _gain_layout([ffn_norm[0, 0], mix_norm[0]])
    shared["gB"] = _gain_layout([ffn_norm[0, 1], ple_norm[0], ffn_norm[1, 0], mix_norm[1]])
    shared["gC"] = _gain_layout([ffn_norm[1, 1], ple_norm[1]], extra=f(gdn_out_norm)[0])
    rows = np.concatenate([ar(0, 256), 512 + ar(0, 256), ar(256, 512), 512 + ar(256, 512)])
    shared["w_att_out"] = np.ascontiguousarray(att_w_out[rows])
    shared["w_gdn_z"] = np.ascontiguousarray(gdn_w_in[:, 4096:6144])
    shared["w_gdn_out"] = gdn_w_out
    shared["g_c32"] = gdn_consts()
    sm = np.zeros((128, 8), np.float32)
    sm[:, 0] = 1.0
    sm[:, 1] = 1e-6
    sm[:, 2] = LN_QSCALE
    shared["g_small"] = sm

    maps = []
    for c in range(NCORES):
        b, half = c // 2, c % 2
        m = dict(shared)
        m["xT"] = np.ascontiguousarray(x[tok(c)].T)
        m["pT0"] = np.ascontiguousarray(p[0][tok(c)].T)
        m["pT1"] = np.ascontiguousarray(p[1][tok(c)].T)
        s = np.zeros((128, 2), np.float32)
        s[:, half] = 1.0
        m["sel"] = s
        h4 = half * 256
        cols = np.concatenate([ar(h4, h4 + 256), 512 + ar(h4, h4 + 256), 1536 + ar(h4, h4 + 256),
                               2048 + ar(half * 64, half * 64 + 64), 1024 + ar(h4, h4 + 256), 2176 + ar(half * 64, half * 64 + 64)])
        m["w_att_in"] = np.ascontiguousarray(att_w_in[:, cols])
        cols = np.concatenate([ar(half * 512, half * 512 + 512), 1024 + ar(half * 512, half * 512 + 512),
                               2048 + ar(half * 1024, half * 1024 + 1024),
                               6160 + ar(half * 8, half * 8 + 8), 6144 + ar(half * 8, half * 8 + 8)])
        m["w_gdn_in"] = np.ascontiguousarray(gdn_w_in[:, cols])
        m["a_cst"] = attn_consts(half)
        sma = np.zeros((128, 520), np.float32)
        sma[0:64, 0] = qg
        sma[0:64, 1] = kg
        sma[:, 2] = 1.0
        sma[:, 3] = -SHIFT
        sma[:, 4] = 1e-6
        sma[:, 5] = 64e-6
        sma[:, 6] = -2.0794415416798357
        for hl in range(4):
            sma[0:64, 8 + hl * 128:8 + (hl + 1) * 128] = sinks[4 * half + hl]
        m["a_small"] = sma
        chs = np.concatenate([ar((4 * half + mm) * 128, (4 * half + mm + 1) * 128) for mm in range(4)] +
                             [1024 + ar((4 * half + mm) * 128, (4 * half + mm + 1) * 128) for mm in range(4)] +
                             [2048 + ar((8 * half + hv) * 128, (8 * half + hv + 1) * 128) for hv in range(8)])
        cw = conv_w[:, chs]
        m["g_convw"] = np.ascontiguousarray(cw.reshape(4, 16, 128).transpose(2, 1, 0).reshape(128, 64))
        gc = np.zeros((128, 2, 256), np.float32)
        gc[:, 0, :] = np.tile(dt_bias[8 * half:8 * half + 8], 32)[None, :]
        gc[:, 1, :] = np.tile(a_log[8 * half:8 * half + 8], 32)[None, :]
        m["g_gconst"] = gc
        maps.append(m)

    nc = build_fused(stop=_DBG.get("stop"), dump=_DBG.get("dump"))
    if _DBG.get("trace"):
        full = run_bass_kernel_spmd(nc, maps, core_ids=list(range(NCORES)), trace=True)
        _DBG["full"] = full
        res = full.results
    else:
        res = run_bass_kernel_spmd(nc, maps, core_ids=list(range(NCORES))).results
    if _DBG.get("stop"):
        _DBG["res"] = res
        return None
    out = np.concatenate([res[c]["outT"].T for c in range(NCORES)], 0)
    return np.ascontiguousarray(out.reshape(4, 4096, 1024).astype(np.float32))
```

```python
import numpy as np
import concourse.bass as bass
import concourse.mybir as mybir
from concourse.alu_op_type import AluOpType as ALU
from contextlib import ExitStack

F32 = mybir.dt.float32
BF16 = mybir.dt.bfloat16
AF = mybir.ActivationFunctionType
AX = mybir.AxisListType
P = 128


class Buf:
    __slots__ = ("name", "lastw", "reads", "dsem", "dcount", "excl")

    def __init__(self, name):
        self.name = name
        self.lastw = None
        self.reads = {}
        self.dsem = None
        self.dcount = 0
        self.excl = False


class K:
    def __init__(self, nc, es, safe_same=True):
        self.nc = nc
        self.es = es
        self.E = {}
        for n in ("tensor", "vector", "scalar", "gpsimd", "sync"):
            sem = es.enter_context(nc.semaphore("e_" + n))
            self.E[n] = dict(eng=getattr(nc, n), sem=sem, count=0, waited={}, name=n)
        self.safe_same = safe_same
        self.dma_sems = {}
        self.free_dsems = []
        self.nsem_alloc = 0
        self.bar_sem = es.enter_context(nc.semaphore("barrier"))
        self.bar_count = 0
        self.cc_sem = es.enter_context(nc.semaphore("ccsem"))
        self.cc_count = 0
        self.nbuf = 0
        self.final_events = []
        self.ninstr = 0

    def buf(self, name=None):
        self.nbuf += 1
        return Buf(name or ("b%d" % self.nbuf))

    def _wait(self, e, sem, val):
        key = id(sem)
        if e["waited"].get(key, 0) >= val:
            return
        e["eng"].wait_ge(sem, val)
        e["waited"][key] = val
        if getattr(self, "trace", None) is not None:
            self.trace.append((e["name"], "wait", [n for n, x in self.E.items() if x["sem"] is sem] or "dma", val))

    def _emit_waits(self, en, reads, writes):
        e = self.E[en]
        evs = []
        for b in reads:
            if b.lastw is not None:
                evs.append(b.lastw)
            if b.excl:
                evs.extend(ev for ev in b.reads.values() if ev[2] != en)
        for b in writes:
            if b.lastw is not None:
                evs.append(b.lastw)
            evs.extend(b.reads.values())
        for (sem, val, src) in evs:
            if src == en and (en == "tensor" or not self.safe_same):
                continue
            self._wait(e, sem, val)

    def _record(self, ev, reads, writes):
        for b in writes:
            b.lastw = ev
            b.reads = {}
        for b in reads:
            if b in writes:
                continue
            key = id(ev[0])
            old = b.reads.get(key)
            if old is None or old[1] < ev[1]:
                b.reads[key] = ev

    def op(self, en, fn, reads=(), writes=()):
        e = self.E[en]
        self._emit_waits(en, reads, writes)
        ins = fn(e["eng"])
        e["count"] += 1
        ins.then_inc(e["sem"], 1)
        ev = (e["sem"], e["count"], en)
        self._record(ev, reads, writes)
        self.ninstr += 1
        if getattr(self, "trace", None) is not None:
            self.trace.append((en, "op", e["count"], [b.name for b in reads], [b.name for b in writes]))
        return ev

    def dma(self, qn, pairs, reads=(), writes=(), final=False):
        e = self.E[qn]
        self._emit_waits(qn, reads, writes)
        owner = writes[0] if len(writes) else reads[0]
        if owner.dsem is None:
            if self.free_dsems:
                owner.dsem, owner.dcount = self.free_dsems.pop()
            else:
                self.nsem_alloc += 1
                owner.dsem = self.es.enter_context(self.nc.semaphore("d%d_%s" % (self.nsem_alloc, owner.name)))
        for (o, i) in pairs:
            ins = e["eng"].dma_start(out=o, in_=i)
            owner.dcount += 16
            ins.then_inc(owner.dsem, 16)
            self.ninstr += 1
        ev = (owner.dsem, owner.dcount, "dma")
        self.dma_sems[id(owner.dsem)] = [owner.dsem, owner.dcount]
        self._record(ev, reads, writes)
        if final:
            self.final_events.append(ev)
        return ev

    def barrier(self, collective_fn=None):
        g = self.E["gpsimd"]
        for n, e in self.E.items():
            if n != "gpsimd" and e["count"] > 0:
                self._wait(g, e["sem"], e["count"])
        if g["count"] > 0 and self.safe_same:
            self._wait(g, g["sem"], g["count"])
        for sem, val in self.dma_sems.values():
            self._wait(g, sem, val)
        fns = collective_fn if isinstance(collective_fn, (list, tuple)) else ([collective_fn] if collective_fn else [])
        ins = g["eng"].nop()
        self.bar_count += 1
        ins.then_inc(self.bar_sem, 1)
        for n, e in self.E.items():
            self._wait(e, self.bar_sem, self.bar_count)
            for n2, e2 in self.E.items():
                e["waited"][id(e2["sem"])] = max(e["waited"].get(id(e2["sem"]), 0), e2["count"])
            for sem, val in self.dma_sems.values():
                e["waited"][id(sem)] = max(e["waited"].get(id(sem), 0), val)
        events = []
        for fn in fns:
            ins = fn(g["eng"])
            self.cc_count += 1
            ins.then_inc(self.cc_sem, 1)
            events.append((self.cc_sem, self.cc_count, "cc"))
        self.free_dsems.extend((sem, val) for sem, val in self.dma_sems.values())
        self.dma_sems = {}
        return events

    def finish(self):
        e = self.E["sync"]
        for (sem, val, src) in self.final_events:
            self._wait(e, sem, val)
EPS = 1e-6
TT = 512


class RR:
    def __init__(self, items):
        self.items = items
        self.i = 0

    def next(self):
        it = self.items[self.i % len(self.items)]
        self.i += 1
        return it


class RowProg:
    def __init__(self, nc, es, TOK, n_gain, NST=3, NWB=4, safe_same=True, k=None, pfx=""):
        self.nc = nc
        self.es = es
        k = self.k = k if k is not None else K(nc, es, safe_same=safe_same)
        self.TOK = TOK
        self.pfx = pfx
        self.NTT = TOK // TT
        NTT = self.NTT

        def alloc(name, shape, dt):
            return es.enter_context(nc.sbuf_tensor(pfx + "sb_" + name, shape, dt))

        self.hT = alloc("hT", [P, 8, TOK], F32)
        self.hTb = [[k.buf("hT%d_%d" % (c, t)) for t in range(NTT)] for c in range(8)]
        self.hn = alloc("hn", [P, 8, TOK], BF16)
        self.hnb = [[k.buf("hn%d_%d" % (c, t)) for t in range(NTT)] for c in range(8)]
        self.act = alloc("act", [P, 11, TOK], BF16)
        self.actb = [[k.buf("ac%d_%d" % (c, t)) for t in range(NTT)] for c in range(11)]
        wst = alloc("wst", [P, NST, 2048], F32)
        self.wst = RR([(k.buf("wst%d" % i), wst[:, i, :]) for i in range(NST)])
        wbf = alloc("wbf", [P, NWB, 2048], BF16)
        self.wbf = RR([(k.buf("wbf%d" % i), wbf[:, i, :]) for i in range(NWB)])
        sq = alloc("sq", [P, 2, TT], BF16)
        self.sq = RR([(k.buf("sq%d" % i), sq[:, i, :]) for i in range(2)])
        t32 = alloc("t32", [P, 4, TT], F32)
        self.t32 = RR([(k.buf("t32_%d" % i), t32[:, i, :]) for i in range(4)])
        self.ones = alloc("ones", [P, P], BF16)
        self.onesb = k.buf("ones")
        self.gains = alloc("gains", [P, n_gain * 8 + 1], F32)
        self.n_gain = n_gain
        self.gainsb = k.buf("gains")
        ps = [es.enter_context(nc.psum_tensor(pfx + "ps%d" % i, [P, TT], F32)) for i in range(7)]
        self.ps = RR([(k.buf("ps%d" % i), ps[i][:, :]) for i in range(7)])
        self.items = []
        k.op("vector", lambda e: e.memset(self.ones[:], 1.0), writes=[self.onesb])
        self.epsc = alloc("epsc", [P, 1], F32)
        self.epsap = self.epsc[:, 0:1]
        k.op("vector", lambda e: e.memset(self.epsc[:], EPS), writes=[self.onesb])

    def ts(self, tt):
        return slice(tt * TT, (tt + 1) * TT)

    def item(self, specs, fn):
        self.items.append((specs, fn))

    def load_gains(self, g_dram):
        self.k.dma("sync", [(self.gains[:], g_dram)], writes=[self.gainsb])

    def obuf(self, rc):
        if rc < 8:
            return self.hnb[rc], self.hn[:, rc, :]
        return self.actb[rc - 8], self.act[:, rc - 8, :]

    def _issue_load(self, spec):
        k = self.k
        w, r0, R, c0, ncols = spec
        assert R * ncols <= 2048
        sb, sap = self.wst.next()
        wb, wap = self.wbf.next()
        src = w[r0 * P:(r0 + R) * P, c0:c0 + ncols].rearrange("(r p) n -> p r n", p=P)
        dst = sap[:, 0:R * ncols].rearrange("p (r n) -> p r n", r=R)
        k.dma("sync", [(dst, src)], writes=[sb])
        k.op("gpsimd", lambda e: e.tensor_copy(out=wap[:, 0:R * ncols], in_=sap[:, 0:R * ncols]),
             reads=[sb], writes=[wb])
        return wb, wap[:, 0:R * ncols].rearrange("p (r n) -> p r n", r=R)

    def emit(self, lookahead=1, finish=True):
        items = self.items
        loaded = {}
        nl = 0
        for i, (specs, fn) in enumerate(items):
            while nl < len(items) and nl <= i + lookahead:
                loaded[nl] = [self._issue_load(s) for s in items[nl][0]]
                nl += 1
            fn(loaded.pop(i))
        if finish:
            self.k.finish()

    def load_h(self, xT_dram):
        def fn(_):
            for c in range(8):
                self.k.dma("sync", [(self.hT[:, c, :], xT_dram[c * P:(c + 1) * P, :])], writes=self.hTb[c])
        self.item([], fn)

    def store_h(self, out_dram, final=True):
        def fn(_):
            for c in range(8):
                self.k.dma("sync", [(out_dram[c * P:(c + 1) * P, :], self.hT[:, c, :])], reads=self.hTb[c], final=final)
        self.item([], fn)

    def load_sel(self, sel_dram):
        self.sel = self.es.enter_context(self.nc.sbuf_tensor(self.pfx + "sb_sel", [P, 2], F32))
        self.selb = self.k.buf("sel")
        self.k.dma("sync", [(self.sel[:], sel_dram)], writes=[self.selb])

    def hn_to_dram(self, gi, dst):
        self.norm(gi)

        def fn(_):
            for c in range(8):
                self.k.dma("sync", [(dst(c * P, (c + 1) * P), self.hn[:, c, :])], reads=self.hnb[c])
        self.item([], fn)

    def hn_from_dram(self, src):
        def fn(_):
            for c in range(8):
                sap_, sbuf_ = src(c * P, (c + 1) * P)
                self.k.dma("sync", [(self.hn[:, c, :], sap_)], reads=[sbuf_], writes=self.hnb[c])
        self.item([], fn)

    def proj_fm(self, w, c0w, n_out, out_dram, row0, col0):
        k = self.k
        c0 = 0
        while c0 < n_out:
            ncols = min(256, n_out - c0)

            def fn(tiles, c0=c0, ncols=ncols):
                (wb, wap), = tiles
                j0 = 0
                while j0 < ncols:
                    m = min(P, ncols - j0)
                    for tt in range(self.NTT):
                        ts = self.ts(tt)
                        pb, pap = self.ps.next()
                        for c in range(8):
                            k.op("tensor", lambda e: e.matmul(pap[0:m, :], lhsT=wap[:, c, j0:j0 + m], rhs=self.hn[:, c, ts],
                                                              start=(c == 0), stop=(c == 7)),
                                 reads=[wb, self.hnb[c][tt]], writes=[pb])
                        eb, eap = self.t32.next()
                        if tt % 2 == 0:
                            k.op("scalar", lambda e: e.copy(out=eap[0:m, :], in_=pap[0:m, :]), reads=[pb], writes=[eb])
                        else:
                            k.op("vector", lambda e: e.tensor_copy(out=eap[0:m, :], in_=pap[0:m, :]), reads=[pb], writes=[eb])
                        r0 = row0 + c0 + j0
                        k.dma("sync", [(out_dram[r0:r0 + m, col0 + tt * TT:col0 + (tt + 1) * TT], eap[0:m, :])], reads=[eb])
                    j0 += m
            self.item([(w, 0, 8, c0w + c0, ncols)], fn)
            c0 += ncols

    def proj_tm(self, w, c0w, ncols, out_dram, tok0, ocol0):
        k = self.k

        def fn(tiles):
            (wb, wap), = tiles
            for tb in range(self.TOK // P):
                tt = (tb * P) // TT
                pb, pap = self.ps.next()
                for c in range(8):
                    k.op("tensor", lambda e: e.matmul(pap[:, 0:ncols], lhsT=self.hn[:, c, tb * P:(tb + 1) * P], rhs=wap[:, c, 0:ncols],
                                                      start=(c == 0), stop=(c == 7)),
                         reads=[wb, self.hnb[c][tt]], writes=[pb])
                eb, eap = self.t32.next()
                if tb % 2 == 0:
                    k.op("scalar", lambda e: e.copy(out=eap[:, 0:ncols], in_=pap[:, 0:ncols]), reads=[pb], writes=[eb])
                else:
                    k.op("vector", lambda e: e.tensor_copy(out=eap[:, 0:ncols], in_=pap[:, 0:ncols]), reads=[pb], writes=[eb])
                k.dma("sync", [(out_dram[tok0 + tb * P:tok0 + (tb + 1) * P, ocol0:ocol0 + ncols], eap[:, 0:ncols])], reads=[eb])
        self.item([(w, 0, 8, c0w, ncols)], fn)

    def _load_sel_chunk(self, G, rc):
        k = self.k
        T = self.TOK
        ab, aap = self.wst.next()
        gap_, gbuf_ = G(rc)
        k.dma("sync", [(aap[:, 0:T], gap_[:, 0:T])], reads=[gbuf_], writes=[ab])
        bb, bap = self.wst.next()
        k.dma("sync", [(bap[:, 0:T], gap_[:, T:2 * T])], reads=[gbuf_], writes=[bb])
        k.op("scalar", lambda e: e.activation(out=aap[:, 0:T], in_=aap[:, 0:T], func=AF.Copy, scale=self.sel[:, 0:1]),
             reads=[ab, self.selb], writes=[ab])
        return ab, aap, bb, bap

    def mix_in_sel(self, G, nrc, w_out):
        k = self.k

        def fn(_):
            for rc in range(nrc):
                bufs, dap = self.obuf(rc)
                ab, aap, bb, bap = self._load_sel_chunk(G, rc)
                k.op("vector", lambda e: e.scalar_tensor_tensor(out=dap, in0=bap[:, 0:self.TOK], scalar=self.sel[:, 1:2],
                                                                in1=aap[:, 0:self.TOK], op0=ALU.mult, op1=ALU.add),
                     reads=[ab, bb, self.selb], writes=bufs)
        self.item([], fn)
        self._mix_matmuls(nrc, w_out)

    def _mix_matmuls(self, nrc, w_out):
        k = self.k
        for dc in range(8):
            def fn(tiles, dc=dc):
                (wb, wap), = tiles
                for tt in range(self.NTT):
                    ts = self.ts(tt)
                    pb, pap = self.ps.next()
                    for rc in range(nrc):
                        bufs, oap = self.obuf(rc)
                        k.op("tensor", lambda e: e.matmul(pap, lhsT=wap[:, rc, :], rhs=oap[:, ts],
                                                          start=(rc == 0), stop=(rc == nrc - 1)),
                             reads=[wb, bufs[tt]], writes=[pb])
                    k.op("vector", lambda e: e.tensor_tensor(out=self.hT[:, dc, ts], in0=pap, in1=self.hT[:, dc, ts], op=ALU.add),
                         reads=[pb, self.hTb[dc][tt]], writes=[self.hTb[dc][tt]])
            self.item([(w_out, 0, nrc, dc * P, P)], fn)

    def gdn_gate_mix_in_sel(self, G, zT_dram, w_out):
        k = self.k
        gcol = self.gains[:, self.n_gain * 8:self.n_gain * 8 + 1]

        def fn(_):
            for rc in range(16):
                bufs, dap = self.obuf(rc)
                ob, oap, bb, bap = self._load_sel_chunk(G, rc)
                k.op("vector", lambda e: e.scalar_tensor_tensor(out=oap[:, 0:self.TOK], in0=bap[:, 0:self.TOK], scalar=self.sel[:, 1:2],
                                                                in1=oap[:, 0:self.TOK], op0=ALU.mult, op1=ALU.add),
                     reads=[ob, bb, self.selb], writes=[ob])
                zb, zap = self.wst.next()
                k.dma("sync", [(zap[:, 0:self.TOK], zT_dram[rc * P:(rc + 1) * P, :])], writes=[zb])
                for tt in range(self.NTT):
                    ts = self.ts(tt)
                    qb, qap = self.sq.next()
                    k.op("scalar", lambda e: e.activation(out=qap, in_=oap[:, ts], func=AF.Square), reads=[ob], writes=[qb])
                    pb, pap = self.ps.next()
                    k.op("tensor", lambda e: e.matmul(pap, lhsT=self.ones[:], rhs=qap, start=True, stop=True),
                         reads=[qb, self.onesb], writes=[pb])
                    tb, tap = self.t32.next()
                    k.op("scalar", lambda e: e.activation(out=tap, in_=pap, func=AF.Ln, scale=1.0 / 128.0, bias=self.epsap),
                         reads=[pb, self.onesb], writes=[tb])
                    k.op("scalar", lambda e: e.activation(out=tap, in_=tap, func=AF.Exp, scale=-0.5), reads=[tb], writes=[tb])
                    k.op("vector", lambda e: e.scalar_tensor_tensor(out=oap[:, ts], in0=oap[:, ts], scalar=gcol, in1=tap,
                                                                    op0=ALU.mult, op1=ALU.mult),
                         reads=[ob, tb, self.gainsb], writes=[ob])
                for tt in range(self.NTT):
                    ts = self.ts(tt)
                    k.op("scalar", lambda e: e.activation(out=zap[:, ts], in_=zap[:, ts], func=AF.Silu), reads=[zb], writes=[zb])
                    k.op("vector", lambda e: e.tensor_tensor(out=dap[:, ts], in0=oap[:, ts], in1=zap[:, ts], op=ALU.mult),
                         reads=[ob, zb], writes=[bufs[tt]])
        self.item([], fn)
        self._mix_matmuls(16, w_out)

    def norm(self, gi):
        def fn(_):
            k = self.k
            for tt in range(self.NTT):
                ts = self.ts(tt)
                pb, pap = self.ps.next()
                for c in range(8):
                    qb, qap = self.sq.next()
                    k.op("scalar", lambda e: e.activation(out=qap, in_=self.hT[:, c, ts], func=AF.Square),
                         reads=[self.hTb[c][tt]], writes=[qb])
                    k.op("tensor", lambda e: e.matmul(pap, lhsT=self.ones[:], rhs=qap, start=(c == 0), stop=(c == 7)),
                         reads=[qb, self.onesb], writes=[pb])
                tb, tap = self.t32.next()
                k.op("scalar", lambda e: e.activation(out=tap, in_=pap, func=AF.Ln, scale=1.0 / 1024.0, bias=self.epsap),
                     reads=[pb, self.onesb], writes=[tb])
                rb, rap = self.t32.next()
                k.op("scalar", lambda e: e.activation(out=rap, in_=tap, func=AF.Exp, scale=-0.5), reads=[tb], writes=[rb])
                for c in range(8):
                    k.op("vector", lambda e: e.scalar_tensor_tensor(
                        out=self.hn[:, c, ts], in0=self.hT[:, c, ts], scalar=self.gains[:, gi * 8 + c:gi * 8 + c + 1],
                        in1=rap, op0=ALU.mult, op1=ALU.mult),
                        reads=[self.hTb[c][tt], rb, self.gainsb], writes=[self.hnb[c][tt]])
        self.item([], fn)

    def ffn(self, gi, wg, wu, wd):
        self.norm(gi)
        k = self.k
        for half in range(2):
            f0 = half * 11
            groups = [(0, 2), (2, 2), (4, 2), (6, 2), (8, 2), (10, 1)]
            for (fl0, nfc) in groups:
                def fn(tiles, fl0=fl0, nfc=nfc):
                    (gb, gap), (ub, uap) = tiles
                    for j in range(nfc):
                        for tt in range(self.NTT):
                            ts = self.ts(tt)
                            pgb, pg = self.ps.next()
                            pub, pu = self.ps.next()
                            for c in range(8):
                                k.op("tensor", lambda e: e.matmul(pg, lhsT=gap[:, c, j * P:(j + 1) * P], rhs=self.hn[:, c, ts],
                                                                  start=(c == 0), stop=(c == 7)),
                                     reads=[gb, self.hnb[c][tt]], writes=[pgb])
                            for c in range(8):
                                k.op("tensor", lambda e: e.matmul(pu, lhsT=uap[:, c, j * P:(j + 1) * P], rhs=self.hn[:, c, ts],
                                                                  start=(c == 0), stop=(c == 7)),
                                     reads=[ub, self.hnb[c][tt]], writes=[pub])
                            sb, sap = self.t32.next()
                            k.op("scalar", lambda e: e.activation(out=sap, in_=pg, func=AF.Silu), reads=[pgb], writes=[sb])
                            k.op("vector", lambda e: e.tensor_tensor(out=self.act[:, fl0 + j, ts], in0=sap, in1=pu, op=ALU.mult),
                                 reads=[sb, pub], writes=[self.actb[fl0 + j][tt]])
                c0 = (f0 + fl0) * P
                self.item([(wg, 0, 8, c0, nfc * P), (wu, 0, 8, c0, nfc * P)], fn)
            for dc in range(8):
                def fn(tiles, dc=dc):
                    (wb, wap), = tiles
                    for tt in range(self.NTT):
                        ts = self.ts(tt)
                        pb, pap = self.ps.next()
                        for f in range(11):
                            k.op("tensor", lambda e: e.matmul(pap, lhsT=wap[:, f, :], rhs=self.act[:, f, ts],
                                                              start=(f == 0), stop=(f == 10)),
                                 reads=[wb, self.actb[f][tt]], writes=[pb])
                        k.op("vector", lambda e: e.scalar_tensor_tensor(
                            out=self.hT[:, dc, ts], in0=pap, scalar=0.5, in1=self.hT[:, dc, ts],
                            op0=ALU.mult, op1=ALU.add),
                            reads=[pb, self.hTb[dc][tt]], writes=[self.hTb[dc][tt]])
                self.item([(wd, f0, 11, dc * P, P)], fn)

    def proj_out(self, gi, w, n_out, out_dram):
        self.norm(gi)
        k = self.k
        c0 = 0
        while c0 < n_out:
            ncols = min(256, n_out - c0)

            def fn(tiles, c0=c0, ncols=ncols):
                (wb, wap), = tiles
                j0 = 0
                while j0 < ncols:
                    m = min(P, ncols - j0)
                    for tt in range(self.NTT):
                        ts = self.ts(tt)
                        pb, pap = self.ps.next()
                        for c in range(8):
                            k.op("tensor", lambda e: e.matmul(pap[0:m, :], lhsT=wap[:, c, j0:j0 + m], rhs=self.hn[:, c, ts],
                                                              start=(c == 0), stop=(c == 7)),
                                 reads=[wb, self.hnb[c][tt]], writes=[pb])
                        eb, eap = self.t32.next()
                        if tt % 2 == 0:
                            k.op("scalar", lambda e: e.copy(out=eap[0:m, :], in_=pap[0:m, :]), reads=[pb], writes=[eb])
                        else:
                            k.op("vector", lambda e: e.tensor_copy(out=eap[0:m, :], in_=pap[0:m, :]), reads=[pb], writes=[eb])
                        k.dma("sync", [(out_dram[c0 + j0:c0 + j0 + m, ts], eap[0:m, :])], reads=[eb], final=True)
                    j0 += m
            self.item([(w, 0, 8, c0, ncols)], fn)
            c0 += ncols

    def load_T_bf16(self, src_dram, nrc):
        def fn(_):
            k = self.k
            for rc in range(nrc):
                bufs, dap = self.obuf(rc)
                sb, sap = self.wst.next()
                k.dma("sync", [(sap[:, 0:self.TOK], src_dram[rc * P:(rc + 1) * P, :])], writes=[sb])
                k.op("gpsimd", lambda e: e.tensor_copy(out=dap, in_=sap[:, 0:self.TOK]), reads=[sb], writes=bufs)
        self.item([], fn)

    def mix_in(self, oT_dram, nrc, w_out):
        self.load_T_bf16(oT_dram, nrc)
        k = self.k
        for dc in range(8):
            def fn(tiles, dc=dc):
                (wb, wap), = tiles
                for tt in range(self.NTT):
                    ts = self.ts(tt)
                    pb, pap = self.ps.next()
                    for rc in range(nrc):
                        bufs, oap = self.obuf(rc)
                        k.op("tensor", lambda e: e.matmul(pap, lhsT=wap[:, rc, :], rhs=oap[:, ts],
                                                          start=(rc == 0), stop=(rc == nrc - 1)),
                             reads=[wb, bufs[tt]], writes=[pb])
                    k.op("vector", lambda e: e.tensor_tensor(out=self.hT[:, dc, ts], in0=pap, in1=self.hT[:, dc, ts], op=ALU.add),
                         reads=[pb, self.hTb[dc][tt]], writes=[self.hTb[dc][tt]])
            self.item([(w_out, 0, nrc, dc * P, P)], fn)

    def ple(self, gi, wpg, wpp, pT_dram):
        self.norm(gi)
        k = self.k

        def fnp(_):
            for rc in range(2):
                sb, sap = self.wst.next()
                k.dma("sync", [(sap[:, 0:self.TOK], pT_dram[rc * P:(rc + 1) * P, :])], writes=[sb])
                k.op("gpsimd", lambda e: e.tensor_copy(out=self.act[:, rc, :], in_=sap[:, 0:self.TOK]),
                     reads=[sb], writes=self.actb[rc])
        self.item([], fnp)
        for dc in range(8):
            def fn(tiles, dc=dc):
                (gb, gap), (pb_, pap_) = tiles
                for tt in range(self.NTT):
                    ts = self.ts(tt)
                    pgb, pg = self.ps.next()
                    ppb, pp = self.ps.next()
                    for c in range(8):
                        k.op("tensor", lambda e: e.matmul(pg, lhsT=gap[:, c, :], rhs=self.hn[:, c, ts],
                                                          start=(c == 0), stop=(c == 7)),
                             reads=[gb, self.hnb[c][tt]], writes=[pgb])
                    for c in range(2):
                        k.op("tensor", lambda e: e.matmul(pp, lhsT=pap_[:, c, :], rhs=self.act[:, c, ts],
                                                          start=(c == 0), stop=(c == 1)),
                             reads=[pb_, self.actb[c][tt]], writes=[ppb])
                    sb, sap = self.t32.next()
                    k.op("scalar", lambda e: e.activation(out=sap, in_=pg, func=AF.Sigmoid), reads=[pgb], writes=[sb])
                    mb, map_ = self.t32.next()
                    k.op("vector", lambda e: e.tensor_tensor(out=map_, in0=sap, in1=pp, op=ALU.mult),
                         reads=[sb, ppb], writes=[mb])
                    k.op("vector", lambda e: e.tensor_tensor(out=self.hT[:, dc, ts], in0=map_, in1=self.hT[:, dc, ts], op=ALU.add),
                         reads=[mb, self.hTb[dc][tt]], writes=[self.hTb[dc][tt]])
            self.item([(wpg, 0, 8, dc * P, P), (wpp, 0, 2, dc * P, P)], fn)

    def gdn_gate_mix_in(self, oT_dram, zT_dram, w_out):
        k = self.k
        gcol = self.gains[:, self.n_gain * 8:self.n_gain * 8 + 1]

        def fn(_):
            for rc in range(16):
                bufs, dap = self.obuf(rc)
                ob, oap = self.wst.next()
                k.dma("sync", [(oap[:, 0:self.TOK], oT_dram[rc * P:(rc + 1) * P, :])], writes=[ob])
                zb, zap = self.wst.next()
                k.dma("sync", [(zap[:, 0:self.TOK], zT_dram[rc * P:(rc + 1) * P, :])], writes=[zb])
                rstd = []
                for tt in range(self.NTT):
                    ts = self.ts(tt)
                    qb, qap = self.sq.next()
                    k.op("scalar", lambda e: e.activation(out=qap, in_=oap[:, ts], func=AF.Square), reads=[ob], writes=[qb])
                    pb, pap = self.ps.next()
                    k.op("tensor", lambda e: e.matmul(pap, lhsT=self.ones[:], rhs=qap, start=True, stop=True),
                         reads=[qb, self.onesb], writes=[pb])
                    tb, tap = self.t32.next()
                    k.op("scalar", lambda e: e.activation(out=tap, in_=pap, func=AF.Ln, scale=1.0 / 128.0, bias=self.epsap),
                         reads=[pb, self.onesb], writes=[tb])
                    k.op("scalar", lambda e: e.activation(out=tap, in_=tap, func=AF.Exp, scale=-0.5), reads=[tb], writes=[tb])
                    k.op("vector", lambda e: e.scalar_tensor_tensor(out=oap[:, ts], in0=oap[:, ts], scalar=gcol, in1=tap,
                                                                    op0=ALU.mult, op1=ALU.mult),
                         reads=[ob, tb, self.gainsb], writes=[ob])
                for tt in range(self.NTT):
                    ts = self.ts(tt)
                    k.op("scalar", lambda e: e.activation(out=zap[:, ts], in_=zap[:, ts], func=AF.Silu), reads=[zb], writes=[zb])
                    k.op("vector", lambda e: e.tensor_tensor(out=dap[:, ts], in0=oap[:, ts], in1=zap[:, ts], op=ALU.mult),
                         reads=[ob, zb], writes=[bufs[tt]])
        self.item([], fn)
        for dc in range(8):
            def fn2(tiles, dc=dc):
                (wb, wap), = tiles
                for tt in range(self.NTT):
                    ts = self.ts(tt)
                    pb, pap = self.ps.next()
                    for rc in range(16):
                        bufs, oap = self.obuf(rc)
                        k.op("tensor", lambda e: e.matmul(pap, lhsT=wap[:, rc, :], rhs=oap[:, ts],
                                                          start=(rc == 0), stop=(rc == 15)),
                             reads=[wb, bufs[tt]], writes=[pb])
                    k.op("vector", lambda e: e.tensor_tensor(out=self.hT[:, dc, ts], in0=pap, in1=self.hT[:, dc, ts], op=ALU.add),
                         reads=[pb, self.hTb[dc][tt]], writes=[self.hTb[dc][tt]])
            self.item([(w_out, 0, 16, dc * P, P)], fn2)
SEQ = 4096
NBLK = 32
BIG = 30000.0
SHIFT = 8.0


class AttnProg:
    final = True

    def __init__(self, nc, es, D, safe_same=True, k=None, pfx=""):
        self.nc = nc
        self.es = es
        k = self.k = k if k is not None else K(nc, es, safe_same=safe_same)
        self.D = D

        def alloc(name, shape, dt):
            return es.enter_context(nc.sbuf_tensor(pfx + "sa_" + name, shape, dt))

        def rr(name, shape, dt, n):
            t = alloc(name, [shape[0], n] + list(shape[1:]), dt)
            return RR([(k.buf("%s%d" % (name, i)), t[:, i]) for i in range(n)])

        self.alloc = alloc
        self.cst = alloc("cst", [P, 128 * 4 + 4 * 512 + 2 * 512], BF16)
        self.cb = k.buf("cst")
        self.small = alloc("small", [P, 8 + 512], F32)
        self.smallb = k.buf("small")
        self.stage = rr("stage", [P, SEQ], F32, 2)
        self.qT = rr("qT", [64, SEQ], BF16, 4)
        self.kT = rr("kT", [64, SEQ], BF16, 4)
        self.v = rr("v", [P, NBLK * 64], BF16, 5)
        self.e32 = [rr("e32_%d" % s, [P, 512], F32, 1) for s in range(4)]
        self.sp = [rr("sp_%d" % s, [P, 512], BF16, 2) for s in range(4)]
        self.w = [rr("w_%d" % s, [P, 512], BF16, 1) for s in range(4)]
        self.ls = [rr("ls_%d" % s, [P, 512], BF16, 2) for s in range(4)]
        self.ost = rr("ost", [64, 512], F32, 2)
        ps = [es.enter_context(nc.psum_tensor(pfx + "psa%d" % i, [P, 512], F32)) for i in range(8)]
        self.psb = [(k.buf("psa%d" % i), ps[i][:, :]) for i in range(8)]
        for b, _ in self.psb:
            b.excl = True

    def consts(self):
        k, D = self.k, self.D
        sb, sap = self.stage.next()
        n = 128 * 4 + 4 * 512 + 2 * 512
        k.dma("sync", [(sap[:, 0:n], D["cst"])], writes=[sb])
        k.op("vector", lambda e: e.tensor_copy(out=self.cst[:], in_=sap[:, 0:n]), reads=[sb], writes=[self.cb])
        k.dma("sync", [(self.small[:], D["small"])], writes=[self.smallb])
        c = self.cst
        self.tri = c[:, 0:128]
        self.ident = c[:, 128:256]
        self.nident = c[:, 256:384]
        self.ones = c[:, 384:512]
        self.masks = [c[:, 512 + j * 512:512 + (j + 1) * 512] for j in range(4)]
        self.swab = [c[:, 2560 + j * 512:2560 + (j + 1) * 512] for j in range(2)]
        s = self.small
        self.gq = s[0:64, 0:1]
        self.gk = s[0:64, 1:2]
        self.one = s[:, 2:3]
        self.nshift = s[:, 3:4]
        self.eps = s[:, 4:5]
        self.ln8 = s[:, 6:7]
        self.zero = s[:, 7:8]
        self.sinks = s[0:64, 8:520]
        k.op("scalar", lambda e: e.activation(out=self.sinks, in_=self.sinks, func=AF.Exp, bias=self.nshift[0:64, :]),
             reads=[self.smallb], writes=[self.smallb])

    def load_sb_head(self, h):
        k, D = self.k, self.D
        qb, qap = self.qT.next()
        kb_, kap = self.kT.next()
        vb, vap = self.v.next()
        sb, sap = self.stage.next()
        k.dma("sync", [(sap[0:64, :], D["sqT"][h * 64:(h + 1) * 64, :])], writes=[sb])
        k.op("scalar", lambda e: e.activation(out=qap, in_=sap[0:64, :], func=AF.Copy, scale=0.125), reads=[sb], writes=[qb])
        sb, sap = self.stage.next()
        k.dma("sync", [(sap[0:64, :], D["skT"][h * 64:(h + 1) * 64, :])], writes=[sb])
        k.op("vector", lambda e: e.tensor_copy(out=kap, in_=sap[0:64, :]), reads=[sb], writes=[kb_])
        sb, sap = self.stage.next()
        k.dma("sync", [(sap[:, 0:NBLK * 64].rearrange("p (b d) -> p b d", d=64), D["sv"][h])], writes=[sb])
        k.op("gpsimd", lambda e: e.tensor_copy(out=vap, in_=sap[:, 0:NBLK * 64]), reads=[sb], writes=[vb])
        return (qb, qap, kb_, kap, vb, vap)

    def sb_stream(self, s, h, tiles):
        k, D = self.k, self.D
        (qb, qap, kb_, kap, vb, vap) = tiles
        cb = self.cb
        pzb, pz = self.psb[2 * s]
        pob, po = self.psb[2 * s + 1]
        for qs in range(8):
            q_sl = slice(qs * 512, (qs + 1) * 512)
            lsum = None
            kbs = list(range(4 * qs + 3, -1, -1))
            for idx, kb in enumerate(kbs):
                k_sl = slice(kb * 128, (kb + 1) * 128)
                j = kb - 4 * qs
                diag = j >= 0
                k.op("tensor", lambda e: e.matmul(pz, lhsT=kap[:, k_sl], rhs=qap[:, q_sl], start=True, stop=not diag),
                     reads=[kb_, qb], writes=[pzb])
                if diag:
                    k.op("tensor", lambda e: e.matmul(pz, lhsT=self.nident, rhs=self.masks[j], start=False, stop=True),
                         reads=[cb], writes=[pzb])
                yield
                eb, eap = self.e32[s].next()
                k.op("scalar", lambda e: e.activation(out=eap, in_=pz, func=AF.Exp), reads=[pzb], writes=[eb])
                yield
                spb, spap = self.sp[s].next()
                k.op("scalar", lambda e: e.activation(out=spap, in_=eap, func=AF.Ln, bias=self.one),
                     reads=[eb, self.smallb], writes=[spb])
                yield
                k.op("tensor", lambda e: e.matmul(pz, lhsT=self.tri, rhs=spap, start=True, stop=(lsum is None)),
                     reads=[cb, spb], writes=[pzb])
                if lsum is not None:
                    k.op("tensor", lambda e: e.matmul(pz, lhsT=self.ones, rhs=lsum[1], start=False, stop=True),
                         reads=[cb, lsum[0]], writes=[pzb])
                yield
                k.op("scalar", lambda e: e.activation(out=pz, in_=pz, func=AF.Exp, scale=-1.0), reads=[pzb], writes=[pzb])
                yield
                wb, wap = self.w[s].next()
                k.op("vector", lambda e: e.tensor_tensor(out=wap, in0=pz, in1=eap, op=ALU.mult), reads=[pzb, eb], writes=[wb])
                if idx < len(kbs) - 1:
                    if lsum is None:
                        lsum = (spb, spap)
                    else:
                        lb, lap = self.ls[s].next()
                        k.op("vector", lambda e: e.tensor_tensor(out=lap, in0=lsum[1], in1=spap, op=ALU.add),
                             reads=[lsum[0], spb], writes=[lb])
                        lsum = (lb, lap)
                yield
                k.op("tensor", lambda e: e.matmul(po[0:64, :], lhsT=vap[:, kb * 64:(kb + 1) * 64], rhs=wap,
                                                  start=(idx == 0), stop=(idx == len(kbs) - 1)),
                     reads=[vb, wb], writes=[pob])
                yield
            ob, oap = self.ost.next()
            k.op("vector", lambda e: e.tensor_copy(out=oap, in_=po[0:64, :]), reads=[pob], writes=[ob])
            if callable(D["oT"]):
                dst = D["oT"](h * 64, (h + 1) * 64)[:, q_sl]
            else:
                dst = D["oT"][h * 64:(h + 1) * 64, q_sl]
            k.dma("sync", [(dst, oap)], reads=[ob], final=self.final)
            yield

    def swa(self):
        k, D = self.k, self.D
        cb = self.cb
        alloc = self.alloc
        qslots = [self.qT.next() for _ in range(4)]
        knb, kn = self.kT.next()
        sq = RR([(k.buf("ssq%d" % i), alloc("ssq%d" % i, [64, 512], BF16)[:, :]) for i in range(2)])
        t32 = RR([(k.buf("st32_%d" % i), alloc("st32_%d" % i, [64, 512], F32)[:, :]) for i in range(3)])
        vb, vap = self.v.next()
        sb, sap = self.stage.next()
        k.dma("sync", [(sap[:, 0:NBLK * 64].rearrange("p (b d) -> p b d", d=64), D["bv"])], writes=[sb])
        k.op("gpsimd", lambda e: e.tensor_copy(out=vap, in_=sap[:, 0:NBLK * 64]), reads=[sb], writes=[vb])
        pnorm = RR([self.psb[6], self.psb[7]])
        pz_rr = RR([self.psb[0], self.psb[1]])
        po_rr = RR([self.psb[2], self.psb[3]])
        pd_rr = RR([self.psb[4], self.psb[5]])

        def qknorm(src_dram, dst_ap, dst_buf, gain, lnbias):
            sb, sap = self.stage.next()
            k.dma("sync", [(sap[0:64, :], src_dram)], writes=[sb])
            for tt in range(8):
                ts = slice(tt * 512, (tt + 1) * 512)
                qb_, qap_ = sq.next()
                k.op("scalar", lambda e: e.activation(out=qap_, in_=sap[0:64, ts], func=AF.Square), reads=[sb], writes=[qb_])
                pb, pap = pnorm.next()
                k.op("tensor", lambda e: e.matmul(pap[0:64, :], lhsT=self.ones[0:64, 0:64], rhs=qap_, start=True, stop=True),
                     reads=[qb_, cb], writes=[pb])
                tb, tap = t32.next()
                k.op("scalar", lambda e: e.activation(out=tap, in_=pap[0:64, :], func=AF.Ln, scale=1.0 / 64.0, bias=self.eps[0:64, :]),
                     reads=[pb, self.smallb], writes=[tb])
                k.op("scalar", lambda e: e.activation(out=tap, in_=tap, func=AF.Exp, scale=-0.5, bias=lnbias[0:64, :]),
                     reads=[tb, self.smallb], writes=[tb])
                k.op("vector", lambda e: e.scalar_tensor_tensor(out=dst_ap[:, ts], in0=sap[0:64, ts], scalar=gain, in1=tap,
                                                                op0=ALU.mult, op1=ALU.mult),
                     reads=[sb, tb, self.smallb], writes=[dst_buf])

        qknorm(D["bkT"], kn, knb, self.gk, self.zero)
        for hl in range(4):
            qknorm(D["bqT"][hl * 64:(hl + 1) * 64, :], qslots[hl][1], qslots[hl][0], self.gq, self.ln8)
        qnb = [qslots[hl][0] for hl in range(4)]

        pw = RR([(k.buf("pw%d" % i), alloc("pw%d" % i, [P, 512], BF16)[:, :]) for i in range(3)])
        ost = RR([(k.buf("so%d" % i), alloc("so%d" % i, [64, 4, 512], F32)) for i in range(2)])
        den = RR([(k.buf("dn%d" % i), alloc("dn%d" % i, [64, 512], F32)[:, :]) for i in range(2)])
        for qg in range(8):
            osb, osap = ost.next()
            for qi in range(4):
                qb = qg * 4 + qi
                q_sl = slice(qb * 128, (qb + 1) * 128)
                pob, po = po_rr.next()
                pdb, pd = pd_rr.next()
                kbl = [qb] if qb == 0 else [qb - 1, qb]
                for ii, kb in enumerate(kbl):
                    k_sl = slice(kb * 128, (kb + 1) * 128)
                    which = 1 if kb == qb else 0
                    pzb, pz = pz_rr.next()
                    for hl in range(4):
                        k.op("tensor", lambda e: e.matmul(pz[:, hl * 128:(hl + 1) * 128], lhsT=kn[:, k_sl], rhs=qslots[hl][1][:, q_sl],
                                                          start=(hl == 0), stop=False, skip_group_check=True),
                             reads=[knb, qnb[hl]], writes=[pzb])
                    k.op("tensor", lambda e: e.matmul(pz, lhsT=self.ident, rhs=self.swab[which], start=False, stop=True,
                                                      skip_group_check=True),
                         reads=[cb], writes=[pzb])
                    wb, wap = pw.next()
                    k.op("scalar", lambda e: e.activation(out=wap, in_=pz, func=AF.Exp, bias=self.nshift),
                         reads=[pzb, self.smallb], writes=[wb])
                    k.op("tensor", lambda e: e.matmul(po[0:64, :], lhsT=vap[:, kb * 64:(kb + 1) * 64], rhs=wap,
                                                      start=(ii == 0), stop=(ii == len(kbl) - 1)),
                         reads=[vb, wb], writes=[pob])
                    k.op("tensor", lambda e: e.matmul(pd[0:64, :], lhsT=self.ones[:, 0:64], rhs=wap,
                                                      start=(ii == 0), stop=(ii == len(kbl) - 1)),
                         reads=[cb, wb], writes=[pdb])
                db, dap = den.next()
                k.op("vector", lambda e: e.tensor_tensor(out=dap, in0=pd[0:64, :], in1=self.sinks, op=ALU.add),
                     reads=[pdb, self.smallb], writes=[db])
                k.op("vector", lambda e: e.reciprocal(out=dap, in_=dap), reads=[db], writes=[db])
                k.op("vector", lambda e: e.tensor_tensor(out=osap[:, :, qi * 128:(qi + 1) * 128],
                                                         in0=po[0:64, :].rearrange("p (h q) -> p h q", h=4),
                                                         in1=dap.rearrange("p (h q) -> p h q", h=4), op=ALU.mult),
                     reads=[pob, db], writes=[osb])
            if callable(D["oT"]):
                pairs = [(D["oT"]((4 + hl) * 64, (5 + hl) * 64)[:, qg * 512:(qg + 1) * 512], osap[:, hl, :]) for hl in range(4)]
            else:
                pairs = [(D["oT"][(4 + hl) * 64:(5 + hl) * 64, qg * 512:(qg + 1) * 512], osap[:, hl, :]) for hl in range(4)]
            k.dma("sync", pairs, reads=[osb], final=self.final)

    def emit(self, finish=True):
        self.consts()
        tiles = [self.load_sb_head(h) for h in range(4)]
        streams = [self.sb_stream(s, s, tiles[s]) for s in range(4)]
        while streams:
            for g in list(streams):
                try:
                    next(g)
                except StopIteration:
                    streams.remove(g)
        self.swa()
        if finish:
            self.k.finish()


def attn_consts(half):
    n = 128 * 4 + 4 * 512 + 2 * 512
    c = np.zeros((128, n), np.float32)
    kk = np.arange(128)[:, None]
    qq = np.arange(128)[None, :]
    c[:, 0:128] = (kk >= qq)
    c[:, 128:256] = np.eye(128)
    c[:, 256:384] = -np.eye(128)
    c[:, 384:512] = 1.0
    ql = np.arange(512)[None, :]
    for j in range(4):
        c[:, 512 + j * 512:512 + (j + 1) * 512] = np.where(j * 128 + kk >= ql, BIG, 0.0)
    for hl in range(4):
        slope = 2.0 ** (-(4 * half + hl + 1))
        dist_prev = qq + 128 - kk
        dist_cur = qq - kk
        c[:, 2560 + hl * 128:2560 + (hl + 1) * 128] = np.where(kk > qq, -slope * dist_prev, -BIG)
        c[:, 3072 + hl * 128:3072 + (hl + 1) * 128] = np.where(kk <= qq, -slope * dist_cur, -BIG)
    return c
GC = 128
NCH = 32
GBIG = 30000.0
LN_QSCALE = -0.5 * 4.852030263919617


def run_streams(streams):
    streams = list(streams)
    while streams:
        for s in list(streams):
            try:
                next(s)
            except StopIteration:
                streams.remove(s)


class GdnProg:
    def __init__(self, nc, es, D, safe_same=True, inv_fp32=True, k=None, pfx="", fused=False):
        self.nc = nc
        self.es = es
        self.fused = fused
        k = self.k = k if k is not None else K(nc, es, safe_same=safe_same)
        self.D = D
        self.inv_fp32 = inv_fp32

        def alloc(name, shape, dt):
            return es.enter_context(nc.sbuf_tensor(pfx + "sg_" + name, shape, dt))

        def rr(name, shape, dt, n):
            t = alloc(name, [shape[0], n] + list(shape[1:]), dt)
            return RR([(k.buf("%s%d" % (name, i)), t[:, i]) for i in range(n)])

        self.alloc = alloc
        self.rr = rr
        self.c32 = alloc("c32", [P, 6 * 128], F32)
        self.cb = k.buf("c32")
        self.identb = alloc("identb", [P, P], BF16)
        self.onesb = alloc("onesb", [P, P], BF16)
        self.small = alloc("small", [P, 8], F32)
        self.convw = alloc("convw", [P, 64], F32)
        self.diagw = alloc("diagw", [P, 64, P], BF16)
        self.gt = alloc("gates", [P, 6, 256], F32)
        self.gb = k.buf("gates")
        self.xst = rr("xst", [P, 515], F32, 2)
        self.xb = rr("xb", [P, 515], BF16, 2)
        qT = alloc("qT", [P, 2, 4, 512], BF16)
        kT = alloc("kT", [P, 2, 4, 512], BF16)
        vt = alloc("vt", [P, 2, 4, 8, P], BF16)
        self.qT, self.kT, self.vt = qT, kT, vt
        self.qTb = [[k.buf("qT%d_%d" % (s, m)) for m in range(4)] for s in range(2)]
        self.kTb = [[k.buf("kT%d_%d" % (s, m)) for m in range(4)] for s in range(2)]
        self.vtb = [[k.buf("vt%d_%d" % (s, h)) for h in range(8)] for s in range(2)]
        self.e32 = rr("e32", [P, 512], F32, 3)
        self.y32 = rr("y32", [P, 512], F32, 2)
        self.yb = rr("yb", [P, 512], BF16, 2)
        self.sq = rr("sq", [P, 512], BF16, 2)
        self.r32 = rr("r32", [P, 512], F32, 2)
        self.egc = rr("egc", [P, 24], F32, 3)
        self.Xt = alloc("Xt", [P, 2, 8, P], BF16)
        self.AT = alloc("AT", [P, 2, 8, P], BF16)
        self.kh = alloc("kh", [P, 2, 8, P], BF16)
        self.Xtb = [[k.buf("Xt%d_%d" % (s, h)) for h in range(8)] for s in range(2)]
        self.ATb = [[k.buf("AT%d_%d" % (s, h)) for h in range(8)] for s in range(2)]
        self.khb = [[k.buf("kh%d_%d" % (s, h)) for h in range(8)] for s in range(2)]
        self.f32t = rr("f32t", [P, P], F32, 48)
        self.S = alloc("S", [P, 8, P], F32)
        self.Sbf = alloc("Sbf", [P, 8, P], BF16)
        self.Sb = [k.buf("S%d" % h) for h in range(8)]
        self.Sbfb = [k.buf("Sbf%d" % h) for h in range(8)]
        self.Rb = rr("R", [P, P], BF16, 8)
        self.vn = rr("vn", [P, P], BF16, 8)
        self.tmp = rr("tmp", [P, P], F32, 8)
        self.ost = alloc("ost", [P, 2, 8, P], F32)
        self.ostT = alloc("ostT", [P, 2, 8, P], F32)
        self.ostTb = [[k.buf("ostT%d_%d" % (s, h)) for h in range(8)] for s in range(2)]
        self.ostb = [[k.buf("ost%d_%d" % (s, h)) for h in range(8)] for s in range(2)]
        pc = [es.enter_context(nc.psum_tensor(pfx + "pgc%d" % i, [P, 512], F32)) for i in range(1)]
        self.pc = RR([(k.buf("pgc%d" % i), pc[i][:, :]) for i in range(1)])
        pb = es.enter_context(nc.psum_tensor(pfx + "pgb", [P, 1024], BF16))
        pbb = k.buf("pgb")
        pbb.excl = True
        self.pvt = (pbb, pb[:, 0:512])
        self.pkt = RR([(pbb, pb[:, 512 + i * 128:512 + (i + 1) * 128]) for i in range(3)])
        self.fence_ap = pb[0:1, 896:1024]
        self.pbb = pbb
        pq = [es.enter_context(nc.psum_tensor(pfx + "pgq%d" % i, [P, 512], F32)) for i in range(6)]
        bq = [k.buf("pgq%d" % i) for i in range(6)]
        for b in bq + [self.pc.items[0][0]]:
            b.excl = True
        self.pqb = RR([(bq[i], pq[i]) for i in range(3)])
        self.psqb = RR([(bq[i], pq[i]) for i in range(3, 6)])

    def fence(self, banks):
        self.k.op("tensor", lambda e: e.transpose(self.fence_ap, self.identb[:, 0:1], self.identb[:]),
                  reads=[self.cb], writes=[self.pbb] + list(banks))

    def phase0(self):
        k, D = self.k, self.D
        k.dma("sync", [(self.c32[:], D["c32"])], writes=[self.cb])
        c = self.c32
        self.LE = c[:, 0:128]
        self.GT = c[:, 128:256]
        self.MASKB = c[:, 256:384]
        self.I32 = c[:, 384:512]
        self.ONES32 = c[:, 512:640]
        self.STRICT = c[:, 640:768]
        k.op("vector", lambda e: e.tensor_copy(out=self.identb[:], in_=self.I32), reads=[self.cb], writes=[self.cb])
        k.op("vector", lambda e: e.tensor_copy(out=self.onesb[:], in_=self.ONES32), reads=[self.cb], writes=[self.cb])
        k.dma("sync", [(self.small[:], D["small"]), (self.convw[:], D["convw"])], writes=[self.cb])
        self.one = self.small[:, 0:1]
        self.eps = self.small[:, 1:2]
        self.lnq = self.small[:, 2:3]
        self.zero = self.small[:, 3:4]
        for i in range(64):
            k.op("gpsimd", lambda e: e.tensor_scalar(out=self.diagw[:, i, :], in0=self.identb[:], scalar1=self.convw[:, i:i + 1],
                                                     scalar2=None, op0=ALU.mult), reads=[self.cb], writes=[self.cb])
        g = self.gt
        if self.fused:
            gt = D["gtok"]
            k.dma("sync", [(g[:, 0, :].rearrange("p (c h) -> p c h", h=8), gt[:, 0:8].rearrange("(c p) h -> p c h", p=P)),
                           (g[:, 1, :].rearrange("p (c h) -> p c h", h=8), gt[:, 8:16].rearrange("(c p) h -> p c h", p=P)),
                           (g[:, 2:4, :], D["gconst"])], writes=[self.gb])
        else:
            k.dma("sync", [(g[:, 0:4, :], D["gates"])], writes=[self.gb])
        A, BL, DTB, ALOG, G, BETA = (g[:, i, :] for i in range(6))
        gb = [self.gb]
        k.op("vector", lambda e: e.tensor_tensor(out=A, in0=A, in1=DTB, op=ALU.add), reads=gb, writes=gb)
        k.op("scalar", lambda e: e.activation(out=A, in_=A, func=AF.Exp), reads=gb, writes=gb)
        k.op("scalar", lambda e: e.activation(out=A, in_=A, func=AF.Ln, bias=self.one), reads=gb + [self.cb], writes=gb)
        k.op("scalar", lambda e: e.activation(out=ALOG, in_=ALOG, func=AF.Exp), reads=gb, writes=gb)
        k.op("vector", lambda e: e.scalar_tensor_tensor(out=G, in0=A, scalar=-1.0, in1=ALOG, op0=ALU.mult, op1=ALU.mult),
             reads=gb, writes=gb)
        k.op("scalar", lambda e: e.activation(out=BL, in_=BL, func=AF.Exp, scale=-1.0), reads=gb, writes=gb)
        k.op("vector", lambda e: e.tensor_scalar(out=BL, in0=BL, scalar1=1.0, scalar2=None, op0=ALU.add), reads=gb, writes=gb)
        k.op("vector", lambda e: e.reciprocal(out=BETA, in_=BL), reads=gb, writes=gb)
        self.G, self.BETA = G, BETA
        k.op("vector", lambda e: e.memset(self.S[:], 0.0), writes=self.Sb)
        k.op("vector", lambda e: e.memset(self.Sbf[:], 0.0), writes=self.Sbfb)

    def prologue_rows(self, t, rows):
        k, D = self.k, self.D
        slot = t % 2
        cb = self.cb
        for r in rows:
            sb, sap = self.xst.next()
            if not self.fused:
                k.dma("sync", [(sap, D["xT"][r * P:(r + 1) * P, t * 512:t * 512 + 515])], writes=[sb])
            elif t == 0:
                k.op("gpsimd", lambda e: e.memset(sap[:, 0:3], 0.0), writes=[sb])
                k.dma("sync", [(sap[:, 3:515], D["xT"][r * P:(r + 1) * P, 0:512])], writes=[sb])
            else:
                k.dma("sync", [(sap, D["xT"][r * P:(r + 1) * P, t * 512 - 3:t * 512 + 512])], writes=[sb])
            xbb, xbap = self.xb.next()
            k.op("gpsimd", lambda e: e.tensor_copy(out=xbap, in_=sap), reads=[sb], writes=[xbb])
            yield
            pcb, pcap = self.pc.next()
            for j in range(4):
                k.op("tensor", lambda e: e.matmul(pcap, lhsT=self.diagw[:, r * 4 + j, :], rhs=xbap[:, j:j + 512],
                                                  start=(j == 0), stop=(j == 3)), reads=[cb, xbb], writes=[pcb])
            yield
            eb, eap = self.e32.next()
            k.op("scalar", lambda e: e.activation(out=eap, in_=pcap, func=AF.Exp, scale=-1.0), reads=[pcb], writes=[eb])
            yield
            k.op("scalar", lambda e: e.activation(out=eap, in_=eap, func=AF.Ln, bias=self.one), reads=[eb, cb], writes=[eb])
            yield
            k.op("scalar", lambda e: e.activation(out=eap, in_=eap, func=AF.Exp, scale=-1.0), reads=[eb], writes=[eb])
            yield
            if r < 8:
                yb_, yap = self.y32.next()
            else:
                yb_, yap = self.yb.next()
            k.op("vector", lambda e: e.tensor_tensor(out=yap, in0=pcap, in1=eap, op=ALU.mult), reads=[pcb, eb], writes=[yb_])
            yield
            if r < 8:
                m = r % 4
                qb_, qap_ = self.sq.next()
                k.op("scalar", lambda e: e.activation(out=qap_, in_=yap, func=AF.Square), reads=[yb_], writes=[qb_])
                yield
                pnb, pnap = self.pc.next()
                k.op("tensor", lambda e: e.matmul(pnap, lhsT=self.onesb[:], rhs=qap_, start=True, stop=True),
                     reads=[qb_, cb], writes=[pnb])
                yield
                rb, rap = self.r32.next()
                k.op("scalar", lambda e: e.activation(out=rap, in_=pnap, func=AF.Ln, bias=self.eps), reads=[pnb, cb], writes=[rb])
                yield
                bias = self.lnq if r < 4 else self.zero
                k.op("scalar", lambda e: e.activation(out=rap, in_=rap, func=AF.Exp, scale=-0.5, bias=bias), reads=[rb, cb], writes=[rb])
                yield
                if r < 4:
                    dst, dbuf = self.qT[:, slot, m, :], self.qTb[slot][m]
                else:
                    dst, dbuf = self.kT[:, slot, m, :], self.kTb[slot][m]
                k.op("vector", lambda e: e.tensor_tensor(out=dst, in0=yap, in1=rap, op=ALU.mult), reads=[yb_, rb], writes=[dbuf])
                yield
            else:
                hv = r - 8
                pvb, pvap = self.pvt
                for cc in range(4):
                    k.op("tensor", lambda e: e.transpose(pvap[:, cc * P:(cc + 1) * P], yap[:, cc * P:(cc + 1) * P], self.identb[:]),
                         reads=[yb_, cb], writes=[pvb])
                yield
                k.op("vector", lambda e: e.tensor_copy(out=self.vt[:, slot, :, hv, :], in_=pvap.rearrange("p (c d) -> p c d", c=4)),
                     reads=[pvb], writes=[self.vtb[slot][hv]])
                yield

    def pre(self, c):
        k = self.k
        cb = self.cb
        t, c4 = c // 4, c % 4
        slot = t % 2
        cs = c % 2
        csl = slice(c4 * P, (c4 + 1) * P)
        egb, egap = self.egc.next()
        self.egcur = getattr(self, "egcur", {})
        self.egcur[c] = (egb, egap)
        pb, pap = self.pqb.next()
        k.op("tensor", lambda e: e.matmul(pap[:, 0:8], lhsT=self.LE, rhs=self.G[:, c * 8:(c + 1) * 8], start=True, stop=True),
             reads=[cb, self.gb], writes=[pb])
        k.op("tensor", lambda e: e.matmul(pap[:, 8:16], lhsT=self.ONES32, rhs=self.G[:, c * 8:(c + 1) * 8], start=True, stop=True),
             reads=[cb, self.gb], writes=[pb])
        self.fence([pb])
        k.op("scalar", lambda e: e.activation(out=egap[:, 0:16], in_=pap[:, 0:16], func=AF.Exp), reads=[pb], writes=[egb])
        k.op("vector", lambda e: e.tensor_scalar(out=egap[:, 16:24], in0=egap[:, 0:8], scalar1=-1.0, scalar2=None, op0=ALU.mult),
             reads=[egb], writes=[egb])
        yield
        for grp in range(2):
            heads = list(range(4 * grp, 4 * grp + 4))
            kheads = [2 * grp, 2 * grp + 1]
            Gm = {}
            for h in heads:
                Gm[h] = self.f32t.next()
                k.op("gpsimd", lambda e: e.tensor_scalar(out=Gm[h][1], in0=self.GT, scalar1=self.G[:, c * 8 + h:c * 8 + h + 1],
                                                         scalar2=None, op0=ALU.mult), reads=[cb, self.gb], writes=[Gm[h][0]])
            yield
            pZ = {}
            bk = self.pqb.next()
            for h in heads:
                pZ[h] = (bk[0], bk[1][:, (h % 4) * P:(h % 4 + 1) * P])
                k.op("tensor", lambda e: e.matmul(pZ[h][1], lhsT=Gm[h][1], rhs=self.LE, start=True, stop=False),
                     reads=[Gm[h][0], cb], writes=[pZ[h][0]])
                k.op("tensor", lambda e: e.matmul(pZ[h][1], lhsT=self.I32, rhs=self.MASKB, start=False, stop=True),
                     reads=[cb], writes=[pZ[h][0]])
            self.fence([bk[0]])
            yield
            Dm = {}
            for h in heads:
                Dm[h] = self.f32t.next()
                k.op("scalar", lambda e: e.activation(out=Dm[h][1], in_=pZ[h][1], func=AF.Exp), reads=[pZ[h][0]], writes=[Dm[h][0]])
            yield
            pKQ, pKK, pkt, Dms = {}, {}, {}, {}
            bk = self.pqb.next()
            for m in kheads:
                kTc = self.kT[:, slot, m, csl]
                qTc = self.qT[:, slot, m, csl]
                pKQ[m] = (bk[0], bk[1][:, (m % 2) * P:(m % 2 + 1) * P])
                k.op("tensor", lambda e: e.matmul(pKQ[m][1], lhsT=kTc, rhs=qTc, start=True, stop=True),
                     reads=[self.kTb[slot][m], self.qTb[slot][m]], writes=[pKQ[m][0]])
                pKK[m] = (bk[0], bk[1][:, (2 + m % 2) * P:(3 + m % 2) * P])
                k.op("tensor", lambda e: e.matmul(pKK[m][1], lhsT=kTc, rhs=kTc, start=True, stop=True),
                     reads=[self.kTb[slot][m]], writes=[pKK[m][0]])
                pkt[m] = self.pkt.next()
                k.op("tensor", lambda e: e.transpose(pkt[m][1], kTc, self.identb[:]), reads=[self.kTb[slot][m], cb], writes=[pkt[m][0]])
            for h in heads:
                Dms[h] = self.f32t.next()
                k.op("gpsimd", lambda e: e.tensor_tensor(out=Dms[h][1], in0=Dm[h][1], in1=self.STRICT, op=ALU.mult),
                     reads=[Dm[h][0], cb], writes=[Dms[h][0]])
            yield
            Pt, Q_, W = {}, {}, {}
            for h in heads:
                m = h // 2
                k.op("vector", lambda e: e.tensor_tensor(out=self.AT[:, cs, h, :], in0=pKQ[m][1], in1=Dm[h][1], op=ALU.mult),
                     reads=[pKQ[m][0], Dm[h][0]], writes=[self.ATb[cs][h]])
                k.op("vector", lambda e: e.tensor_scalar(out=self.kh[:, cs, h, :], in0=pkt[m][1], scalar1=Dm[h][1][:, 127:128],
                                                         scalar2=None, op0=ALU.mult),
                     reads=[pkt[m][0], Dm[h][0]], writes=[self.khb[cs][h]])
                Pt[h] = self.f32t.next()
                k.op("vector", lambda e: e.scalar_tensor_tensor(out=Pt[h][1], in0=pKK[m][1], scalar=self.BETA[:, c * 8 + h:c * 8 + h + 1],
                                                                in1=Dms[h][1], op0=ALU.mult, op1=ALU.mult),
                     reads=[pKK[m][0], self.gb, Dms[h][0]], writes=[Pt[h][0]])
            yield
            pT = {}
            bk = self.pqb.next()
            for h in heads:
                pT[h] = (bk[0], bk[1][:, (h % 4) * P:(h % 4 + 1) * P])
                k.op("tensor", lambda e: e.transpose(pT[h][1], Pt[h][1], self.I32), reads=[Pt[h][0], cb], writes=[pT[h][0]])
            self.fence([bk[0]])
            yield
            for h in heads:
                Q_[h] = self.f32t.next()
                k.op("scalar", lambda e: e.copy(out=Q_[h][1], in_=pT[h][1]), reads=[pT[h][0]], writes=[Q_[h][0]])
                W[h] = self.f32t.next()
                k.op("gpsimd", lambda e: e.tensor_tensor(out=W[h][1], in0=self.I32, in1=Pt[h][1], op=ALU.subtract),
                     reads=[cb, Pt[h][0]], writes=[W[h][0]])
            yield
            pP, pQ = {}, {}
            bkP = self.pqb.next()
            bkQ = self.pqb.next()
            for h in heads:
                pQ[h] = (bkQ[0], bkQ[1][:, (h % 4) * P:(h % 4 + 1) * P])
                k.op("tensor", lambda e: e.matmul(pQ[h][1], lhsT=Pt[h][1], rhs=Q_[h][1], start=True, stop=True),
                     reads=[Q_[h][0], Pt[h][0]], writes=[pQ[h][0]])
            for h in heads:
                pP[h] = (bkP[0], bkP[1][:, (h % 4) * P:(h % 4 + 1) * P])
                k.op("tensor", lambda e: e.matmul(pP[h][1], lhsT=Q_[h][1], rhs=Pt[h][1], start=True, stop=True),
                     reads=[Q_[h][0], Pt[h][0]], writes=[pP[h][0]])
            self.fence([bkP[0], bkQ[0]])
            yield
            nP, nQ = {}, {}
            for h in heads:
                nP[h] = self.f32t.next()
                k.op("scalar", lambda e: e.copy(out=nP[h][1], in_=pP[h][1]), reads=[pP[h][0]], writes=[nP[h][0]])
                nQ[h] = self.f32t.next()
                k.op("vector", lambda e: e.tensor_copy(out=nQ[h][1], in_=pQ[h][1]), reads=[pQ[h][0]], writes=[nQ[h][0]])
            Pt, Q_ = nP, nQ
            yield
            for step in range(1, 7):
                pW, pP, pQ = {}, {}, {}
                bkW = self.pqb.next()
                for h in heads:
                    pW[h] = (bkW[0], bkW[1][:, (h % 4) * P:(h % 4 + 1) * P])
                    k.op("tensor", lambda e: e.matmul(pW[h][1], lhsT=Q_[h][1], rhs=W[h][1], start=True, stop=True),
                         reads=[Q_[h][0], W[h][0]], writes=[pW[h][0]])
                if step <= 5:
                    bkQ = self.pqb.next()
                    for h in heads:
                        pQ[h] = (bkQ[0], bkQ[1][:, (h % 4) * P:(h % 4 + 1) * P])
                        k.op("tensor", lambda e: e.matmul(pQ[h][1], lhsT=Pt[h][1], rhs=Q_[h][1], start=True, stop=True),
                             reads=[Q_[h][0], Pt[h][0]], writes=[pQ[h][0]])
                if step <= 4:
                    bkP = self.pqb.next()
                    for h in heads:
                        pP[h] = (bkP[0], bkP[1][:, (h % 4) * P:(h % 4 + 1) * P])
                        k.op("tensor", lambda e: e.matmul(pP[h][1], lhsT=Q_[h][1], rhs=Pt[h][1], start=True, stop=True),
                             reads=[Q_[h][0], Pt[h][0]], writes=[pP[h][0]])
                self.fence([bkW[0]] + ([bkQ[0]] if step <= 5 else []) + ([bkP[0]] if step <= 4 else []))
                yield
                nW, nP, nQ = {}, {}, {}
                for h in heads:
                    if step < 6:
                        nW[h] = self.f32t.next()
                        k.op("vector", lambda e: e.tensor_tensor(out=nW[h][1], in0=pW[h][1], in1=W[h][1], op=ALU.add),
                             reads=[pW[h][0], W[h][0]], writes=[nW[h][0]])
                    else:
                        k.op("vector", lambda e: e.tensor_tensor(out=self.Xt[:, cs, h, :], in0=pW[h][1], in1=W[h][1], op=ALU.add),
                             reads=[pW[h][0], W[h][0]], writes=[self.Xtb[cs][h]])
                    if step <= 5:
                        nQ[h] = self.f32t.next()
                        k.op("scalar", lambda e: e.copy(out=nQ[h][1], in_=pQ[h][1]), reads=[pQ[h][0]], writes=[nQ[h][0]])
                    if step <= 4:
                        nP[h] = self.f32t.next()
                        k.op("scalar", lambda e: e.copy(out=nP[h][1], in_=pP[h][1]), reads=[pP[h][0]], writes=[nP[h][0]])
                W, Pt, Q_ = nW, nP, nQ
                yield

    def seq(self, c):
        for grp in range(2):
            yield from self.seq_grp(c, grp)

    def seq_grp(self, c, grp):
        k, D = self.k, self.D
        t, c4 = c // 4, c % 4
        slot = t % 2
        cs = c % 2
        csl = slice(c4 * P, (c4 + 1) * P)
        heads = list(range(4 * grp, 4 * grp + 4))
        egb, egap = self.egcur[c]
        p1, pa = {}, {}
        bk1 = self.psqb.next()
        bka = self.psqb.next()
        for h in heads:
            m = h // 2
            p1[h] = (bk1[0], bk1[1][:, (h % 4) * P:(h % 4 + 1) * P])
            k.op("tensor", lambda e: e.matmul(p1[h][1], lhsT=self.kT[:, slot, m, csl], rhs=self.Sbf[:, h, :], start=True, stop=True),
                 reads=[self.kTb[slot][m], self.Sbfb[h]], writes=[p1[h][0]])
            pa[h] = (bka[0], bka[1][:, (h % 4) * P:(h % 4 + 1) * P])
            k.op("tensor", lambda e: e.matmul(pa[h][1], lhsT=self.qT[:, slot, m, csl], rhs=self.Sbf[:, h, :], start=True, stop=True),
                 reads=[self.qTb[slot][m], self.Sbfb[h]], writes=[pa[h][0]])
        yield
        R, tmp = {}, {}
        for h in heads:
            R[h] = self.Rb.next()
            k.op("vector", lambda e: e.scalar_tensor_tensor(out=R[h][1], in0=p1[h][1], scalar=egap[:, 16 + h:17 + h],
                                                            in1=self.vt[:, slot, c4, h, :], op0=ALU.mult, op1=ALU.add),
                 reads=[p1[h][0], egb, self.vtb[slot][h]], writes=[R[h][0]])
            tmp[h] = self.tmp.next()
            k.op("scalar", lambda e: e.activation(out=tmp[h][1], in_=pa[h][1], func=AF.Copy, scale=egap[:, h:h + 1]),
                 reads=[pa[h][0], egb], writes=[tmp[h][0]])
        yield
        p2 = {}
        bk2 = self.psqb.next()
        for h in heads:
            p2[h] = (bk2[0], bk2[1][:, (h % 4) * P:(h % 4 + 1) * P])
            k.op("tensor", lambda e: e.matmul(p2[h][1], lhsT=self.Xt[:, cs, h, :], rhs=R[h][1], start=True, stop=True),
                 reads=[self.Xtb[cs][h], R[h][0]], writes=[p2[h][0]])
        yield
        vn = {}
        for h in heads:
            vn[h] = self.vn.next()
            k.op("scalar", lambda e: e.activation(out=vn[h][1], in_=p2[h][1], func=AF.Copy, scale=self.BETA[:, c * 8 + h:c * 8 + h + 1]),
                 reads=[p2[h][0], self.gb], writes=[vn[h][0]])
        yield
        pb_, p3 = {}, {}
        bkb = self.psqb.next()
        bk3 = self.psqb.next()
        for h in heads:
            pb_[h] = (bkb[0], bkb[1][:, (h % 4) * P:(h % 4 + 1) * P])
            k.op("tensor", lambda e: e.matmul(pb_[h][1], lhsT=self.AT[:, cs, h, :], rhs=vn[h][1], start=True, stop=True),
                 reads=[self.ATb[cs][h], vn[h][0]], writes=[pb_[h][0]])
            p3[h] = (bk3[0], bk3[1][:, (h % 4) * P:(h % 4 + 1) * P])
            k.op("tensor", lambda e: e.matmul(p3[h][1], lhsT=self.kh[:, cs, h, :], rhs=vn[h][1], start=True, stop=True),
                 reads=[self.khb[cs][h], vn[h][0]], writes=[p3[h][0]])
        yield
        for h in heads:
            k.op("vector", lambda e: e.tensor_tensor(out=self.ost[:, cs, h, :], in0=pb_[h][1], in1=tmp[h][1], op=ALU.add),
                 reads=[pb_[h][0], tmp[h][0]], writes=[self.ostb[cs][h]])
            k.op("vector", lambda e: e.scalar_tensor_tensor(out=self.S[:, h, :], in0=self.S[:, h, :], scalar=egap[:, 8 + h:9 + h],
                                                            in1=p3[h][1], op0=ALU.mult, op1=ALU.add),
                 reads=[self.Sb[h], egb, p3[h][0]], writes=[self.Sb[h]])
        yield
        for h in heads:
            k.op("gpsimd", lambda e: e.tensor_copy(out=self.Sbf[:, h, :], in_=self.S[:, h, :]), reads=[self.Sb[h]], writes=[self.Sbfb[h]])
        if not self.fused:
            if grp == 1:
                k.dma("sync", [(D["o_tok"][c * P:(c + 1) * P, :], self.ost[:, cs, :, :].rearrange("p h d -> p (h d)"))],
                      reads=self.ostb[cs], final=True)
            yield
            return
        bkt = self.psqb.next()
        for h in heads:
            k.op("tensor", lambda e: e.transpose(bkt[1][:, (h % 4) * P:(h % 4 + 1) * P], self.ost[:, cs, h, :], self.I32),
                 reads=[self.ostb[cs][h], self.cb], writes=[bkt[0]])
        self.fence([bkt[0]])
        yield
        for h in heads:
            k.op("scalar", lambda e: e.copy(out=self.ostT[:, cs, h, :], in_=bkt[1][:, (h % 4) * P:(h % 4 + 1) * P]),
                 reads=[bkt[0]], writes=[self.ostTb[cs][h]])
        if grp == 1:
            k.dma("sync", [(D["oT"](h * P, (h + 1) * P)[:, c * P:(c + 1) * P], self.ostT[:, cs, h, :]) for h in range(8)],
                  reads=self.ostTb[cs])
        yield

    def emit(self, nchunks=NCH, stop=None, finish=True):
        self.phase0()
        if stop == "phase0":
            self.k.finish(); return
        run_streams([self.prologue_rows(0, range(16) if stop != "pro1" else [0, 8])])
        if stop in ("pro", "pro1"):
            self.k.finish(); return
        if stop is not None and stop.startswith("pre"):
            n = int(stop[3:] or 1000)
            g = self.pre(0)
            for _ in range(n):
                try:
                    next(g)
                except StopIteration:
                    break
            self.k.finish(); return
        run_streams([self.pre(0)])
        pro = []
        ntiles = (nchunks + 3) // 4
        for c in range(nchunks):
            t = c // 4
            if c % 4 == 0 and t + 1 < ntiles:
                pro = [self.prologue_rows(t + 1, range(16))]
            main = [self.seq(c)]
            if c + 1 < nchunks and (c % 4 != 3):
                main.append(self.pre(c + 1))
            streams = list(main)
            while streams:
                for s_ in list(streams):
                    try:
                        next(s_)
                    except StopIteration:
                        streams.remove(s_)
                for s_ in list(pro):
                    try:
                        next(s_)
                        next(s_)
                    except StopIteration:
                        pro.remove(s_)
            if c % 4 == 3 and c + 1 < nchunks:
                run_streams(pro)
                pro = []
                run_streams([self.pre(c + 1)])
        if finish:
            self.k.finish()


def gdn_consts():
    c = np.zeros((128, 768), np.float32)
    p = np.arange(128)[:, None]
    i = np.arange(128)[None, :]
    c[:, 0:128] = (p <= i)
    c[:, 128:256] = (p > i)
    c[:, 256:384] = np.where(i < p, -GBIG, 0.0)
    c[:, 384:512] = np.eye(128)
    c[:, 512:640] = 1.0
    c[:, 640:768] = (p < i)
    return c
from concourse.bass_utils import run_bass_kernel_spmd

NCORES = 8
_DBG = {}
TOKC = 2048
PAIRS = [[0, 1], [2, 3], [4, 5], [6, 7]]


def _dram(nc, name, shape, kind="ExternalInput", dt=None):
    return nc.dram_tensor(name, list(shape), dt or F32, kind=kind).ap()


def _scratch(nc, name, shape, dt=None):
    return nc.dram_tensor(name, list(shape), dt or F32)


class Chunked:
    def __init__(self, nc, name, rows, cols, rc, dt=None, gathered=True):
        self.rc, self.rows, self.cols = rc, rows, cols
        self.n = rows // rc
        self.src = [nc.dram_tensor("%s_%d" % (name, j), [rc, cols], dt or F32) for j in range(self.n)]
        self.dst = [nc.dram_tensor("G%s_%d" % (name, j), [2 * rc, cols], dt or F32) for j in range(self.n)] if gathered else []
        self.gb = [Buf("G%s_%d" % (name, j)) for j in range(self.n)]

    def set_events(self, events):
        for b, ev in zip(self.gb, events):
            b.lastw = ev
            b.reads = {}

    def gbuf(self, r0):
        return self.gb[r0 // self.rc]

    def own(self, r0, r1):
        j = r0 // self.rc
        assert (r1 - 1) // self.rc == j
        return self.src[j].ap()[r0 - j * self.rc:r1 - j * self.rc, :]

    def gat(self, rank, r0, r1):
        j = r0 // self.rc
        assert (r1 - 1) // self.rc == j
        return self.dst[j].ap()[rank * self.rc + r0 - j * self.rc:rank * self.rc + r1 - j * self.rc, :]

    def gathers(self):
        return [(lambda g, s=s, d=d: g.collective_compute("AllGather", ALU.bypass, replica_groups=PAIRS,
                                                         ins=[s.ap().opt()], outs=[d.ap().opt()]))
                for s, d in zip(self.src, self.dst)]


def _gain_layout(vecs, extra=None):
    g = np.stack(vecs).astype(np.float32).reshape(len(vecs), 8, 128).transpose(2, 0, 1).reshape(128, len(vecs) * 8)
    col = np.zeros((128, 1), np.float32) if extra is None else np.asarray(extra, np.float32).reshape(128, 1)
    return np.ascontiguousarray(np.concatenate([g, col], 1))


def build_fused(stop=None, dump=None):
    nc = bass.Bass("TRN2", target_bir_lowering=False)
    I = {}
    def inp(name, shape):
        I[name] = _dram(nc, name, shape)
        return I[name]
    xT = inp("xT", [1024, TOKC])
    sel = inp("sel", [128, 2])
    gA = inp("gA", [128, 17]); gB = inp("gB", [128, 33]); gC = inp("gC", [128, 17])
    wg = [inp("wg%d" % i, [1024, 2816]) for i in range(4)]
    wu = [inp("wu%d" % i, [1024, 2816]) for i in range(4)]
    wd = [inp("wd%d" % i, [2816, 1024]) for i in range(4)]
    w_att_in = inp("w_att_in", [1024, 1152])
    w_att_out = inp("w_att_out", [1024, 1024])
    w_gdn_in = inp("w_gdn_in", [1024, 2064])
    w_gdn_z = inp("w_gdn_z", [1024, 2048])
    w_gdn_out = inp("w_gdn_out", [2048, 1024])
    wpg = [inp("wpg%d" % i, [1024, 1024]) for i in range(2)]
    wpp = [inp("wpp%d" % i, [256, 1024]) for i in range(2)]
    pT = [inp("pT%d" % i, [256, TOKC]) for i in range(2)]
    a_cst = inp("a_cst", [128, 3584]); a_small = inp("a_small", [128, 520])
    g_c32 = inp("g_c32", [128, 768]); g_small = inp("g_small", [128, 8]); g_convw = inp("g_convw", [128, 64])
    g_gconst = inp("g_gconst", [128, 2, 256])
    outT = _dram(nc, "outT", [1024, TOKC], "ExternalOutput")
    hsp = _scratch(nc, "hsp", [1024, TOKC])
    hnA = Chunked(nc, "hnA", 1024, TOKC, 512, BF16)
    projA = _scratch(nc, "projA", [832, 4096]); vtokA = _scratch(nc, "vtokA", [4096, 320])
    oA = Chunked(nc, "oA", 512, 4096, 128)
    zB = _scratch(nc, "zB", [2048, TOKC])
    hnB = Chunked(nc, "hnB", 1024, TOKC, 512, BF16)
    projB = _scratch(nc, "projB", [2048, 4096]); gtokB = _scratch(nc, "gtokB", [4096, 16])
    oB = Chunked(nc, "oB", 1024, 4096, 128)
    SCR = dict(hsp=hsp, projA=projA, vtokA=vtokA, zB=zB, projB=projB, gtokB=gtokB,
               GhnA0=hnA.dst[0], GoA0=oA.dst[0], GoB0=oB.dst[0], oB0=oB.src[0], oA0=oA.src[0])

    dumps = {}

    def maybe_stop(k, name):
        if stop != name:
            return False
        if dump:
            src = SCR[dump]
            o = nc.dram_tensor("dbg", list(src.shape), src.dtype, kind="ExternalOutput").ap()
            b = k.buf("dbg")
            k.dma("sync", [(o, src.ap())], reads=[b], final=True)
        k.finish()
        return True

    with ExitStack() as es:
        k = K(nc, es)
        with ExitStack() as pes:
            rp = RowProg(nc, pes, TOKC, 2, k=k, pfx="p1")
            rp.load_gains(gA)
            rp.load_h(xT)
            rp.ffn(0, wg[0], wu[0], wd[0])
            rp.hn_to_dram(1, hnA.own)
            rp.store_h(hsp.ap(), final=False)
            rp.emit(finish=False)
            hnA.set_events(k.barrier(hnA.gathers()))
        if maybe_stop(k, "p1") or maybe_stop(k, "p1nocc"):
            return nc
        with ExitStack() as pes:
            rp = RowProg(nc, pes, TOKC, 2, k=k, pfx="p2")
            for r in range(2):
                rp.hn_from_dram(lambda r0, r1, r=r: (hnA.gat(r, r0, r1), hnA.gbuf(r0)))
                rp.proj_fm(w_att_in, 0, 832, projA.ap(), 0, r * TOKC)
                rp.proj_tm(w_att_in, 832, 256, vtokA.ap(), r * TOKC, 0)
                rp.proj_tm(w_att_in, 1088, 64, vtokA.ap(), r * TOKC, 256)
            rp.emit(finish=False)
            k.barrier()
        if maybe_stop(k, "p2"):
            return nc
        with ExitStack() as pes:
            pa, va = projA.ap(), vtokA.ap()
            D = dict(sqT=pa[0:256, :], skT=pa[256:512, :], bqT=pa[512:768, :], bkT=pa[768:832, :],
                     sv=[va[:, hl * 64:(hl + 1) * 64].rearrange("(b p) d -> p b d", p=P) for hl in range(4)],
                     bv=va[:, 256:320].rearrange("(b p) d -> p b d", p=P),
                     cst=a_cst, small=a_small, oT=oA.own)
            ap_ = AttnProg(nc, pes, D, k=k, pfx="p3")
            ap_.final = False
            ap_.emit(finish=False)
            oA.set_events(k.barrier(oA.gathers()))
        if maybe_stop(k, "p3"):
            return nc
        with ExitStack() as pes:
            rp = RowProg(nc, pes, TOKC, 4, k=k, pfx="p4")
            rp.load_gains(gB)
            rp.load_sel(sel)
            rp.load_h(hsp.ap())
            rp.mix_in_sel(lambda rc: (oA.gat(rc // 4, (rc % 4) * P, (rc % 4 + 1) * P), oA.gbuf((rc % 4) * P)), 8, w_att_out)
            rp.ffn(0, wg[1], wu[1], wd[1])
            rp.ple(1, wpg[0], wpp[0], pT[0])
            rp.ffn(2, wg[2], wu[2], wd[2])
            rp.hn_to_dram(3, hnB.own)
            rp.proj_fm(w_gdn_z, 0, 2048, zB.ap(), 0, 0)
            rp.store_h(hsp.ap(), final=False)
            rp.emit(finish=False)
            hnB.set_events(k.barrier(hnB.gathers()))
        if maybe_stop(k, "p4"):
            return nc
        with ExitStack() as pes:
            rp = RowProg(nc, pes, TOKC, 2, k=k, pfx="p5")
            for r in range(2):
                rp.hn_from_dram(lambda r0, r1, r=r: (hnB.gat(r, r0, r1), hnB.gbuf(r0)))
                rp.proj_fm(w_gdn_in, 0, 2048, projB.ap(), 0, r * TOKC)
                rp.proj_tm(w_gdn_in, 2048, 16, gtokB.ap(), r * TOKC, 0)
            rp.emit(finish=False)
            k.barrier()
        if maybe_stop(k, "p5"):
            return nc
        with ExitStack() as pes:
            D = dict(xT=projB.ap(), gtok=gtokB.ap(), gconst=g_gconst, convw=g_convw, small=g_small, c32=g_c32, oT=oB.own)
            gp = GdnProg(nc, pes, D, k=k, pfx="p6", fused=True)
            gp.emit(NCH, finish=False)
            oB.set_events(k.barrier(oB.gathers()))
        if maybe_stop(k, "p6"):
            return nc
        with ExitStack() as pes:
            rp = RowProg(nc, pes, TOKC, 2, k=k, pfx="p7")
            rp.load_gains(gC)
            rp.load_sel(sel)
            rp.load_h(hsp.ap())
            rp.gdn_gate_mix_in_sel(lambda rc: (oB.gat(rc // 8, (rc % 8) * P, (rc % 8 + 1) * P), oB.gbuf((rc % 8) * P)), zB.ap(), w_gdn_out)
            rp.ffn(0, wg[3], wu[3], wd[3])
            rp.ple(1, wpg[1], wpp[1], pT[1])
            rp.store_h(outT, final=True)
            rp.emit(finish=True)
    return nc


def kernel(x, p, ffn_norm, ffn_w_gate, ffn_w_up, ffn_w_down, mix_norm,
           att_w_in, att_q_norm, att_k_norm, att_sinks, att_w_out,
           gdn_w_in, gdn_conv_w, gdn_a_log, gdn_dt_bias, gdn_out_norm, gdn_w_out,
           ple_norm, ple_w_gate, ple_w_proj):
    f = lambda a: np.ascontiguousarray(np.asarray(a, dtype=np.float32))
    x = f(x).reshape(-1, 1024)
    p = f(p).reshape(2, -1, 256)
    ffn_norm, mix_norm, ple_norm = f(ffn_norm), f(mix_norm), f(ple_norm)
    wg, wu, wd = f(ffn_w_gate), f(ffn_w_up), f(ffn_w_down)
    att_w_in, att_w_out = f(att_w_in)[0], f(att_w_out)[0]
    gdn_w_in, gdn_w_out = f(gdn_w_in)[0], f(gdn_w_out)[0]
    conv_w, a_log, dt_bias = f(gdn_conv_w)[0], f(gdn_a_log)[0], f(gdn_dt_bias)[0]
    qg, kg, sinks = f(att_q_norm)[0], f(att_k_norm)[0], f(att_sinks)[0]
    tok = lambda c: slice(c * TOKC, (c + 1) * TOKC)
    ar = np.arange

    shared = {}
    for i, (l, j) in enumerate([(0, 0), (0, 1), (1, 0), (1, 1)]):
        shared["wg%d" % i] = wg[l, j]
        shared["wu%d" % i] = wu[l, j]
        shared["wd%d" % i] = wd[l, j]
    for i in range(2):
        shared["wpg%d" % i] = f(ple_w_gate)[i]
        shared["wpp%d" % i] = f(ple_w_proj)[i]
    shared["gA"] = _gain_layout([ffn_norm[0, 0], mix_norm[0]])
    shared["gB"] = _gain_layout([ffn_norm[0, 1], ple_norm[0], ffn_norm[1, 0], mix_norm[1]])
    shared["gC"] = _gain_layout([ffn_norm[1, 1], ple_norm[1]], extra=f(gdn_out_norm)[0])
    rows = np.concatenate([ar(0, 256), 512 + ar(0, 256), ar(256, 512), 512 + ar(256, 512)])
    shared["w_att_out"] = np.ascontiguousarray(att_w_out[rows])
    shared["w_gdn_z"] = np.ascontiguousarray(gdn_w_in[:, 4096:6144])
    shared["w_gdn_out"] = gdn_w_out
    shared["g_c32"] = gdn_consts()
    sm = np.zeros((128, 8), np.float32)
    sm[:, 0] = 1.0
    sm[:, 1] = 1e-6
    sm[:, 2] = LN_QSCALE
    shared["g_small"] = sm

    maps = []
    for c in range(NCORES):
        b, half = c // 2, c % 2
        m = dict(shared)
        m["xT"] = np.ascontiguousarray(x[tok(c)].T)
        m["pT0"] = np.ascontiguousarray(p[0][tok(c)].T)
        m["pT1"] = np.ascontiguousarray(p[1][tok(c)].T)
        s = np.zeros((128, 2), np.float32)
        s[:, half] = 1.0
        m["sel"] = s
        h4 = half * 256
        cols = np.concatenate([ar(h4, h4 + 256), 512 + ar(h4, h4 + 256), 1536 + ar(h4, h4 + 256),
                               2048 + ar(half * 64, half * 64 + 64), 1024 + ar(h4, h4 + 256), 2176 + ar(half * 64, half * 64 + 64)])
        m["w_att_in"] = np.ascontiguousarray(att_w_in[:, cols])
        cols = np.concatenate([ar(half * 512, half * 512 + 512), 1024 + ar(half * 512, half * 512 + 512),
                               2048 + ar(half * 1024, half * 1024 + 1024),
                               6160 + ar(half * 8, half * 8 + 8), 6144 + ar(half * 8, half * 8 + 8)])
        m["w_gdn_in"] = np.ascontiguousarray(gdn_w_in[:, cols])
        m["a_cst"] = attn_consts(half)
        sma = np.zeros((128, 520), np.float32)
        sma[0:64, 0] = qg
        sma[0:64, 1] = kg
        sma[:, 2] = 1.0
        sma[:, 3] = -SHIFT
        sma[:, 4] = 1e-6
        sma[:, 5] = 64e-6
        sma[:, 6] = -2.0794415416798357
        for hl in range(4):
            sma[0:64, 8 + hl * 128:8 + (hl + 1) * 128] = sinks[4 * half + hl]
        m["a_small"] = sma
        chs = np.concatenate([ar((4 * half + mm) * 128, (4 * half + mm + 1) * 128) for mm in range(4)] +
                             [1024 + ar((4 * half + mm) * 128, (4 * half + mm + 1) * 128) for mm in range(4)] +
                             [2048 + ar((8 * half + hv) * 128, (8 * half + hv + 1) * 128) for hv in range(8)])
        cw = conv_w[:, chs]
        m["g_convw"] = np.ascontiguousarray(cw.reshape(4, 16, 128).transpose(2, 1, 0).reshape(128, 64))
        gc = np.zeros((128, 2, 256), np.float32)
        gc[:, 0, :] = np.tile(dt_bias[8 * half:8 * half + 8], 32)[None, :]
        gc[:, 1, :] = np.tile(a_log[8 * half:8 * half + 8], 32)[None, :]
        m["g_gconst"] = gc
        maps.append(m)

    nc = build_fused(stop=_DBG.get("stop"), dump=_DBG.get("dump"))
    if _DBG.get("trace"):
        full = run_bass_kernel_spmd(nc, maps, core_ids=list(range(NCORES)), trace=True)
        _DBG["full"] = full
        res = full.results
    else:
        res = run_bass_kernel_spmd(nc, maps, core_ids=list(range(NCORES))).results
    if _DBG.get("stop"):
        _DBG["res"] = res
        return None
    out = np.concatenate([res[c]["outT"].T for c in range(NCORES)], 0)
    return np.ascontiguousarray(out.reshape(4, 4096, 1024).astype(np.float32))
```

```python
import numpy as np
import concourse.bass as bass
import concourse.mybir as mybir
from concourse.alu_op_type import AluOpType as ALU
from contextlib import ExitStack

F32 = mybir.dt.float32
BF16 = mybir.dt.bfloat16
AF = mybir.ActivationFunctionType
AX = mybir.AxisListType
P = 128


class Buf:
    __slots__ = ("name", "lastw", "reads", "dsem", "dcount", "excl")

    def __init__(self, name):
        self.name = name
        self.lastw = None
        self.reads = {}
        self.dsem = None
        self.dcount = 0
        self.excl = False


class K:
    def __init__(self, nc, es, safe_same=True):
        self.nc = nc
        self.es = es
        self.E = {}
        for n in ("tensor", "vector", "scalar", "gpsimd", "sync"):
            sem = es.enter_context(nc.semaphore("e_" + n))
            self.E[n] = dict(eng=getattr(nc, n), sem=sem, count=0, waited={}, name=n)
        self.safe_same = safe_same
        self.dma_sems = {}
        self.free_dsems = []
        self.nsem_alloc = 0
        self.bar_sem = es.enter_context(nc.semaphore("barrier"))
        self.bar_count = 0
        self.cc_sem = es.enter_context(nc.semaphore("ccsem"))
        self.cc_count = 0
        self.nbuf = 0
        self.final_events = []
        self.ninstr = 0

    def buf(self, name=None):
        self.nbuf += 1
        return Buf(name or ("b%d" % self.nbuf))

    def _wait(self, e, sem, val):
        key = id(sem)
        if e["waited"].get(key, 0) >= val:
            return
        e["eng"].wait_ge(sem, val)
        e["waited"][key] = val
        if getattr(self, "trace", None) is not None:
            self.trace.append((e["name"], "wait", [n for n, x in self.E.items() if x["sem"] is sem] or "dma", val))

    def _emit_waits(self, en, reads, writes):
        e = self.E[en]
        evs = []
        for b in reads:
            if b.lastw is not None:
                evs.append(b.lastw)
            if b.excl:
                evs.extend(ev for ev in b.reads.values() if ev[2] != en)
        for b in writes:
            if b.lastw is not None:
                evs.append(b.lastw)
            evs.extend(b.reads.values())
        for (sem, val, src) in evs:
            if src == en and (en == "tensor" or not self.safe_same):
                continue
            self._wait(e, sem, val)

    def _record(self, ev, reads, writes):
        for b in writes:
            b.lastw = ev
            b.reads = {}
        for b in reads:
            if b in writes:
                continue
            key = id(ev[0])
            old = b.reads.get(key)
            if old is None or old[1] < ev[1]:
                b.reads[key] = ev

    def op(self, en, fn, reads=(), writes=()):
        e = self.E[en]
        self._emit_waits(en, reads, writes)
        ins = fn(e["eng"])
        e["count"] += 1
        ins.then_inc(e["sem"], 1)
        ev = (e["sem"], e["count"], en)
        self._record(ev, reads, writes)
        self.ninstr += 1
        if getattr(self, "trace", None) is not None:
            self.trace.append((en, "op", e["count"], [b.name for b in reads], [b.name for b in writes]))
        return ev

    def dma(self, qn, pairs, reads=(), writes=(), final=False):
        e = self.E[qn]
        self._emit_waits(qn, reads, writes)
        owner = writes[0] if len(writes) else reads[0]
        if owner.dsem is None:
            if self.free_dsems:
                owner.dsem, owner.dcount = self.free_dsems.pop()
            else:
                self.nsem_alloc += 1
                owner.dsem = self.es.enter_context(self.nc.semaphore("d%d_%s" % (self.nsem_alloc, owner.name)))
        for (o, i) in pairs:
            ins = e["eng"].dma_start(out=o, in_=i)
            owner.dcount += 16
            ins.then_inc(owner.dsem, 16)
            self.ninstr += 1
        ev = (owner.dsem, owner.dcount, "dma")
        self.dma_sems[id(owner.dsem)] = [owner.dsem, owner.dcount]
        self._record(ev, reads, writes)
        if final:
            self.final_events.append(ev)
        return ev

    def barrier(self, collective_fn=None):
        g = self.E["gpsimd"]
        for n, e in self.E.items():
            if n != "gpsimd" and e["count"] > 0:
                self._wait(g, e["sem"], e["count"])
        if g["count"] > 0:
            self._wait(g, g["sem"], g["count"])
        for sem, val in self.dma_sems.values():
            self._wait(g, sem, val)
        fns = collective_fn if isinstance(collective_fn, (list, tuple)) else ([collective_fn] if collective_fn else [])
        ins = g["eng"].nop()
        self.bar_count += 1
        ins.then_inc(self.bar_sem, 1)
        for n, e in self.E.items():
            self._wait(e, self.bar_sem, self.bar_count)
            for n2, e2 in self.E.items():
                e["waited"][id(e2["sem"])] = max(e["waited"].get(id(e2["sem"]), 0), e2["count"])
            for sem, val in self.dma_sems.values():
                e["waited"][id(sem)] = max(e["waited"].get(id(sem), 0), val)
        events = []
        for fn in fns:
            ins = fn(g["eng"])
            self.cc_count += 1
            ins.then_inc(self.cc_sem, 1)
            events.append((self.cc_sem, self.cc_count, "cc"))
        self.free_dsems.extend((sem, val) for sem, val in self.dma_sems.values())
        self.dma_sems = {}
        return events

    def finish(self):
        e = self.E["sync"]
        for (sem, val, src) in self.final_events:
            self._wait(e, sem, val)
EPS = 1e-6
TT = 512


class RR:
    def __init__(self, items):
        self.items = items
        self.i = 0

    def next(self):
        it = self.items[self.i % len(self.items)]
        self.i += 1
        return it


class RowProg:
    def __init__(self, nc, es, TOK, n_gain, NST=3, NWB=4, safe_same=True, k=None, pfx=""):
        self.nc = nc
        self.es = es
        k = self.k = k if k is not None else K(nc, es, safe_same=safe_same)
        self.TOK = TOK
        self.pfx = pfx
        self.NTT = TOK // TT
        NTT = self.NTT

        def alloc(name, shape, dt):
            return es.enter_context(nc.sbuf_tensor(pfx + "sb_" + name, shape, dt))

        self.hT = alloc("hT", [P, 8, TOK], F32)
        self.hTb = [[k.buf("hT%d_%d" % (c, t)) for t in range(NTT)] for c in range(8)]
        self.hn = alloc("hn", [P, 8, TOK], BF16)
        self.hnb = [[k.buf("hn%d_%d" % (c, t)) for t in range(NTT)] for c in range(8)]
        self.act = alloc("act", [P, 11, TOK], BF16)
        self.actb = [[k.buf("ac%d_%d" % (c, t)) for t in range(NTT)] for c in range(11)]
        wst = alloc("wst", [P, NST, 2048], F32)
        self.wst = RR([(k.buf("wst%d" % i), wst[:, i, :]) for i in range(NST)])
        wbf = alloc("wbf", [P, NWB, 2048], BF16)
        self.wbf = RR([(k.buf("wbf%d" % i), wbf[:, i, :]) for i in range(NWB)])
        sq = alloc("sq", [P, 2, TT], BF16)
        self.sq = RR([(k.buf("sq%d" % i), sq[:, i, :]) for i in range(2)])
        t32 = alloc("t32", [P, 4, TT], F32)
        self.t32 = RR([(k.buf("t32_%d" % i), t32[:, i, :]) for i in range(4)])
        self.ones = alloc("ones", [P, P], BF16)
        self.onesb = k.buf("ones")
        self.gains = alloc("gains", [P, n_gain * 8 + 1], F32)
        self.n_gain = n_gain
        self.gainsb = k.buf("gains")
        ps = [es.enter_context(nc.psum_tensor(pfx + "ps%d" % i, [P, TT], F32)) for i in range(7)]
        self.ps = RR([(k.buf("ps%d" % i), ps[i][:, :]) for i in range(7)])
        self.items = []
        k.op("vector", lambda e: e.memset(self.ones[:], 1.0), writes=[self.onesb])
        self.epsc = alloc("epsc", [P, 1], F32)
        self.epsap = self.epsc[:, 0:1]
        k.op("vector", lambda e: e.memset(self.epsc[:], EPS), writes=[self.onesb])

    def ts(self, tt):
        return slice(tt * TT, (tt + 1) * TT)

    def item(self, specs, fn):
        self.items.append((specs, fn))

    def load_gains(self, g_dram):
        self.k.dma("sync", [(self.gains[:], g_dram)], writes=[self.gainsb])

    def obuf(self, rc):
        if rc < 8:
            return self.hnb[rc], self.hn[:, rc, :]
        return self.actb[rc - 8], self.act[:, rc - 8, :]

    def _issue_load(self, spec):
        k = self.k
        w, r0, R, c0, ncols = spec
        assert R * ncols <= 2048
        sb, sap = self.wst.next()
        wb, wap = self.wbf.next()
        src = w[r0 * P:(r0 + R) * P, c0:c0 + ncols].rearrange("(r p) n -> p r n", p=P)
        dst = sap[:, 0:R * ncols].rearrange("p (r n) -> p r n", r=R)
        k.dma("sync", [(dst, src)], writes=[sb])
        k.op("gpsimd", lambda e: e.tensor_copy(out=wap[:, 0:R * ncols], in_=sap[:, 0:R * ncols]),
             reads=[sb], writes=[wb])
        return wb, wap[:, 0:R * ncols].rearrange("p (r n) -> p r n", r=R)

    def emit(self, lookahead=1, finish=True):
        items = self.items
        loaded = {}
        nl = 0
        for i, (specs, fn) in enumerate(items):
            while nl < len(items) and nl <= i + lookahead:
                loaded[nl] = [self._issue_load(s) for s in items[nl][0]]
                nl += 1
            fn(loaded.pop(i))
        if finish:
            self.k.finish()

    def load_h(self, xT_dram):
        def fn(_):
            for c in range(8):
                self.k.dma("sync", [(self.hT[:, c, :], xT_dram[c * P:(c + 1) * P, :])], writes=self.hTb[c])
        self.item([], fn)

    def store_h(self, out_dram, final=True):
        def fn(_):
            for c in range(8):
                self.k.dma("sync", [(out_dram[c * P:(c + 1) * P, :], self.hT[:, c, :])], reads=self.hTb[c], final=final)
        self.item([], fn)

    def load_sel(self, sel_dram):
        self.sel = self.es.enter_context(self.nc.sbuf_tensor(self.pfx + "sb_sel", [P, 2], F32))
        self.selb = self.k.buf("sel")
        self.k.dma("sync", [(self.sel[:], sel_dram)], writes=[self.selb])

    def hn_to_dram(self, gi, dst):
        self.norm(gi)

        def fn(_):
            for c in range(8):
                self.k.dma("sync", [(dst(c * P, (c + 1) * P), self.hn[:, c, :])], reads=self.hnb[c])
        self.item([], fn)

    def hn_from_dram(self, src):
        def fn(_):
            for c in range(8):
                sap_, sbuf_ = src(c * P, (c + 1) * P)
                self.k.dma("sync", [(self.hn[:, c, :], sap_)], reads=[sbuf_], writes=self.hnb[c])
        self.item([], fn)

    def proj_fm(self, w, c0w, n_out, out_dram, row0, col0):
        k = self.k
        c0 = 0
        while c0 < n_out:
            ncols = min(256, n_out - c0)

            def fn(tiles, c0=c0, ncols=ncols):
                (wb, wap), = tiles
                j0 = 0
                while j0 < ncols:
                    m = min(P, ncols - j0)
                    for tt in range(self.NTT):
                        ts = self.ts(tt)
                        pb, pap = self.ps.next()
                        for c in range(8):
                            k.op("tensor", lambda e: e.matmul(pap[0:m, :], lhsT=wap[:, c, j0:j0 + m], rhs=self.hn[:, c, ts],
                                                              start=(c == 0), stop=(c == 7)),
                                 reads=[wb, self.hnb[c][tt]], writes=[pb])
                        eb, eap = self.t32.next()
                        if tt % 2 == 0:
                            k.op("scalar", lambda e: e.copy(out=eap[0:m, :], in_=pap[0:m, :]), reads=[pb], writes=[eb])
                        else:
                            k.op("vector", lambda e: e.tensor_copy(out=eap[0:m, :], in_=pap[0:m, :]), reads=[pb], writes=[eb])
                        r0 = row0 + c0 + j0
                        k.dma("sync", [(out_dram[r0:r0 + m, col0 + tt * TT:col0 + (tt + 1) * TT], eap[0:m, :])], reads=[eb])
                    j0 += m
            self.item([(w, 0, 8, c0w + c0, ncols)], fn)
            c0 += ncols

    def proj_tm(self, w, c0w, ncols, out_dram, tok0, ocol0):
        k = self.k

        def fn(tiles):
            (wb, wap), = tiles
            for tb in range(self.TOK // P):
                tt = (tb * P) // TT
                pb, pap = self.ps.next()
                for c in range(8):
                    k.op("tensor", lambda e: e.matmul(pap[:, 0:ncols], lhsT=self.hn[:, c, tb * P:(tb + 1) * P], rhs=wap[:, c, 0:ncols],
                                                      start=(c == 0), stop=(c == 7)),
                         reads=[wb, self.hnb[c][tt]], writes=[pb])
                eb, eap = self.t32.next()
                if tb % 2 == 0:
                    k.op("scalar", lambda e: e.copy(out=eap[:, 0:ncols], in_=pap[:, 0:ncols]), reads=[pb], writes=[eb])
                else:
                    k.op("vector", lambda e: e.tensor_copy(out=eap[:, 0:ncols], in_=pap[:, 0:ncols]), reads=[pb], writes=[eb])
                k.dma("sync", [(out_dram[tok0 + tb * P:tok0 + (tb + 1) * P, ocol0:ocol0 + ncols], eap[:, 0:ncols])], reads=[eb])
        self.item([(w, 0, 8, c0w, ncols)], fn)

    def _load_sel_chunk(self, G, rc):
        k = self.k
        T = self.TOK
        ab, aap = self.wst.next()
        gap_, gbuf_ = G(rc)
        k.dma("sync", [(aap[:, 0:T], gap_[:, 0:T])], reads=[gbuf_], writes=[ab])
        bb, bap = self.wst.next()
        k.dma("sync", [(bap[:, 0:T], gap_[:, T:2 * T])], reads=[gbuf_], writes=[bb])
        k.op("scalar", lambda e: e.activation(out=aap[:, 0:T], in_=aap[:, 0:T], func=AF.Copy, scale=self.sel[:, 0:1]),
             reads=[ab, self.selb], writes=[ab])
        return ab, aap, bb, bap

    def mix_in_sel(self, G, nrc, w_out):
        k = self.k

        def fn(_):
            for rc in range(nrc):
                bufs, dap = self.obuf(rc)
                ab, aap, bb, bap = self._load_sel_chunk(G, rc)
                k.op("vector", lambda e: e.scalar_tensor_tensor(out=dap, in0=bap[:, 0:self.TOK], scalar=self.sel[:, 1:2],
                                                                in1=aap[:, 0:self.TOK], op0=ALU.mult, op1=ALU.add),
                     reads=[ab, bb, self.selb], writes=bufs)
        self.item([], fn)
        self._mix_matmuls(nrc, w_out)

    def _mix_matmuls(self, nrc, w_out):
        k = self.k
        for dc in range(8):
            def fn(tiles, dc=dc):
                (wb, wap), = tiles
                for tt in range(self.NTT):
                    ts = self.ts(tt)
                    pb, pap = self.ps.next()
                    for rc in range(nrc):
                        bufs, oap = self.obuf(rc)
                        k.op("tensor", lambda e: e.matmul(pap, lhsT=wap[:, rc, :], rhs=oap[:, ts],
                                                          start=(rc == 0), stop=(rc == nrc - 1)),
                             reads=[wb, bufs[tt]], writes=[pb])
                    k.op("vector", lambda e: e.tensor_tensor(out=self.hT[:, dc, ts], in0=pap, in1=self.hT[:, dc, ts], op=ALU.add),
                         reads=[pb, self.hTb[dc][tt]], writes=[self.hTb[dc][tt]])
            self.item([(w_out, 0, nrc, dc * P, P)], fn)

    def gdn_gate_mix_in_sel(self, G, zT_dram, w_out):
        k = self.k
        gcol = self.gains[:, self.n_gain * 8:self.n_gain * 8 + 1]

        def fn(_):
            for rc in range(16):
                bufs, dap = self.obuf(rc)
                ob, oap, bb, bap = self._load_sel_chunk(G, rc)
                k.op("vector", lambda e: e.scalar_tensor_tensor(out=oap[:, 0:self.TOK], in0=bap[:, 0:self.TOK], scalar=self.sel[:, 1:2],
                                                                in1=oap[:, 0:self.TOK], op0=ALU.mult, op1=ALU.add),
                     reads=[ob, bb, self.selb], writes=[ob])
                zb, zap = self.wst.next()
                k.dma("sync", [(zap[:, 0:self.TOK], zT_dram[rc * P:(rc + 1) * P, :])], writes=[zb])
                for tt in range(self.NTT):
                    ts = self.ts(tt)
                    qb, qap = self.sq.next()
                    k.op("scalar", lambda e: e.activation(out=qap, in_=oap[:, ts], func=AF.Square), reads=[ob], writes=[qb])
                    pb, pap = self.ps.next()
                    k.op("tensor", lambda e: e.matmul(pap, lhsT=self.ones[:], rhs=qap, start=True, stop=True),
                         reads=[qb, self.onesb], writes=[pb])
                    tb, tap = self.t32.next()
                    k.op("scalar", lambda e: e.activation(out=tap, in_=pap, func=AF.Ln, scale=1.0 / 128.0, bias=self.epsap),
                         reads=[pb, self.onesb], writes=[tb])
                    k.op("scalar", lambda e: e.activation(out=tap, in_=tap, func=AF.Exp, scale=-0.5), reads=[tb], writes=[tb])
                    k.op("vector", lambda e: e.scalar_tensor_tensor(out=oap[:, ts], in0=oap[:, ts], scalar=gcol, in1=tap,
                                                                    op0=ALU.mult, op1=ALU.mult),
                         reads=[ob, tb, self.gainsb], writes=[ob])
                for tt in range(self.NTT):
                    ts = self.ts(tt)
                    k.op("scalar", lambda e: e.activation(out=zap[:, ts], in_=zap[:, ts], func=AF.Silu), reads=[zb], writes=[zb])
                    k.op("vector", lambda e: e.tensor_tensor(out=dap[:, ts], in0=oap[:, ts], in1=zap[:, ts], op=ALU.mult),
                         reads=[ob, zb], writes=[bufs[tt]])
        self.item([], fn)
        self._mix_matmuls(16, w_out)

    def norm(self, gi):
        def fn(_):
            k = self.k
            for tt in range(self.NTT):
                ts = self.ts(tt)
                pb, pap = self.ps.next()
                for c in range(8):
                    qb, qap = self.sq.next()
                    k.op("scalar", lambda e: e.activation(out=qap, in_=self.hT[:, c, ts], func=AF.Square),
                         reads=[self.hTb[c][tt]], writes=[qb])
                    k.op("tensor", lambda e: e.matmul(pap, lhsT=self.ones[:], rhs=qap, start=(c == 0), stop=(c == 7)),
                         reads=[qb, self.onesb], writes=[pb])
                tb, tap = self.t32.next()
                k.op("scalar", lambda e: e.activation(out=tap, in_=pap, func=AF.Ln, scale=1.0 / 1024.0, bias=self.epsap),
                     reads=[pb, self.onesb], writes=[tb])
                rb, rap = self.t32.next()
                k.op("scalar", lambda e: e.activation(out=rap, in_=tap, func=AF.Exp, scale=-0.5), reads=[tb], writes=[rb])
                for c in range(8):
                    k.op("vector", lambda e: e.scalar_tensor_tensor(
                        out=self.hn[:, c, ts], in0=self.hT[:, c, ts], scalar=self.gains[:, gi * 8 + c:gi * 8 + c + 1],
                        in1=rap, op0=ALU.mult, op1=ALU.mult),
                        reads=[self.hTb[c][tt], rb, self.gainsb], writes=[self.hnb[c][tt]])
        self.item([], fn)

    def ffn(self, gi, wg, wu, wd):
        self.norm(gi)
        k = self.k
        for half in range(2):
            f0 = half * 11
            groups = [(0, 2), (2, 2), (4, 2), (6, 2), (8, 2), (10, 1)]
            for (fl0, nfc) in groups:
                def fn(tiles, fl0=fl0, nfc=nfc):
                    (gb, gap), (ub, uap) = tiles
                    for j in range(nfc):
                        for tt in range(self.NTT):
                            ts = self.ts(tt)
                            pgb, pg = self.ps.next()
                            pub, pu = self.ps.next()
                            for c in range(8):
                                k.op("tensor", lambda e: e.matmul(pg, lhsT=gap[:, c, j * P:(j + 1) * P], rhs=self.hn[:, c, ts],
                                                                  start=(c == 0), stop=(c == 7)),
                                     reads=[gb, self.hnb[c][tt]], writes=[pgb])
                            for c in range(8):
                                k.op("tensor", lambda e: e.matmul(pu, lhsT=uap[:, c, j * P:(j + 1) * P], rhs=self.hn[:, c, ts],
                                                                  start=(c == 0), stop=(c == 7)),
                                     reads=[ub, self.hnb[c][tt]], writes=[pub])
                            sb, sap = self.t32.next()
                            k.op("scalar", lambda e: e.activation(out=sap, in_=pg, func=AF.Silu), reads=[pgb], writes=[sb])
                            k.op("vector", lambda e: e.tensor_tensor(out=self.act[:, fl0 + j, ts], in0=sap, in1=pu, op=ALU.mult),
                                 reads=[sb, pub], writes=[self.actb[fl0 + j][tt]])
                c0 = (f0 + fl0) * P
                self.item([(wg, 0, 8, c0, nfc * P), (wu, 0, 8, c0, nfc * P)], fn)
            for dc in range(8):
                def fn(tiles, dc=dc):
                    (wb, wap), = tiles
                    for tt in range(self.NTT):
                        ts = self.ts(tt)
                        pb, pap = self.ps.next()
                        for f in range(11):
                            k.op("tensor", lambda e: e.matmul(pap, lhsT=wap[:, f, :], rhs=self.act[:, f, ts],
                                                              start=(f == 0), stop=(f == 10)),
                                 reads=[wb, self.actb[f][tt]], writes=[pb])
                        k.op("vector", lambda e: e.scalar_tensor_tensor(
                            out=self.hT[:, dc, ts], in0=pap, scalar=0.5, in1=self.hT[:, dc, ts],
                            op0=ALU.mult, op1=ALU.add),
                            reads=[pb, self.hTb[dc][tt]], writes=[self.hTb[dc][tt]])
                self.item([(wd, f0, 11, dc * P, P)], fn)

    def proj_out(self, gi, w, n_out, out_dram):
        self.norm(gi)
        k = self.k
        c0 = 0
        while c0 < n_out:
            ncols = min(256, n_out - c0)

            def fn(tiles, c0=c0, ncols=ncols):
                (wb, wap), = tiles
                j0 = 0
                while j0 < ncols:
                    m = min(P, ncols - j0)
                    for tt in range(self.NTT):
                        ts = self.ts(tt)
                        pb, pap = self.ps.next()
                        for c in range(8):
                            k.op("tensor", lambda e: e.matmul(pap[0:m, :], lhsT=wap[:, c, j0:j0 + m], rhs=self.hn[:, c, ts],
                                                              start=(c == 0), stop=(c == 7)),
                                 reads=[wb, self.hnb[c][tt]], writes=[pb])
                        eb, eap = self.t32.next()
                        if tt % 2 == 0:
                            k.op("scalar", lambda e: e.copy(out=eap[0:m, :], in_=pap[0:m, :]), reads=[pb], writes=[eb])
                        else:
                            k.op("vector", lambda e: e.tensor_copy(out=eap[0:m, :], in_=pap[0:m, :]), reads=[pb], writes=[eb])
                        k.dma("sync", [(out_dram[c0 + j0:c0 + j0 + m, ts], eap[0:m, :])], reads=[eb], final=True)
                    j0 += m
            self.item([(w, 0, 8, c0, ncols)], fn)
            c0 += ncols

    def load_T_bf16(self, src_dram, nrc):
        def fn(_):
            k = self.k
            for rc in range(nrc):
                bufs, dap = self.obuf(rc)
                sb, sap = self.wst.next()
                k.dma("sync", [(sap[:, 0:self.TOK], src_dram[rc * P:(rc + 1) * P, :])], writes=[sb])
                k.op("gpsimd", lambda e: e.tensor_copy(out=dap, in_=sap[:, 0:self.TOK]), reads=[sb], writes=bufs)
        self.item([], fn)

    def mix_in(self, oT_dram, nrc, w_out):
        self.load_T_bf16(oT_dram, nrc)
        k = self.k
        for dc in range(8):
            def fn(tiles, dc=dc):
                (wb, wap), = tiles
                for tt in range(self.NTT):
                    ts = self.ts(tt)
                    pb, pap = self.ps.next()
                    for rc in range(nrc):
                        bufs, oap = self.obuf(rc)
                        k.op("tensor", lambda e: e.matmul(pap, lhsT=wap[:, rc, :], rhs=oap[:, ts],
                                                          start=(rc == 0), stop=(rc == nrc - 1)),
                             reads=[wb, bufs[tt]], writes=[pb])
                    k.op("vector", lambda e: e.tensor_tensor(out=self.hT[:, dc, ts], in0=pap, in1=self.hT[:, dc, ts], op=ALU.add),
                         reads=[pb, self.hTb[dc][tt]], writes=[self.hTb[dc][tt]])
            self.item([(w_out, 0, nrc, dc * P, P)], fn)

    def ple(self, gi, wpg, wpp, pT_dram):
        self.norm(gi)
        k = self.k

        def fnp(_):
            for rc in range(2):
                sb, sap = self.wst.next()
                k.dma("sync", [(sap[:, 0:self.TOK], pT_dram[rc * P:(rc + 1) * P, :])], writes=[sb])
                k.op("gpsimd", lambda e: e.tensor_copy(out=self.act[:, rc, :], in_=sap[:, 0:self.TOK]),
                     reads=[sb], writes=self.actb[rc])
        self.item([], fnp)
        for dc in range(8):
            def fn(tiles, dc=dc):
                (gb, gap), (pb_, pap_) = tiles
                for tt in range(self.NTT):
                    ts = self.ts(tt)
                    pgb, pg = self.ps.next()
                    ppb, pp = self.ps.next()
                    for c in range(8):
                        k.op("tensor", lambda e: e.matmul(pg, lhsT=gap[:, c, :], rhs=self.hn[:, c, ts],
                                                          start=(c == 0), stop=(c == 7)),
                             reads=[gb, self.hnb[c][tt]], writes=[pgb])
                    for c in range(2):
                        k.op("tensor", lambda e: e.matmul(pp, lhsT=pap_[:, c, :], rhs=self.act[:, c, ts],
                                                          start=(c == 0), stop=(c == 1)),
                             reads=[pb_, self.actb[c][tt]], writes=[ppb])
                    sb, sap = self.t32.next()
                    k.op("scalar", lambda e: e.activation(out=sap, in_=pg, func=AF.Sigmoid), reads=[pgb], writes=[sb])
                    mb, map_ = self.t32.next()
                    k.op("vector", lambda e: e.tensor_tensor(out=map_, in0=sap, in1=pp, op=ALU.mult),
                         reads=[sb, ppb], writes=[mb])
                    k.op("vector", lambda e: e.tensor_tensor(out=self.hT[:, dc, ts], in0=map_, in1=self.hT[:, dc, ts], op=ALU.add),
                         reads=[mb, self.hTb[dc][tt]], writes=[self.hTb[dc][tt]])
            self.item([(wpg, 0, 8, dc * P, P), (wpp, 0, 2, dc * P, P)], fn)

    def gdn_gate_mix_in(self, oT_dram, zT_dram, w_out):
        k = self.k
        gcol = self.gains[:, self.n_gain * 8:self.n_gain * 8 + 1]

        def fn(_):
            for rc in range(16):
                bufs, dap = self.obuf(rc)
                ob, oap = self.wst.next()
                k.dma("sync", [(oap[:, 0:self.TOK], oT_dram[rc * P:(rc + 1) * P, :])], writes=[ob])
                zb, zap = self.wst.next()
                k.dma("sync", [(zap[:, 0:self.TOK], zT_dram[rc * P:(rc + 1) * P, :])], writes=[zb])
                rstd = []
                for tt in range(self.NTT):
                    ts = self.ts(tt)
                    qb, qap = self.sq.next()
                    k.op("scalar", lambda e: e.activation(out=qap, in_=oap[:, ts], func=AF.Square), reads=[ob], writes=[qb])
                    pb, pap = self.ps.next()
                    k.op("tensor", lambda e: e.matmul(pap, lhsT=self.ones[:], rhs=qap, start=True, stop=True),
                         reads=[qb, self.onesb], writes=[pb])
                    tb, tap = self.t32.next()
                    k.op("scalar", lambda e: e.activation(out=tap, in_=pap, func=AF.Ln, scale=1.0 / 128.0, bias=self.epsap),
                         reads=[pb, self.onesb], writes=[tb])
                    k.op("scalar", lambda e: e.activation(out=tap, in_=tap, func=AF.Exp, scale=-0.5), reads=[tb], writes=[tb])
                    k.op("vector", lambda e: e.scalar_tensor_tensor(out=oap[:, ts], in0=oap[:, ts], scalar=gcol, in1=tap,
                                                                    op0=ALU.mult, op1=ALU.mult),
                         reads=[ob, tb, self.gainsb], writes=[ob])
                for tt in range(self.NTT):
                    ts = self.ts(tt)
                    k.op("scalar", lambda e: e.activation(out=zap[:, ts], in_=zap[:, ts], func=AF.Silu), reads=[zb], writes=[zb])
                    k.op("vector", lambda e: e.tensor_tensor(out=dap[:, ts], in0=oap[:, ts], in1=zap[:, ts], op=ALU.mult),
                         reads=[ob, zb], writes=[bufs[tt]])
        self.item([], fn)
        for dc in range(8):
            def fn2(tiles, dc=dc):
                (wb, wap), = tiles
                for tt in range(self.NTT):
                    ts = self.ts(tt)
                    pb, pap = self.ps.next()
                    for rc in range(16):
                        bufs, oap = self.obuf(rc)
                        k.op("tensor", lambda e: e.matmul(pap, lhsT=wap[:, rc, :], rhs=oap[:, ts],
                                                          start=(rc == 0), stop=(rc == 15)),
                             reads=[wb, bufs[tt]], writes=[pb])
                    k.op("vector", lambda e: e.tensor_tensor(out=self.hT[:, dc, ts], in0=pap, in1=self.hT[:, dc, ts], op=ALU.add),
                         reads=[pb, self.hTb[dc][tt]], writes=[self.hTb[dc][tt]])
            self.item([(w_out, 0, 16, dc * P, P)], fn2)
SEQ = 4096
NBLK = 32
BIG = 30000.0
SHIFT = 8.0


class AttnProg:
    final = True

    def __init__(self, nc, es, D, safe_same=True, k=None, pfx=""):
        self.nc = nc
        self.es = es
        k = self.k = k if k is not None else K(nc, es, safe_same=safe_same)
        self.D = D

        def alloc(name, shape, dt):
            return es.enter_context(nc.sbuf_tensor(pfx + "sa_" + name, shape, dt))

        def rr(name, shape, dt, n):
            t = alloc(name, [shape[0], n] + list(shape[1:]), dt)
            return RR([(k.buf("%s%d" % (name, i)), t[:, i]) for i in range(n)])

        self.alloc = alloc
        self.cst = alloc("cst", [P, 128 * 4 + 4 * 512 + 2 * 512], BF16)
        self.cb = k.buf("cst")
        self.small = alloc("small", [P, 8 + 512], F32)
        self.smallb = k.buf("small")
        self.stage = rr("stage", [P, SEQ], F32, 2)
        self.qT = rr("qT", [64, SEQ], BF16, 4)
        self.kT = rr("kT", [64, SEQ], BF16, 4)
        self.v = rr("v", [P, NBLK * 64], BF16, 5)
        self.e32 = [rr("e32_%d" % s, [P, 512], F32, 1) for s in range(4)]
        self.sp = [rr("sp_%d" % s, [P, 512], BF16, 2) for s in range(4)]
        self.w = [rr("w_%d" % s, [P, 512], BF16, 1) for s in range(4)]
        self.ls = [rr("ls_%d" % s, [P, 512], BF16, 2) for s in range(4)]
        self.ost = rr("ost", [64, 512], F32, 2)
        ps = [es.enter_context(nc.psum_tensor(pfx + "psa%d" % i, [P, 512], F32)) for i in range(8)]
        self.psb = [(k.buf("psa%d" % i), ps[i][:, :]) for i in range(8)]
        for b, _ in self.psb:
            b.excl = True

    def consts(self):
        k, D = self.k, self.D
        sb, sap = self.stage.next()
        n = 128 * 4 + 4 * 512 + 2 * 512
        k.dma("sync", [(sap[:, 0:n], D["cst"])], writes=[sb])
        k.op("vector", lambda e: e.tensor_copy(out=self.cst[:], in_=sap[:, 0:n]), reads=[sb], writes=[self.cb])
        k.dma("sync", [(self.small[:], D["small"])], writes=[self.smallb])
        c = self.cst
        self.tri = c[:, 0:128]
        self.ident = c[:, 128:256]
        self.nident = c[:, 256:384]
        self.ones = c[:, 384:512]
        self.masks = [c[:, 512 + j * 512:512 + (j + 1) * 512] for j in range(4)]
        self.swab = [c[:, 2560 + j * 512:2560 + (j + 1) * 512] for j in range(2)]
        s = self.small
        self.gq = s[0:64, 0:1]
        self.gk = s[0:64, 1:2]
        self.one = s[:, 2:3]
        self.nshift = s[:, 3:4]
        self.eps = s[:, 4:5]
        self.ln8 = s[:, 6:7]
        self.zero = s[:, 7:8]
        self.sinks = s[0:64, 8:520]
        k.op("scalar", lambda e: e.activation(out=self.sinks, in_=self.sinks, func=AF.Exp, bias=self.nshift[0:64, :]),
             reads=[self.smallb], writes=[self.smallb])

    def load_sb_head(self, h):
        k, D = self.k, self.D
        qb, qap = self.qT.next()
        kb_, kap = self.kT.next()
        vb, vap = self.v.next()
        sb, sap = self.stage.next()
        k.dma("sync", [(sap[0:64, :], D["sqT"][h * 64:(h + 1) * 64, :])], writes=[sb])
        k.op("scalar", lambda e: e.activation(out=qap, in_=sap[0:64, :], func=AF.Copy, scale=0.125), reads=[sb], writes=[qb])
        sb, sap = self.stage.next()
        k.dma("sync", [(sap[0:64, :], D["skT"][h * 64:(h + 1) * 64, :])], writes=[sb])
        k.op("vector", lambda e: e.tensor_copy(out=kap, in_=sap[0:64, :]), reads=[sb], writes=[kb_])
        sb, sap = self.stage.next()
        k.dma("sync", [(sap[:, 0:NBLK * 64].rearrange("p (b d) -> p b d", d=64), D["sv"][h])], writes=[sb])
        k.op("gpsimd", lambda e: e.tensor_copy(out=vap, in_=sap[:, 0:NBLK * 64]), reads=[sb], writes=[vb])
        return (qb, qap, kb_, kap, vb, vap)

    def sb_stream(self, s, h, tiles):
        k, D = self.k, self.D
        (qb, qap, kb_, kap, vb, vap) = tiles
        cb = self.cb
        pzb, pz = self.psb[2 * s]
        pob, po = self.psb[2 * s + 1]
        for qs in range(8):
            q_sl = slice(qs * 512, (qs + 1) * 512)
            lsum = None
            kbs = list(range(4 * qs + 3, -1, -1))
            for idx, kb in enumerate(kbs):
                k_sl = slice(kb * 128, (kb + 1) * 128)
                j = kb - 4 * qs
                diag = j >= 0
                k.op("tensor", lambda e: e.matmul(pz, lhsT=kap[:, k_sl], rhs=qap[:, q_sl], start=True, stop=not diag),
                     reads=[kb_, qb], writes=[pzb])
                if diag:
                    k.op("tensor", lambda e: e.matmul(pz, lhsT=self.nident, rhs=self.masks[j], start=False, stop=True),
                         reads=[cb], writes=[pzb])
                yield
                eb, eap = self.e32[s].next()
                k.op("scalar", lambda e: e.activation(out=eap, in_=pz, func=AF.Exp), reads=[pzb], writes=[eb])
                yield
                spb, spap = self.sp[s].next()
                k.op("scalar", lambda e: e.activation(out=spap, in_=eap, func=AF.Ln, bias=self.one),
                     reads=[eb, self.smallb], writes=[spb])
                yield
                k.op("tensor", lambda e: e.matmul(pz, lhsT=self.tri, rhs=spap, start=True, stop=(lsum is None)),
                     reads=[cb, spb], writes=[pzb])
                if lsum is not None:
                    k.op("tensor", lambda e: e.matmul(pz, lhsT=self.ones, rhs=lsum[1], start=False, stop=True),
                         reads=[cb, lsum[0]], writes=[pzb])
                yield
                k.op("scalar", lambda e: e.activation(out=pz, in_=pz, func=AF.Exp, scale=-1.0), reads=[pzb], writes=[pzb])
                yield
                wb, wap = self.w[s].next()
                k.op("vector", lambda e: e.tensor_tensor(out=wap, in0=pz, in1=eap, op=ALU.mult), reads=[pzb, eb], writes=[wb])
                if idx < len(kbs) - 1:
                    if lsum is None:
                        lsum = (spb, spap)
                    else:
                        lb, lap = self.ls[s].next()
                        k.op("vector", lambda e: e.tensor_tensor(out=lap, in0=lsum[1], in1=spap, op=ALU.add),
                             reads=[lsum[0], spb], writes=[lb])
                        lsum = (lb, lap)
                yield
                k.op("tensor", lambda e: e.matmul(po[0:64, :], lhsT=vap[:, kb * 64:(kb + 1) * 64], rhs=wap,
                                                  start=(idx == 0), stop=(idx == len(kbs) - 1)),
                     reads=[vb, wb], writes=[pob])
                yield
            ob, oap = self.ost.next()
            k.op("vector", lambda e: e.tensor_copy(out=oap, in_=po[0:64, :]), reads=[pob], writes=[ob])
            if callable(D["oT"]):
                dst = D["oT"](h * 64, (h + 1) * 64)[:, q_sl]
            else:
                dst = D["oT"][h * 64:(h + 1) * 64, q_sl]
            k.dma("sync", [(dst, oap)], reads=[ob], final=self.final)
            yield

    def swa(self):
        k, D = self.k, self.D
        cb = self.cb
        alloc = self.alloc
        qslots = [self.qT.next() for _ in range(4)]
        knb, kn = self.kT.next()
        sq = RR([(k.buf("ssq%d" % i), alloc("ssq%d" % i, [64, 512], BF16)[:, :]) for i in range(2)])
        t32 = RR([(k.buf("st32_%d" % i), alloc("st32_%d" % i, [64, 512], F32)[:, :]) for i in range(3)])
        vb, vap = self.v.next()
        sb, sap = self.stage.next()
        k.dma("sync", [(sap[:, 0:NBLK * 64].rearrange("p (b d) -> p b d", d=64), D["bv"])], writes=[sb])
        k.op("gpsimd", lambda e: e.tensor_copy(out=vap, in_=sap[:, 0:NBLK * 64]), reads=[sb], writes=[vb])
        pnorm = RR([self.psb[6], self.psb[7]])
        pz_rr = RR([self.psb[0], self.psb[1]])
        po_rr = RR([self.psb[2], self.psb[3]])
        pd_rr = RR([self.psb[4], self.psb[5]])

        def qknorm(src_dram, dst_ap, dst_buf, gain, lnbias):
            sb, sap = self.stage.next()
            k.dma("sync", [(sap[0:64, :], src_dram)], writes=[sb])
            for tt in range(8):
                ts = slice(tt * 512, (tt + 1) * 512)
                qb_, qap_ = sq.next()
                k.op("scalar", lambda e: e.activation(out=qap_, in_=sap[0:64, ts], func=AF.Square), reads=[sb], writes=[qb_])
                pb, pap = pnorm.next()
                k.op("tensor", lambda e: e.matmul(pap[0:64, :], lhsT=self.ones[0:64, 0:64], rhs=qap_, start=True, stop=True),
                     reads=[qb_, cb], writes=[pb])
                tb, tap = t32.next()
                k.op("scalar", lambda e: e.activation(out=tap, in_=pap[0:64, :], func=AF.Ln, scale=1.0 / 64.0, bias=self.eps[0:64, :]),
                     reads=[pb, self.smallb], writes=[tb])
                k.op("scalar", lambda e: e.activation(out=tap, in_=tap, func=AF.Exp, scale=-0.5, bias=lnbias[0:64, :]),
                     reads=[tb, self.smallb], writes=[tb])
                k.op("vector", lambda e: e.scalar_tensor_tensor(out=dst_ap[:, ts], in0=sap[0:64, ts], scalar=gain, in1=tap,
                                                                op0=ALU.mult, op1=ALU.mult),
                     reads=[sb, tb, self.smallb], writes=[dst_buf])

        qknorm(D["bkT"], kn, knb, self.gk, self.zero)
        for hl in range(4):
            qknorm(D["bqT"][hl * 64:(hl + 1) * 64, :], qslots[hl][1], qslots[hl][0], self.gq, self.ln8)
        qnb = [qslots[hl][0] for hl in range(4)]

        pw = RR([(k.buf("pw%d" % i), alloc("pw%d" % i, [P, 512], BF16)[:, :]) for i in range(3)])
        ost = RR([(k.buf("so%d" % i), alloc("so%d" % i, [64, 4, 512], F32)) for i in range(2)])
        den = RR([(k.buf("dn%d" % i), alloc("dn%d" % i, [64, 512], F32)[:, :]) for i in range(2)])
        for qg in range(8):
            osb, osap = ost.next()
            for qi in range(4):
                qb = qg * 4 + qi
                q_sl = slice(qb * 128, (qb + 1) * 128)
                pob, po = po_rr.next()
                pdb, pd = pd_rr.next()
                kbl = [qb] if qb == 0 else [qb - 1, qb]
                for ii, kb in enumerate(kbl):
                    k_sl = slice(kb * 128, (kb + 1) * 128)
                    which = 1 if kb == qb else 0
                    pzb, pz = pz_rr.next()
                    for hl in range(4):
                        k.op("tensor", lambda e: e.matmul(pz[:, hl * 128:(hl + 1) * 128], lhsT=kn[:, k_sl], rhs=qslots[hl][1][:, q_sl],
                                                          start=(hl == 0), stop=False, skip_group_check=True),
                             reads=[knb, qnb[hl]], writes=[pzb])
                    k.op("tensor", lambda e: e.matmul(pz, lhsT=self.ident, rhs=self.swab[which], start=False, stop=True,
                                                      skip_group_check=True),
                         reads=[cb], writes=[pzb])
                    wb, wap = pw.next()
                    k.op("scalar", lambda e: e.activation(out=wap, in_=pz, func=AF.Exp, bias=self.nshift),
                         reads=[pzb, self.smallb], writes=[wb])
                    k.op("tensor", lambda e: e.matmul(po[0:64, :], lhsT=vap[:, kb * 64:(kb + 1) * 64], rhs=wap,
                                                      start=(ii == 0), stop=(ii == len(kbl) - 1)),
                         reads=[vb, wb], writes=[pob])
                    k.op("tensor", lambda e: e.matmul(pd[0:64, :], lhsT=self.ones[:, 0:64], rhs=wap,
                                                      start=(ii == 0), stop=(ii == len(kbl) - 1)),
                         reads=[cb, wb], writes=[pdb])
                db, dap = den.next()
                k.op("vector", lambda e: e.tensor_tensor(out=dap, in0=pd[0:64, :], in1=self.sinks, op=ALU.add),
                     reads=[pdb, self.smallb], writes=[db])
                k.op("vector", lambda e: e.reciprocal(out=dap, in_=dap), reads=[db], writes=[db])
                k.op("vector", lambda e: e.tensor_tensor(out=osap[:, :, qi * 128:(qi + 1) * 128],
                                                         in0=po[0:64, :].rearrange("p (h q) -> p h q", h=4),
                                                         in1=dap.rearrange("p (h q) -> p h q", h=4), op=ALU.mult),
                     reads=[pob, db], writes=[osb])
            if callable(D["oT"]):
                pairs = [(D["oT"]((4 + hl) * 64, (5 + hl) * 64)[:, qg * 512:(qg + 1) * 512], osap[:, hl, :]) for hl in range(4)]
            else:
                pairs = [(D["oT"][(4 + hl) * 64:(5 + hl) * 64, qg * 512:(qg + 1) * 512], osap[:, hl, :]) for hl in range(4)]
            k.dma("sync", pairs, reads=[osb], final=self.final)

    def emit(self, finish=True):
        self.consts()
        tiles = [self.load_sb_head(h) for h in range(4)]
        streams = [self.sb_stream(s, s, tiles[s]) for s in range(4)]
        while streams:
            for g in list(streams):
                try:
                    next(g)
                except StopIteration:
                    streams.remove(g)
        self.swa()
        if finish:
            self.k.finish()


def attn_consts(half):
    n = 128 * 4 + 4 * 512 + 2 * 512
    c = np.zeros((128, n), np.float32)
    kk = np.arange(128)[:, None]
    qq = np.arange(128)[None, :]
    c[:, 0:128] = (kk >= qq)
    c[:, 128:256] = np.eye(128)
    c[:, 256:384] = -np.eye(128)
    c[:, 384:512] = 1.0
    ql = np.arange(512)[None, :]
    for j in range(4):
        c[:, 512 + j * 512:512 + (j + 1) * 512] = np.where(j * 128 + kk >= ql, BIG, 0.0)
    for hl in range(4):
        slope = 2.0 ** (-(4 * half + hl + 1))
        dist_prev = qq + 128 - kk
        dist_cur = qq - kk
        c[:, 2560 + hl * 128:2560 + (hl + 1) * 128] = np.where(kk > qq, -slope * dist_prev, -BIG)
        c[:, 3072 + hl * 128:3072 + (hl + 1) * 128] = np.where(kk <= qq, -slope * dist_cur, -BIG)
    return c
GC = 128
NCH = 32
GBIG = 30000.0
LN_QSCALE = -0.5 * 4.852030263919617


def run_streams(streams):
    streams = list(streams)
    while streams:
        for s in list(streams):
            try:
                next(s)
            except StopIteration:
                streams.remove(s)


class GdnProg:
    def __init__(self, nc, es, D, safe_same=True, inv_fp32=True, k=None, pfx="", fused=False):
        self.nc = nc
        self.es = es
        self.fused = fused
        k = self.k = k if k is not None else K(nc, es, safe_same=safe_same)
        self.D = D
        self.inv_fp32 = inv_fp32

        def alloc(name, shape, dt):
            return es.enter_context(nc.sbuf_tensor(pfx + "sg_" + name, shape, dt))

        def rr(name, shape, dt, n):
            t = alloc(name, [shape[0], n] + list(shape[1:]), dt)
            return RR([(k.buf("%s%d" % (name, i)), t[:, i]) for i in range(n)])

        self.alloc = alloc
        self.rr = rr
        self.c32 = alloc("c32", [P, 6 * 128], F32)
        self.cb = k.buf("c32")
        self.identb = alloc("identb", [P, P], BF16)
        self.onesb = alloc("onesb", [P, P], BF16)
        self.small = alloc("small", [P, 8], F32)
        self.convw = alloc("convw", [P, 64], F32)
        self.diagw = alloc("diagw", [P, 64, P], BF16)
        self.gt = alloc("gates", [P, 6, 256], F32)
        self.gb = k.buf("gates")
        self.xst = rr("xst", [P, 515], F32, 2)
        self.xb = rr("xb", [P, 515], BF16, 2)
        qT = alloc("qT", [P, 2, 4, 512], BF16)
        kT = alloc("kT", [P, 2, 4, 512], BF16)
        vt = alloc("vt", [P, 2, 4, 8, P], BF16)
        self.qT, self.kT, self.vt = qT, kT, vt
        self.qTb = [[k.buf("qT%d_%d" % (s, m)) for m in range(4)] for s in range(2)]
        self.kTb = [[k.buf("kT%d_%d" % (s, m)) for m in range(4)] for s in range(2)]
        self.vtb = [[k.buf("vt%d_%d" % (s, h)) for h in range(8)] for s in range(2)]
        self.e32 = rr("e32", [P, 512], F32, 3)
        self.y32 = rr("y32", [P, 512], F32, 2)
        self.yb = rr("yb", [P, 512], BF16, 2)
        self.sq = rr("sq", [P, 512], BF16, 2)
        self.r32 = rr("r32", [P, 512], F32, 2)
        self.egc = rr("egc", [P, 24], F32, 3)
        self.Xt = alloc("Xt", [P, 2, 8, P], BF16)
        self.AT = alloc("AT", [P, 2, 8, P], BF16)
        self.kh = alloc("kh", [P, 2, 8, P], BF16)
        self.Xtb = [[k.buf("Xt%d_%d" % (s, h)) for h in range(8)] for s in range(2)]
        self.ATb = [[k.buf("AT%d_%d" % (s, h)) for h in range(8)] for s in range(2)]
        self.khb = [[k.buf("kh%d_%d" % (s, h)) for h in range(8)] for s in range(2)]
        self.f32t = rr("f32t", [P, P], F32, 48)
        self.S = alloc("S", [P, 8, P], F32)
        self.Sbf = alloc("Sbf", [P, 8, P], BF16)
        self.Sb = [k.buf("S%d" % h) for h in range(8)]
        self.Sbfb = [k.buf("Sbf%d" % h) for h in range(8)]
        self.Rb = rr("R", [P, P], BF16, 8)
        self.vn = rr("vn", [P, P], BF16, 8)
        self.tmp = rr("tmp", [P, P], F32, 8)
        self.ost = alloc("ost", [P, 2, 8, P], F32)
        self.ostT = alloc("ostT", [P, 2, 8, P], F32)
        self.ostTb = [[k.buf("ostT%d_%d" % (s, h)) for h in range(8)] for s in range(2)]
        self.ostb = [[k.buf("ost%d_%d" % (s, h)) for h in range(8)] for s in range(2)]
        pc = [es.enter_context(nc.psum_tensor(pfx + "pgc%d" % i, [P, 512], F32)) for i in range(1)]
        self.pc = RR([(k.buf("pgc%d" % i), pc[i][:, :]) for i in range(1)])
        pb = es.enter_context(nc.psum_tensor(pfx + "pgb", [P, 1024], BF16))
        pbb = k.buf("pgb")
        pbb.excl = True
        self.pvt = (pbb, pb[:, 0:512])
        self.pkt = RR([(pbb, pb[:, 512 + i * 128:512 + (i + 1) * 128]) for i in range(3)])
        self.fence_ap = pb[0:1, 896:1024]
        self.pbb = pbb
        pq = [es.enter_context(nc.psum_tensor(pfx + "pgq%d" % i, [P, 512], F32)) for i in range(6)]
        bq = [k.buf("pgq%d" % i) for i in range(6)]
        for b in bq + [self.pc.items[0][0]]:
            b.excl = True
        self.pqb = RR([(bq[i], pq[i]) for i in range(3)])
        self.psqb = RR([(bq[i], pq[i]) for i in range(3, 6)])

    def fence(self, banks):
        self.k.op("tensor", lambda e: e.transpose(self.fence_ap, self.identb[:, 0:1], self.identb[:]),
                  reads=[self.cb], writes=[self.pbb] + list(banks))

    def phase0(self):
        k, D = self.k, self.D
        k.dma("sync", [(self.c32[:], D["c32"])], writes=[self.cb])
        c = self.c32
        self.LE = c[:, 0:128]
        self.GT = c[:, 128:256]
        self.MASKB = c[:, 256:384]
        self.I32 = c[:, 384:512]
        self.ONES32 = c[:, 512:640]
        self.STRICT = c[:, 640:768]
        k.op("vector", lambda e: e.tensor_copy(out=self.identb[:], in_=self.I32), reads=[self.cb], writes=[self.cb])
        k.op("vector", lambda e: e.tensor_copy(out=self.onesb[:], in_=self.ONES32), reads=[self.cb], writes=[self.cb])
        k.dma("sync", [(self.small[:], D["small"]), (self.convw[:], D["convw"])], writes=[self.cb])
        self.one = self.small[:, 0:1]
        self.eps = self.small[:, 1:2]
        self.lnq = self.small[:, 2:3]
        self.zero = self.small[:, 3:4]
        for i in range(64):
            k.op("gpsimd", lambda e: e.tensor_scalar(out=self.diagw[:, i, :], in0=self.identb[:], scalar1=self.convw[:, i:i + 1],
                                                     scalar2=None, op0=ALU.mult), reads=[self.cb], writes=[self.cb])
        g = self.gt
        if self.fused:
            gt = D["gtok"]
            k.dma("sync", [(g[:, 0, :].rearrange("p (c h) -> p c h", h=8), gt[:, 0:8].rearrange("(c p) h -> p c h", p=P)),
                           (g[:, 1, :].rearrange("p (c h) -> p c h", h=8), gt[:, 8:16].rearrange("(c p) h -> p c h", p=P)),
                           (g[:, 2:4, :], D["gconst"])], writes=[self.gb])
        else:
            k.dma("sync", [(g[:, 0:4, :], D["gates"])], writes=[self.gb])
        A, BL, DTB, ALOG, G, BETA = (g[:, i, :] for i in range(6))
        gb = [self.gb]
        k.op("vector", lambda e: e.tensor_tensor(out=A, in0=A, in1=DTB, op=ALU.add), reads=gb, writes=gb)
        k.op("scalar", lambda e: e.activation(out=A, in_=A, func=AF.Exp), reads=gb, writes=gb)
        k.op("scalar", lambda e: e.activation(out=A, in_=A, func=AF.Ln, bias=self.one), reads=gb + [self.cb], writes=gb)
        k.op("scalar", lambda e: e.activation(out=ALOG, in_=ALOG, func=AF.Exp), reads=gb, writes=gb)
        k.op("vector", lambda e: e.scalar_tensor_tensor(out=G, in0=A, scalar=-1.0, in1=ALOG, op0=ALU.mult, op1=ALU.mult),
             reads=gb, writes=gb)
        k.op("scalar", lambda e: e.activation(out=BL, in_=BL, func=AF.Exp, scale=-1.0), reads=gb, writes=gb)
        k.op("vector", lambda e: e.tensor_scalar(out=BL, in0=BL, scalar1=1.0, scalar2=None, op0=ALU.add), reads=gb, writes=gb)
        k.op("vector", lambda e: e.reciprocal(out=BETA, in_=BL), reads=gb, writes=gb)
        self.G, self.BETA = G, BETA
        k.op("vector", lambda e: e.memset(self.S[:], 0.0), writes=self.Sb)
        k.op("vector", lambda e: e.memset(self.Sbf[:], 0.0), writes=self.Sbfb)

    def prologue_rows(self, t, rows):
        k, D = self.k, self.D
        slot = t % 2
        cb = self.cb
        for r in rows:
            sb, sap = self.xst.next()
            if not self.fused:
                k.dma("sync", [(sap, D["xT"][r * P:(r + 1) * P, t * 512:t * 512 + 515])], writes=[sb])
            elif t == 0:
                k.op("gpsimd", lambda e: e.memset(sap[:, 0:3], 0.0), writes=[sb])
                k.dma("sync", [(sap[:, 3:515], D["xT"][r * P:(r + 1) * P, 0:512])], writes=[sb])
            else:
                k.dma("sync", [(sap, D["xT"][r * P:(r + 1) * P, t * 512 - 3:t * 512 + 512])], writes=[sb])
            xbb, xbap = self.xb.next()
            k.op("gpsimd", lambda e: e.tensor_copy(out=xbap, in_=sap), reads=[sb], writes=[xbb])
            yield
            pcb, pcap = self.pc.next()
            for j in range(4):
                k.op("tensor", lambda e: e.matmul(pcap, lhsT=self.diagw[:, r * 4 + j, :], rhs=xbap[:, j:j + 512],
                                                  start=(j == 0), stop=(j == 3)), reads=[cb, xbb], writes=[pcb])
            yield
            eb, eap = self.e32.next()
            k.op("scalar", lambda e: e.activation(out=eap, in_=pcap, func=AF.Exp, scale=-1.0), reads=[pcb], writes=[eb])
            yield
            k.op("scalar", lambda e: e.activation(out=eap, in_=eap, func=AF.Ln, bias=self.one), reads=[eb, cb], writes=[eb])
            yield
            k.op("scalar", lambda e: e.activation(out=eap, in_=eap, func=AF.Exp, scale=-1.0), reads=[eb], writes=[eb])
            yield
            if r < 8:
                yb_, yap = self.y32.next()
            else:
                yb_, yap = self.yb.next()
            k.op("vector", lambda e: e.tensor_tensor(out=yap, in0=pcap, in1=eap, op=ALU.mult), reads=[pcb, eb], writes=[yb_])
            yield
            if r < 8:
                m = r % 4
                qb_, qap_ = self.sq.next()
                k.op("scalar", lambda e: e.activation(out=qap_, in_=yap, func=AF.Square), reads=[yb_], writes=[qb_])
                yield
                pnb, pnap = self.pc.next()
                k.op("tensor", lambda e: e.matmul(pnap, lhsT=self.onesb[:], rhs=qap_, start=True, stop=True),
                     reads=[qb_, cb], writes=[pnb])
                yield
                rb, rap = self.r32.next()
                k.op("scalar", lambda e: e.activation(out=rap, in_=pnap, func=AF.Ln, bias=self.eps), reads=[pnb, cb], writes=[rb])
                yield
                bias = self.lnq if r < 4 else self.zero
                k.op("scalar", lambda e: e.activation(out=rap, in_=rap, func=AF.Exp, scale=-0.5, bias=bias), reads=[rb, cb], writes=[rb])
                yield
                if r < 4:
                    dst, dbuf = self.qT[:, slot, m, :], self.qTb[slot][m]
                else:
                    dst, dbuf = self.kT[:, slot, m, :], self.kTb[slot][m]
                k.op("vector", lambda e: e.tensor_tensor(out=dst, in0=yap, in1=rap, op=ALU.mult), reads=[yb_, rb], writes=[dbuf])
                yield
            else:
                hv = r - 8
                pvb, pvap = self.pvt
                for cc in range(4):
                    k.op("tensor", lambda e: e.transpose(pvap[:, cc * P:(cc + 1) * P], yap[:, cc * P:(cc + 1) * P], self.identb[:]),
                         reads=[yb_, cb], writes=[pvb])
                yield
                k.op("vector", lambda e: e.tensor_copy(out=self.vt[:, slot, :, hv, :], in_=pvap.rearrange("p (c d) -> p c d", c=4)),
                     reads=[pvb], writes=[self.vtb[slot][hv]])
                yield

    def pre(self, c):
        k = self.k
        cb = self.cb
        t, c4 = c // 4, c % 4
        slot = t % 2
        cs = c % 2
        csl = slice(c4 * P, (c4 + 1) * P)
        egb, egap = self.egc.next()
        self.egcur = getattr(self, "egcur", {})
        self.egcur[c] = (egb, egap)
        pb, pap = self.pqb.next()
        k.op("tensor", lambda e: e.matmul(pap[:, 0:8], lhsT=self.LE, rhs=self.G[:, c * 8:(c + 1) * 8], start=True, stop=True),
             reads=[cb, self.gb], writes=[pb])
        k.op("tensor", lambda e: e.matmul(pap[:, 8:16], lhsT=self.ONES32, rhs=self.G[:, c * 8:(c + 1) * 8], start=True, stop=True),
             reads=[cb, self.gb], writes=[pb])
        self.fence([pb])
        k.op("scalar", lambda e: e.activation(out=egap[:, 0:16], in_=pap[:, 0:16], func=AF.Exp), reads=[pb], writes=[egb])
        k.op("vector", lambda e: e.tensor_scalar(out=egap[:, 16:24], in0=egap[:, 0:8], scalar1=-1.0, scalar2=None, op0=ALU.mult),
             reads=[egb], writes=[egb])
        yield
        for grp in range(2):
            heads = list(range(4 * grp, 4 * grp + 4))
            kheads = [2 * grp, 2 * grp + 1]
            Gm = {}
            for h in heads:
                Gm[h] = self.f32t.next()
                k.op("gpsimd", lambda e: e.tensor_scalar(out=Gm[h][1], in0=self.GT, scalar1=self.G[:, c * 8 + h:c * 8 + h + 1],
                                                         scalar2=None, op0=ALU.mult), reads=[cb, self.gb], writes=[Gm[h][0]])
            yield
            pZ = {}
            bk = self.pqb.next()
            for h in heads:
                pZ[h] = (bk[0], bk[1][:, (h % 4) * P:(h % 4 + 1) * P])
                k.op("tensor", lambda e: e.matmul(pZ[h][1], lhsT=Gm[h][1], rhs=self.LE, start=True, stop=False),
                     reads=[Gm[h][0], cb], writes=[pZ[h][0]])
                k.op("tensor", lambda e: e.matmul(pZ[h][1], lhsT=self.I32, rhs=self.MASKB, start=False, stop=True),
                     reads=[cb], writes=[pZ[h][0]])
            self.fence([bk[0]])
            yield
            Dm = {}
            for h in heads:
                Dm[h] = self.f32t.next()
                k.op("scalar", lambda e: e.activation(out=Dm[h][1], in_=pZ[h][1], func=AF.Exp), reads=[pZ[h][0]], writes=[Dm[h][0]])
            yield
            pKQ, pKK, pkt, Dms = {}, {}, {}, {}
            bk = self.pqb.next()
            for m in kheads:
                kTc = self.kT[:, slot, m, csl]
                qTc = self.qT[:, slot, m, csl]
                pKQ[m] = (bk[0], bk[1][:, (m % 2) * P:(m % 2 + 1) * P])
                k.op("tensor", lambda e: e.matmul(pKQ[m][1], lhsT=kTc, rhs=qTc, start=True, stop=True),
                     reads=[self.kTb[slot][m], self.qTb[slot][m]], writes=[pKQ[m][0]])
                pKK[m] = (bk[0], bk[1][:, (2 + m % 2) * P:(3 + m % 2) * P])
                k.op("tensor", lambda e: e.matmul(pKK[m][1], lhsT=kTc, rhs=kTc, start=True, stop=True),
                     reads=[self.kTb[slot][m]], writes=[pKK[m][0]])
                pkt[m] = self.pkt.next()
                k.op("tensor", lambda e: e.transpose(pkt[m][1], kTc, self.identb[:]), reads=[self.kTb[slot][m], cb], writes=[pkt[m][0]])
            for h in heads:
                Dms[h] = self.f32t.next()
                k.op("gpsimd", lambda e: e.tensor_tensor(out=Dms[h][1], in0=Dm[h][1], in1=self.STRICT, op=ALU.mult),
                     reads=[Dm[h][0], cb], writes=[Dms[h][0]])
            yield
            Pt, Q_, W = {}, {}, {}
            for h in heads:
                m = h // 2
                k.op("vector", lambda e: e.tensor_tensor(out=self.AT[:, cs, h, :], in0=pKQ[m][1], in1=Dm[h][1], op=ALU.mult),
                     reads=[pKQ[m][0], Dm[h][0]], writes=[self.ATb[cs][h]])
                k.op("vector", lambda e: e.tensor_scalar(out=self.kh[:, cs, h, :], in0=pkt[m][1], scalar1=Dm[h][1][:, 127:128],
                                                         scalar2=None, op0=ALU.mult),
                     reads=[pkt[m][0], Dm[h][0]], writes=[self.khb[cs][h]])
                Pt[h] = self.f32t.next()
                k.op("vector", lambda e: e.scalar_tensor_tensor(out=Pt[h][1], in0=pKK[m][1], scalar=self.BETA[:, c * 8 + h:c * 8 + h + 1],
                                                                in1=Dms[h][1], op0=ALU.mult, op1=ALU.mult),
                     reads=[pKK[m][0], self.gb, Dms[h][0]], writes=[Pt[h][0]])
            yield
            pT = {}
            bk = self.pqb.next()
            for h in heads:
                pT[h] = (bk[0], bk[1][:, (h % 4) * P:(h % 4 + 1) * P])
                k.op("tensor", lambda e: e.transpose(pT[h][1], Pt[h][1], self.I32), reads=[Pt[h][0], cb], writes=[pT[h][0]])
            self.fence([bk[0]])
            yield
            for h in heads:
                Q_[h] = self.f32t.next()
                k.op("scalar", lambda e: e.copy(out=Q_[h][1], in_=pT[h][1]), reads=[pT[h][0]], writes=[Q_[h][0]])
                W[h] = self.f32t.next()
                k.op("gpsimd", lambda e: e.tensor_tensor(out=W[h][1], in0=self.I32, in1=Pt[h][1], op=ALU.subtract),
                     reads=[cb, Pt[h][0]], writes=[W[h][0]])
            yield
            pP, pQ = {}, {}
            bkP = self.pqb.next()
            bkQ = self.pqb.next()
            for h in heads:
                pQ[h] = (bkQ[0], bkQ[1][:, (h % 4) * P:(h % 4 + 1) * P])
                k.op("tensor", lambda e: e.matmul(pQ[h][1], lhsT=Pt[h][1], rhs=Q_[h][1], start=True, stop=True),
                     reads=[Q_[h][0], Pt[h][0]], writes=[pQ[h][0]])
            for h in heads:
                pP[h] = (bkP[0], bkP[1][:, (h % 4) * P:(h % 4 + 1) * P])
                k.op("tensor", lambda e: e.matmul(pP[h][1], lhsT=Q_[h][1], rhs=Pt[h][1], start=True, stop=True),
                     reads=[Q_[h][0], Pt[h][0]], writes=[pP[h][0]])
            self.fence([bkP[0], bkQ[0]])
            yield
            nP, nQ = {}, {}
            for h in heads:
                nP[h] = self.f32t.next()
                k.op("scalar", lambda e: e.copy(out=nP[h][1], in_=pP[h][1]), reads=[pP[h][0]], writes=[nP[h][0]])
                nQ[h] = self.f32t.next()
                k.op("vector", lambda e: e.tensor_copy(out=nQ[h][1], in_=pQ[h][1]), reads=[pQ[h][0]], writes=[nQ[h][0]])
            Pt, Q_ = nP, nQ
            yield
            for step in range(1, 7):
                pW, pP, pQ = {}, {}, {}
                bkW = self.pqb.next()
                for h in heads:
                    pW[h] = (bkW[0], bkW[1][:, (h % 4) * P:(h % 4 + 1) * P])
                    k.op("tensor", lambda e: e.matmul(pW[h][1], lhsT=Q_[h][1], rhs=W[h][1], start=True, stop=True),
                         reads=[Q_[h][0], W[h][0]], writes=[pW[h][0]])
                if step <= 5:
                    bkQ = self.pqb.next()
                    for h in heads:
                        pQ[h] = (bkQ[0], bkQ[1][:, (h % 4) * P:(h % 4 + 1) * P])
                        k.op("tensor", lambda e: e.matmul(pQ[h][1], lhsT=Pt[h][1], rhs=Q_[h][1], start=True, stop=True),
                             reads=[Q_[h][0], Pt[h][0]], writes=[pQ[h][0]])
                if step <= 4:
                    bkP = self.pqb.next()
                    for h in heads:
                        pP[h] = (bkP[0], bkP[1][:, (h % 4) * P:(h % 4 + 1) * P])
                        k.op("tensor", lambda e: e.matmul(pP[h][1], lhsT=Q_[h][1], rhs=Pt[h][1], start=True, stop=True),
                             reads=[Q_[h][0], Pt[h][0]], writes=[pP[h][0]])
                self.fence([bkW[0]] + ([bkQ[0]] if step <= 5 else []) + ([bkP[0]] if step <= 4 else []))
                yield
                nW, nP, nQ = {}, {}, {}
                for h in heads:
                    if step < 6:
                        nW[h] = self.f32t.next()
                        k.op("vector", lambda e: e.tensor_tensor(out=nW[h][1], in0=pW[h][1], in1=W[h][1], op=ALU.add),
                             reads=[pW[h][0], W[h][0]], writes=[nW[h][0]])
                    else:
                        k.op("vector", lambda e: e.tensor_tensor(out=self.Xt[:, cs, h, :], in0=pW[h][1], in1=W[h][1], op=ALU.add),
                             reads=[pW[h][0], W[h][0]], writes=[self.Xtb[cs][h]])
                    if step <= 5:
                        nQ[h] = self.f32t.next()
                        k.op("scalar", lambda e: e.copy(out=nQ[h][1], in_=pQ[h][1]), reads=[pQ[h][0]], writes=[nQ[h][0]])
                    if step <= 4:
                        nP[h] = self.f32t.next()
                        k.op("scalar", lambda e: e.copy(out=nP[h][1], in_=pP[h][1]), reads=[pP[h][0]], writes=[nP[h][0]])
                W, Pt, Q_ = nW, nP, nQ
                yield

    def seq(self, c):
        for grp in range(2):
            yield from self.seq_grp(c, grp)

    def seq_grp(self, c, grp):
        k, D = self.k, self.D
        t, c4 = c // 4, c % 4
        slot = t % 2
        cs = c % 2
        csl = slice(c4 * P, (c4 + 1) * P)
        heads = list(range(4 * grp, 4 * grp + 4))
        egb, egap = self.egcur[c]
        p1, pa = {}, {}
        bk1 = self.psqb.next()
        bka = self.psqb.next()
        for h in heads:
            m = h // 2
            p1[h] = (bk1[0], bk1[1][:, (h % 4) * P:(h % 4 + 1) * P])
            k.op("tensor", lambda e: e.matmul(p1[h][1], lhsT=self.kT[:, slot, m, csl], rhs=self.Sbf[:, h, :], start=True, stop=True),
                 reads=[self.kTb[slot][m], self.Sbfb[h]], writes=[p1[h][0]])
            pa[h] = (bka[0], bka[1][:, (h % 4) * P:(h % 4 + 1) * P])
            k.op("tensor", lambda e: e.matmul(pa[h][1], lhsT=self.qT[:, slot, m, csl], rhs=self.Sbf[:, h, :], start=True, stop=True),
                 reads=[self.qTb[slot][m], self.Sbfb[h]], writes=[pa[h][0]])
        yield
        R, tmp = {}, {}
        for h in heads:
            R[h] = self.Rb.next()
            k.op("vector", lambda e: e.scalar_tensor_tensor(out=R[h][1], in0=p1[h][1], scalar=egap[:, 16 + h:17 + h],
                                                            in1=self.vt[:, slot, c4, h, :], op0=ALU.mult, op1=ALU.add),
                 reads=[p1[h][0], egb, self.vtb[slot][h]], writes=[R[h][0]])
            tmp[h] = self.tmp.next()
            k.op("scalar", lambda e: e.activation(out=tmp[h][1], in_=pa[h][1], func=AF.Copy, scale=egap[:, h:h + 1]),
                 reads=[pa[h][0], egb], writes=[tmp[h][0]])
        yield
        p2 = {}
        bk2 = self.psqb.next()
        for h in heads:
            p2[h] = (bk2[0], bk2[1][:, (h % 4) * P:(h % 4 + 1) * P])
            k.op("tensor", lambda e: e.matmul(p2[h][1], lhsT=self.Xt[:, cs, h, :], rhs=R[h][1], start=True, stop=True),
                 reads=[self.Xtb[cs][h], R[h][0]], writes=[p2[h][0]])
        yield
        vn = {}
        for h in heads:
            vn[h] = self.vn.next()
            k.op("scalar", lambda e: e.activation(out=vn[h][1], in_=p2[h][1], func=AF.Copy, scale=self.BETA[:, c * 8 + h:c * 8 + h + 1]),
                 reads=[p2[h][0], self.gb], writes=[vn[h][0]])
        yield
        pb_, p3 = {}, {}
        bkb = self.psqb.next()
        bk3 = self.psqb.next()
        for h in heads:
            pb_[h] = (bkb[0], bkb[1][:, (h % 4) * P:(h % 4 + 1) * P])
            k.op("tensor", lambda e: e.matmul(pb_[h][1], lhsT=self.AT[:, cs, h, :], rhs=vn[h][1], start=True, stop=True),
                 reads=[self.ATb[cs][h], vn[h][0]], writes=[pb_[h][0]])
            p3[h] = (bk3[0], bk3[1][:, (h % 4) * P:(h % 4 + 1) * P])
            k.op("tensor", lambda e: e.matmul(p3[h][1], lhsT=self.kh[:, cs, h, :], rhs=vn[h][1], start=True, stop=True),
                 reads=[self.khb[cs][h], vn[h][0]], writes=[p3[h][0]])
        yield
        for h in heads:
            k.op("vector", lambda e: e.tensor_tensor(out=self.ost[:, cs, h, :], in0=pb_[h][1], in1=tmp[h][1], op=ALU.add),
                 reads=[pb_[h][0], tmp[h][0]], writes=[self.ostb[cs][h]])
            k.op("vector", lambda e: e.scalar_tensor_tensor(out=self.S[:, h, :], in0=self.S[:, h, :], scalar=egap[:, 8 + h:9 + h],
                                                            in1=p3[h][1], op0=ALU.mult, op1=ALU.add),
                 reads=[self.Sb[h], egb, p3[h][0]], writes=[self.Sb[h]])
        yield
        for h in heads:
            k.op("gpsimd", lambda e: e.tensor_copy(out=self.Sbf[:, h, :], in_=self.S[:, h, :]), reads=[self.Sb[h]], writes=[self.Sbfb[h]])
        if not self.fused:
            if grp == 1:
                k.dma("sync", [(D["o_tok"][c * P:(c + 1) * P, :], self.ost[:, cs, :, :].rearrange("p h d -> p (h d)"))],
                      reads=self.ostb[cs], final=True)
            yield
            return
        bkt = self.psqb.next()
        for h in heads:
            k.op("tensor", lambda e: e.transpose(bkt[1][:, (h % 4) * P:(h % 4 + 1) * P], self.ost[:, cs, h, :], self.I32),
                 reads=[self.ostb[cs][h], self.cb], writes=[bkt[0]])
        self.fence([bkt[0]])
        yield
        for h in heads:
            k.op("scalar", lambda e: e.copy(out=self.ostT[:, cs, h, :], in_=bkt[1][:, (h % 4) * P:(h % 4 + 1) * P]),
                 reads=[bkt[0]], writes=[self.ostTb[cs][h]])
        if grp == 1:
            k.dma("sync", [(D["oT"](h * P, (h + 1) * P)[:, c * P:(c + 1) * P], self.ostT[:, cs, h, :]) for h in range(8)],
                  reads=self.ostTb[cs])
        yield

    def emit(self, nchunks=NCH, stop=None, finish=True):
        self.phase0()
        if stop == "phase0":
            self.k.finish(); return
        run_streams([self.prologue_rows(0, range(16) if stop != "pro1" else [0, 8])])
        if stop in ("pro", "pro1"):
            self.k.finish(); return
        if stop is not None and stop.startswith("pre"):
            n = int(stop[3:] or 1000)
            g = self.pre(0)
            for _ in range(n):
                try:
                    next(g)
                except StopIteration:
                    break
            self.k.finish(); return
        run_streams([self.pre(0)])
        pro = []
        ntiles = (nchunks + 3) // 4
        for c in range(nchunks):
            t = c // 4
            if c % 4 == 0 and t + 1 < ntiles:
                pro = [self.prologue_rows(t + 1, range(16))]
            main = [self.seq(c)]
            if c + 1 < nchunks and (c % 4 != 3):
                main.append(self.pre(c + 1))
            streams = list(main)
            while streams:
                for s_ in list(streams):
                    try:
                        next(s_)
                    except StopIteration:
                        streams.remove(s_)
                for s_ in list(pro):
                    try:
                        next(s_)
                        next(s_)
                    except StopIteration:
                        pro.remove(s_)
            if c % 4 == 3 and c + 1 < nchunks:
                run_streams(pro)
                pro = []
                run_streams([self.pre(c + 1)])
        if finish:
            self.k.finish()


def gdn_consts():
    c = np.zeros((128, 768), np.float32)
    p = np.arange(128)[:, None]
    i = np.arange(128)[None, :]
    c[:, 0:128] = (p <= i)
    c[:, 128:256] = (p > i)
    c[:, 256:384] = np.where(i < p, -GBIG, 0.0)
    c[:, 384:512] = np.eye(128)
    c[:, 512:640] = 1.0
    c[:, 640:768] = (p < i)
    return c
from concourse.bass_utils import run_bass_kernel_spmd

NCORES = 8
_DBG = {}
TOKC = 2048
PAIRS = [[0, 1], [2, 3], [4, 5], [6, 7]]


def _dram(nc, name, shape, kind="ExternalInput", dt=None):
    return nc.dram_tensor(name, list(shape), dt or F32, kind=kind).ap()


def _scratch(nc, name, shape, dt=None):
    return nc.dram_tensor(name, list(shape), dt or F32)


class Chunked:
    def __init__(self, nc, name, rows, cols, rc, dt=None, gathered=True):
        self.rc, self.rows, self.cols = rc, rows, cols
        self.n = rows // rc
        self.src = [nc.dram_tensor("%s_%d" % (name, j), [rc, cols], dt or F32) for j in range(self.n)]
        self.dst = [nc.dram_tensor("G%s_%d" % (name, j), [2 * rc, cols], dt or F32) for j in range(self.n)] if gathered else []
        self.gb = [Buf("G%s_%d" % (name, j)) for j in range(self.n)]

    def set_events(self, events):
        for b, ev in zip(self.gb, events):
            b.lastw = ev
            b.reads = {}

    def gbuf(self, r0):
        return self.gb[r0 // self.rc]

    def own(self, r0, r1):
        j = r0 // self.rc
        assert (r1 - 1) // self.rc == j
        return self.src[j].ap()[r0 - j * self.rc:r1 - j * self.rc, :]

    def gat(self, rank, r0, r1):
        j = r0 // self.rc
        assert (r1 - 1) // self.rc == j
        return self.dst[j].ap()[rank * self.rc + r0 - j * self.rc:rank * self.rc + r1 - j * self.rc, :]

    def gathers(self):
        return [(lambda g, s=s, d=d: g.collective_compute("AllGather", ALU.bypass, replica_groups=PAIRS,
                                                         ins=[s.ap().opt()], outs=[d.ap().opt()]))
                for s, d in zip(self.src, self.dst)]


def _gain_layout(vecs, extra=None):
    g = np.stack(vecs).astype(np.float32).reshape(len(vecs), 8, 128).transpose(2, 0, 1).reshape(128, len(vecs) * 8)
    col = np.zeros((128, 1), np.float32) if extra is None else np.asarray(extra, np.float32).reshape(128, 1)
    return np.ascontiguousarray(np.concatenate([g, col], 1))


def build_fused(stop=None, dump=None):
    nc = bass.Bass("TRN2", target_bir_lowering=False)
    I = {}
    def inp(name, shape):
        I[name] = _dram(nc, name, shape)
        return I[name]
    xT = inp("xT", [1024, TOKC])
    sel = inp("sel", [128, 2])
    gA = inp("gA", [128, 17]); gB = inp("gB", [128, 33]); gC = inp("gC", [128, 17])
    wg = [inp("wg%d" % i, [1024, 2816]) for i in range(4)]
    wu = [inp("wu%d" % i, [1024, 2816]) for i in range(4)]
    wd = [inp("wd%d" % i, [2816, 1024]) for i in range(4)]
    w_att_in = inp("w_att_in", [1024, 1152])
    w_att_out = inp("w_att_out", [1024, 1024])
    w_gdn_in = inp("w_gdn_in", [1024, 2064])
    w_gdn_z = inp("w_gdn_z", [1024, 2048])
    w_gdn_out = inp("w_gdn_out", [2048, 1024])
    wpg = [inp("wpg%d" % i, [1024, 1024]) for i in range(2)]
    wpp = [inp("wpp%d" % i, [256, 1024]) for i in range(2)]
    pT = [inp("pT%d" % i, [256, TOKC]) for i in range(2)]
    a_cst = inp("a_cst", [128, 3584]); a_small = inp("a_small", [128, 520])
    g_c32 = inp("g_c32", [128, 768]); g_small = inp("g_small", [128, 8]); g_convw = inp("g_convw", [128, 64])
    g_gconst = inp("g_gconst", [128, 2, 256])
    outT = _dram(nc, "outT", [1024, TOKC], "ExternalOutput")
    hsp = _scratch(nc, "hsp", [1024, TOKC])
    hnA = Chunked(nc, "hnA", 1024, TOKC, 512, BF16)
    projA = _scratch(nc, "projA", [832, 4096]); vtokA = _scratch(nc, "vtokA", [4096, 320])
    oA = Chunked(nc, "oA", 512, 4096, 128)
    zB = _scratch(nc, "zB", [2048, TOKC])
    hnB = Chunked(nc, "hnB", 1024, TOKC, 512, BF16)
    projB = _scratch(nc, "projB", [2048, 4096]); gtokB = _scratch(nc, "gtokB", [4096, 16])
    oB = Chunked(nc, "oB", 1024, 4096, 128)
    SCR = dict(hsp=hsp, projA=projA, vtokA=vtokA, zB=zB, projB=projB, gtokB=gtokB,
               GhnA0=hnA.dst[0], GoA0=oA.dst[0], GoB0=oB.dst[0], oB0=oB.src[0], oA0=oA.src[0])

    dumps = {}

    def maybe_stop(k, name):
        if stop != name:
            return False
        if dump:
            src = SCR[dump]
            o = nc.dram_tensor("dbg", list(src.shape), src.dtype, kind="ExternalOutput").ap()
            b = k.buf("dbg")
            k.dma("sync", [(o, src.ap())], reads=[b], final=True)
        k.finish()
        return True

    with ExitStack() as es:
        k = K(nc, es, safe_same=False)
        with ExitStack() as pes:
            rp = RowProg(nc, pes, TOKC, 2, k=k, pfx="p1")
            rp.load_gains(gA)
            rp.load_h(xT)
            rp.ffn(0, wg[0], wu[0], wd[0])
            rp.hn_to_dram(1, hnA.own)
            rp.store_h(hsp.ap(), final=False)
            rp.emit(finish=False)
            hnA.set_events(k.barrier(hnA.gathers()))
        if maybe_stop(k, "p1") or maybe_stop(k, "p1nocc"):
            return nc
        with ExitStack() as pes:
            rp = RowProg(nc, pes, TOKC, 2, k=k, pfx="p2")
            for r in range(2):
                rp.hn_from_dram(lambda r0, r1, r=r: (hnA.gat(r, r0, r1), hnA.gbuf(r0)))
                rp.proj_fm(w_att_in, 0, 832, projA.ap(), 0, r * TOKC)
                rp.proj_tm(w_att_in, 832, 256, vtokA.ap(), r * TOKC, 0)
                rp.proj_tm(w_att_in, 1088, 64, vtokA.ap(), r * TOKC, 256)
            rp.emit(finish=False)
            k.barrier()
        if maybe_stop(k, "p2"):
            return nc
        with ExitStack() as pes:
            pa, va = projA.ap(), vtokA.ap()
            D = dict(sqT=pa[0:256, :], skT=pa[256:512, :], bqT=pa[512:768, :], bkT=pa[768:832, :],
                     sv=[va[:, hl * 64:(hl + 1) * 64].rearrange("(b p) d -> p b d", p=P) for hl in range(4)],
                     bv=va[:, 256:320].rearrange("(b p) d -> p b d", p=P),
                     cst=a_cst, small=a_small, oT=oA.own)
            ap_ = AttnProg(nc, pes, D, k=k, pfx="p3")
            ap_.final = False
            ap_.emit(finish=False)
            oA.set_events(k.barrier(oA.gathers()))
        if maybe_stop(k, "p3"):
            return nc
        with ExitStack() as pes:
            rp = RowProg(nc, pes, TOKC, 4, k=k, pfx="p4")
            rp.load_gains(gB)
            rp.load_sel(sel)
            rp.load_h(hsp.ap())
            rp.mix_in_sel(lambda rc: (oA.gat(rc // 4, (rc % 4) * P, (rc % 4 + 1) * P), oA.gbuf((rc % 4) * P)), 8, w_att_out)
            rp.ffn(0, wg[1], wu[1], wd[1])
            rp.ple(1, wpg[0], wpp[0], pT[0])
            rp.ffn(2, wg[2], wu[2], wd[2])
            rp.hn_to_dram(3, hnB.own)
            rp.proj_fm(w_gdn_z, 0, 2048, zB.ap(), 0, 0)
            rp.store_h(hsp.ap(), final=False)
            rp.emit(finish=False)
            hnB.set_events(k.barrier(hnB.gathers()))
        if maybe_stop(k, "p4"):
            return nc
        with ExitStack() as pes:
            rp = RowProg(nc, pes, TOKC, 2, k=k, pfx="p5")
            for r in range(2):
                rp.hn_from_dram(lambda r0, r1, r=r: (hnB.gat(r, r0, r1), hnB.gbuf(r0)))
                rp.proj_fm(w_gdn_in, 0, 2048, projB.ap(), 0, r * TOKC)
                rp.proj_tm(w_gdn_in, 2048, 16, gtokB.ap(), r * TOKC, 0)
            rp.emit(finish=False)
            k.barrier()
        if maybe_stop(k, "p5"):
            return nc
        with ExitStack() as pes:
            D = dict(xT=projB.ap(), gtok=gtokB.ap(), gconst=g_gconst, convw=g_convw, small=g_small, c32=g_c32, oT=oB.own)
            gp = GdnProg(nc, pes, D, k=k, pfx="p6", fused=True)
            gp.emit(NCH, finish=False)
            oB.set_events(k.barrier(oB.gathers()))
        if maybe_stop(k, "p6"):
            return nc
        with ExitStack() as pes:
            rp = RowProg(nc, pes, TOKC, 2, k=k, pfx="p7")
            rp.load_gains(gC)
            rp.load_sel(sel)
            rp.load_h(hsp.ap())
            rp.gdn_gate_mix_in_sel(lambda rc: (oB.gat(rc // 8, (rc % 8) * P, (rc % 8 + 1) * P), oB.gbuf((rc % 8) * P)), zB.ap(), w_gdn_out)
            rp.ffn(0, wg[3], wu[3], wd[3])
            rp.ple(1, wpg[1], wpp[1], pT[1])
            rp.store_h(outT, final=True)
            rp.emit(finish=True)
    return nc


def kernel(x, p, ffn_norm, ffn_w_gate, ffn_w_up, ffn_w_down, mix_norm,
           att_w_in, att_q_norm, att_k_norm, att_sinks, att_w_out,
           gdn_w_in, gdn_conv_w, gdn_a_log, gdn_dt_bias, gdn_out_norm, gdn_w_out,
           ple_norm, ple_w_gate, ple_w_proj):
    f = lambda a: np.ascontiguousarray(np.asarray(a, dtype=np.float32))
    x = f(x).reshape(-1, 1024)
    p = f(p).reshape(2, -1, 256)
    ffn_norm, mix_norm, ple_norm = f(ffn_norm), f(mix_norm), f(ple_norm)
    wg, wu, wd = f(ffn_w_gate), f(ffn_w_up), f(ffn_w_down)
    att_w_in, att_w_out = f(att_w_in)[0], f(att_w_out)[0]
    gdn_w_in, gdn_w_out = f(gdn_w_in)[0], f(gdn_w_out)[0]
    conv_w, a_log, dt_bias = f(gdn_conv_w)[0], f(gdn_a_log)[0], f(gdn_dt_bias)[0]
    qg, kg, sinks = f(att_q_norm)[0], f(att_k_norm)[0], f(att_sinks)[0]
    tok = lambda c: slice(c * TOKC, (c + 1) * TOKC)
    ar = np.arange

    shared = {}
    for i, (l, j) in enumerate([(0, 0), (0, 1), (1, 0), (1, 1)]):
        shared["wg%d" % i] = wg[l, j]
        shared["wu%d" % i] = wu[l, j]
        shared["wd%d" % i] = wd[l, j]
    for i in range(2):
        shared["wpg%d" % i] = f(ple_w_gate)[i]
        shared["wpp%d" % i] = f(ple_w_proj)[i]
    shared["gA"] = _gain_layout([ffn_norm[0, 0], mix_norm[0]])
    shared["gB"] = _gain_layout([ffn_norm[0, 1], ple_norm[0], ffn_norm[1, 0], mix_norm[1]])
    shared["gC"] = _gain_layout([ffn_norm[1, 1], ple_norm[1]], extra=f(gdn_out_norm)[0])
    rows = np.concatenate([ar(0, 256), 512 + ar(0, 256), ar(256, 512), 512 + ar(256, 512)])
    shared["w_att_out"] = np.ascontiguousarray(att_w_out[rows])
    shared["w_gdn_z"] = np.ascontiguousarray(gdn_w_in[:, 4096:6144])
    shared["w_gdn_out"] = gdn_w_out
    shared["g_c32"] = gdn_consts()
    sm = np.zeros((128, 8), np.float32)
    sm[:, 0] = 1.0
    sm[:, 1] = 1e-6
    sm[:, 2] = LN_QSCALE
    shared["g_small"] = sm

    maps = []
    for c in range(NCORES):
        b, half = c // 2, c % 2
        m = dict(shared)
        m["xT"] = np.ascontiguousarray(x[tok(c)].T)
        m["pT0"] = np.ascontiguousarray(p[0][tok(c)].T)
        m["pT1"] = np.ascontiguousarray(p[1][tok(c)].T)
        s = np.zeros((128, 2), np.float32)
        s[:, half] = 1.0
        m["sel"] = s
        h4 = half * 256
        cols = np.concatenate([ar(h4, h4 + 256), 512 + ar(h4, h4 + 256), 1536 + ar(h4, h4 + 256),
                               2048 + ar(half * 64, half * 64 + 64), 1024 + ar(h4, h4 + 256), 2176 + ar(half * 64, half * 64 + 64)])
        m["w_att_in"] = np.ascontiguousarray(att_w_in[:, cols])
        cols = np.concatenate([ar(half * 512, half * 512 + 512), 1024 + ar(half * 512, half * 512 + 512),
                               2048 + ar(half * 1024, half * 1024 + 1024),
                               6160 + ar(half * 8, half * 8 + 8), 6144 + ar(half * 8, half * 8 + 8)])
        m["w_gdn_in"] = np.ascontiguousarray(gdn_w_in[:, cols])
        m["a_cst"] = attn_consts(half)
        sma = np.zeros((128, 520), np.float32)
        sma[0:64, 0] = qg
        sma[0:64, 1] = kg
        sma[:, 2] = 1.0
        sma[:, 3] = -SHIFT
        sma[:, 4] = 1e-6
        sma[:, 5] = 64e-6
        sma[:, 6] = -2.0794415416798357
        for hl in range(4):
            sma[0:64, 8 + hl * 128:8 + (hl + 1) * 128] = sinks[4 * half + hl]
        m["a_small"] = sma
        chs = np.concatenate([ar((4 * half + mm) * 128, (4 * half + mm + 1) * 128) for mm in range(4)] +
                             [1024 + ar((4 * half + mm) * 128, (4 * half + mm + 1) * 128) for mm in range(4)] +
                             [2048 + ar((8 * half + hv) * 128, (8 * half + hv + 1) * 128) for hv in range(8)])
        cw = conv_w[:, chs]
        m["g_convw"] = np.ascontiguousarray(cw.reshape(4, 16, 128).transpose(2, 1, 0).reshape(128, 64))
        gc = np.zeros((128, 2, 256), np.float32)
        gc[:, 0, :] = np.tile(dt_bias[8 * half:8 * half + 8], 32)[None, :]
        gc[:, 1, :] = np.tile(a_log[8 * half:8 * half + 8], 32)[None, :]
        m["g_gconst"] = gc
        maps.append(m)

    nc = build_fused(stop=_DBG.get("stop"), dump=_DBG.get("dump"))
    if _DBG.get("trace"):
        full = run_bass_kernel_spmd(nc, maps, core_ids=list(range(NCORES)), trace=True)
        _DBG["full"] = full
        res = full.results
    else:
        res = run_bass_kernel_spmd(nc, maps, core_ids=list(range(NCORES))).results
    if _DBG.get("stop"):
        _DBG["res"] = res
        return None
    out = np.concatenate([res[c]["outT"].T for c in range(NCORES)], 0)
    return np.ascontiguousarray(out.reshape(4, 4096, 1024).astype(np.float32))
```
